# Optimizing a Trainium2 kernel written in Bass

```python
import jax, jax.numpy as jnp
from jax import lax
import numpy as np

D_MODEL = 2048
BATCH = 16
SEQ = 256
DEPTH = 2
DEC_BATCH = 4
DEC_SEQ = 1024
PAST_LEN = 256

GRID_W = 64
HEAD_DIM = 128
ATT_HEADS = 8
ATT_KV_HEADS = 2
Q_BLOCK = 128
ROPE_THETA = 10000.0
GLA_HEADS = 4
GLA_DK = 128
GLA_DV = 128
GLA_RANK = 16
GLA_TAU = 16.0
GLA_CHUNK = 64
RWKV_HEADS = 8
RWKV_HD = 64
RWKV_W_RANK = 64
RWKV_A_RANK = 64
RWKV_G_RANK = 128
RWKV_DECAY_SCALE = 0.606531
RWKV_LN_EPS = 64e-5
FFN_HIDDEN = 5504
N_BRANCHES = 3
EPS = 1e-6

ATT_Q_W = ATT_HEADS * HEAD_DIM
ATT_KV_W = ATT_KV_HEADS * HEAD_DIM
GLA_W = GLA_HEADS * GLA_DK
GLA_V_W = GLA_HEADS * GLA_DV
RWKV_W = RWKV_HEADS * RWKV_HD
RWKV_COLS = 3 * RWKV_W + 2 * RWKV_W_RANK + RWKV_A_RANK + RWKV_G_RANK
IN_COLS = ATT_Q_W + 2 * ATT_KV_W + 2 * GLA_W + 2 * GLA_V_W + 2 * GLA_RANK + RWKV_COLS + N_BRANCHES * D_MODEL
F32 = jnp.float32

kernel_name = 'hybrid_diffusion_ctx_prefix_step'


def _split(x, sizes):
    idx = [int(i) for i in np.cumsum(sizes)[:-1]]
    return jnp.split(x, idx, axis=-1)


def _rms_norm(x, g, eps=EPS):
    xf = x.astype(F32)
    y = xf * lax.rsqrt(jnp.mean(xf * xf, axis=-1, keepdims=True) + eps)
    return (y * g.astype(F32)).astype(x.dtype)


def _head_layer_norm(x, eps):
    xf = x.astype(F32)
    mu = jnp.mean(xf, axis=-1, keepdims=True)
    xc = xf - mu
    return xc * lax.rsqrt(jnp.mean(xc * xc, axis=-1, keepdims=True) + eps)


def _heads(x, n):
    b, t, _ = x.shape
    return x.reshape(b, t, n, -1).transpose(0, 2, 1, 3)


def _merge(x):
    b, n, t, d = x.shape
    return x.transpose(0, 2, 1, 3).reshape(b, t, n * d)


def _shift_prev(x):
    return jnp.pad(x[:, :-1], ((0, 0), (1, 0), (0, 0)))


def _shift_next(x):
    return jnp.pad(x[:, 1:], ((0, 0), (0, 1), (0, 0)))


def _rope_tables(t):
    rows = t // GRID_W
    row = jnp.repeat(jnp.arange(rows, dtype=F32), GRID_W)
    col = jnp.tile(jnp.arange(GRID_W, dtype=F32), rows)
    half = HEAD_DIM // 2
    inv = ROPE_THETA ** (-jnp.arange(0, half, 2, dtype=F32) / half)

    def cos_sin(pos):
        ang = pos[:, None] * inv[None, :]
        ang = jnp.concatenate([ang, ang], axis=-1)
        return jnp.cos(ang), jnp.sin(ang)

    cos_r, sin_r = cos_sin(row)
    cos_c, sin_c = cos_sin(col)
    return (cos_r, sin_r, cos_c, sin_c)


def _rot_half(x):
    x1, x2 = jnp.split(x, 2, axis=-1)
    return jnp.concatenate([-x2, x1], axis=-1)


def _apply_rope(x, tabs):
    cos_r, sin_r, cos_c, sin_c = tabs
    xr, xc = jnp.split(x.astype(F32), 2, axis=-1)
    xr = xr * cos_r + _rot_half(xr) * sin_r
    xc = xc * cos_c + _rot_half(xc) * sin_c
    return jnp.concatenate([xr, xc], axis=-1).astype(x.dtype)


def _blocked_attention(q, k, v):
    b, hq, t, d = q.shape
    hkv = k.shape[1]
    g = hq // hkv
    nb = t // Q_BLOCK
    qb = q.reshape(b, hkv, g, nb, Q_BLOCK, d).transpose(3, 0, 1, 2, 4, 5)
    scale = d ** -0.5

    def one_block(qblk):
        s = jnp.einsum('bhgqd,bhkd->bhgqk', qblk, k).astype(F32) * scale
        p = jax.nn.softmax(s, axis=-1).astype(v.dtype)
        return jnp.einsum('bhgqk,bhkd->bhgqd', p, v)

    o = lax.map(one_block, qb)
    return o.transpose(1, 2, 3, 0, 4, 5).reshape(b, hq, t, d)


def _gla_chunked(q, k, v, log_a, s0):
    b, h, t, dk = q.shape
    dv = v.shape[-1]
    n = t // GLA_CHUNK
    qc = q.astype(F32).reshape(b, h, n, GLA_CHUNK, dk)
    kc = k.astype(F32).reshape(b, h, n, GLA_CHUNK, dk)
    vc = v.astype(F32).reshape(b, h, n, GLA_CHUNK, dv)
    cum = jnp.cumsum(log_a.astype(F32).reshape(b, h, n, GLA_CHUNK, dk), axis=3)
    cum_last = cum[:, :, :, -1:, :]
    q_e = qc * jnp.exp(cum)
    k_e = kc * jnp.exp(-cum)
    k_l = kc * jnp.exp(cum_last - cum)
    mask = jnp.tril(jnp.ones((GLA_CHUNK, GLA_CHUNK), dtype=bool))
    att = jnp.where(mask, jnp.einsum('bhnid,bhnjd->bhnij', q_e, k_e), 0.0)
    o_intra = jnp.einsum('bhnij,bhnjv->bhniv', att, vc)
    chunk_kv = jnp.einsum('bhnjd,bhnjv->bhndv', k_l, vc)
    decay = jnp.exp(cum_last[:, :, :, 0, :])

    def step(s, inp):
        d, kv_n = inp
        return d[..., None] * s + kv_n, s

    s_fin, s_starts = lax.scan(step, s0.astype(F32), (jnp.moveaxis(decay, 2, 0), jnp.moveaxis(chunk_kv, 2, 0)))
    s_starts = jnp.moveaxis(s_starts, 0, 2)
    o = o_intra + jnp.einsum('bhnid,bhndv->bhniv', q_e, s_starts)
    return o.reshape(b, h, t, dv), s_fin


def _rwkv_scan(r, log_w, k, v, kap, a, s0, reverse):
    xs = tuple(jnp.moveaxis(z, 2, 0) for z in (r, jnp.exp(log_w), k, v, kap, a))

    def step(s, inp):
        r_t, w_t, k_t, v_t, kap_t, a_t = inp
        sk = jnp.einsum('bhvk,bhk->bhv', s, kap_t)
        s = s * w_t[:, :, None, :] - sk[..., None] * (kap_t * a_t)[:, :, None, :] + v_t[..., None] * k_t[:, :, None, :]
        return s, jnp.einsum('bhvk,bhk->bhv', s, r_t)

    s_fin, y = lax.scan(step, s0.astype(F32), xs, reverse=reverse)
    return jnp.moveaxis(y, 0, 2), s_fin


def _mixer(h, p, rope, ctx_kv, gla_s0, rwkv_s0):
    proj = h @ p['w_in']
    aq, ak, av, gq, gk, gv, gg, gad, rw_cols, gates = _split(
        proj, (ATT_Q_W, ATT_KV_W, ATT_KV_W, GLA_W, GLA_W, GLA_V_W, GLA_V_W, 2 * GLA_RANK, RWKV_COLS, N_BRANCHES * D_MODEL))

    q = _rms_norm(_heads(aq, ATT_HEADS), p['q_norm'])
    k = _rms_norm(_heads(ak, ATT_KV_HEADS), p['k_norm'])
    v = _heads(av, ATT_KV_HEADS)
    if rope is not None:
        q = _apply_rope(q, rope)
        k = _apply_rope(k, rope)
    if ctx_kv is None:
        k_all, v_all = k, v
    else:
        k_all = jnp.concatenate([ctx_kv[0].astype(k.dtype), k], axis=2)
        v_all = jnp.concatenate([ctx_kv[1].astype(v.dtype), v], axis=2)
    o_att = _merge(_blocked_attention(q, k_all, v_all))

    gq_h = _heads(gq, GLA_HEADS) * (GLA_DK ** -0.5)
    gk_h = _heads(gk, GLA_HEADS)
    gv_h = _heads(gv, GLA_HEADS)
    gad_f, gad_b = jnp.split(gad, 2, axis=-1)
    la_f = _heads(jax.nn.log_sigmoid((gad_f @ p['gla_a_up'][0] + p['gla_a_bias'][0]).astype(F32)) / GLA_TAU, GLA_HEADS)
    la_b = _heads(jax.nn.log_sigmoid((gad_b @ p['gla_a_up'][1] + p['gla_a_bias'][1]).astype(F32)) / GLA_TAU, GLA_HEADS)
    o_f, s_f = _gla_chunked(gq_h, gk_h, gv_h, la_f, gla_s0[:, 0])
    o_b, s_b = _gla_chunked(jnp.flip(gq_h, 2), jnp.flip(gk_h, 2), jnp.flip(gv_h, 2), jnp.flip(la_b, 2), gla_s0[:, 1])
    o_gla = _merge(_rms_norm(o_f + jnp.flip(o_b, 2), p['gla_norm']))
    o_gla = (o_gla * jax.nn.silu(gg.astype(F32))).astype(h.dtype)

    rw = rw_cols + (0.5 * (_shift_prev(rw_cols) + _shift_next(rw_cols)) - rw_cols) * p['rwkv_mu']
    rr, rk, rv, rwd, rad, rgd = _split(rw, (RWKV_W, RWKV_W, RWKV_W, 2 * RWKV_W_RANK, RWKV_A_RANK, RWKV_G_RANK))
    rwd_f, rwd_b = jnp.split(rwd, 2, axis=-1)
    logw_f = -RWKV_DECAY_SCALE * jax.nn.sigmoid((p['rwkv_w0'][0] + jnp.tanh(rwd_f) @ p['rwkv_w_up'][0]).astype(F32))
    logw_b = -RWKV_DECAY_SCALE * jax.nn.sigmoid((p['rwkv_w0'][1] + jnp.tanh(rwd_b) @ p['rwkv_w_up'][1]).astype(F32))
    a = jax.nn.sigmoid((p['rwkv_a0'] + rad @ p['rwkv_a_up']).astype(F32))
    g_out = jax.nn.sigmoid(rgd) @ p['rwkv_g_up']
    rkf = rk.astype(F32)
    k_rep = rkf * (1.0 + (a - 1.0) * p['rwkv_k_alpha'].astype(F32))
    kap = _heads(rkf * p['rwkv_k_xi'].astype(F32), RWKV_HEADS)
    kap = kap * lax.rsqrt(jnp.sum(kap * kap, axis=-1, keepdims=True) + EPS)
    r_h = _heads(rr.astype(F32), RWKV_HEADS)
    v_h = _heads(rv.astype(F32), RWKV_HEADS)
    k_h = _heads(k_rep, RWKV_HEADS)
    a_h = _heads(a, RWKV_HEADS)
    y_f, st_f = _rwkv_scan(r_h, _heads(logw_f, RWKV_HEADS), k_h, v_h, kap, a_h, rwkv_s0[:, 0], False)
    y_b, st_b = _rwkv_scan(r_h, _heads(logw_b, RWKV_HEADS), k_h, v_h, kap, a_h, rwkv_s0[:, 1], True)
    rho = p['rwkv_bonus'].astype(F32).reshape(RWKV_HEADS, 1, RWKV_HD)
    bonus = jnp.sum(r_h * k_h * rho, axis=-1, keepdims=True) * v_h
    y = _merge(_head_layer_norm(y_f + y_b + bonus, RWKV_LN_EPS)) * p['rwkv_ln_w'].astype(F32) + p['rwkv_ln_b'].astype(F32)
    y = (y * g_out.astype(F32)).astype(h.dtype)

    g_att, g_gla, g_rwkv = jnp.split(jax.nn.sigmoid(gates), N_BRANCHES, axis=-1)
    merged = g_att * (o_att @ p['w_br_att']) + g_gla * (o_gla @ p['w_br_gla']) + g_rwkv * (y @ p['w_br_rwkv'])
    out = merged @ p['w_out']
    return out, k, v, jnp.stack([s_f, s_b], axis=1), jnp.stack([st_f, st_b], axis=1)


def _conv_ffn(h, p):
    u = h @ p['ffn_up']
    w = p['ffn_conv_w']
    u = w[0] * _shift_prev(u) + w[1] * u + w[2] * _shift_next(u) + p['ffn_conv_b']
    val, gate = jnp.split(u, 2, axis=-1)
    return (jax.nn.silu(gate) * val) @ p['ffn_down']


def _layer(x, mod, p, rope, ctx_kv, gla_s0, rwkv_s0):
    sh1, sc1, gt1, sh2, sc2, gt2 = jnp.split(mod, 6, axis=-1)
    h = _rms_norm(x, p['norm_mix']) * (1.0 + sc1) + sh1
    out, k, v, gla_st, rwkv_st = _mixer(h, p, rope, ctx_kv, gla_s0, rwkv_s0)
    x = x + gt1 * out
    h = _rms_norm(x, p['norm_ffn']) * (1.0 + sc2) + sh2
    x = x + gt2 * _conv_ffn(h, p)
    return x, k, v, gla_st, rwkv_st


def setup_inputs(seed: int = 0) -> dict:
    key = jax.random.key(seed)
    ctr = [0]

    def nk():
        ctr[0] += 1
        return jax.random.fold_in(key, ctr[0])

    def nrm(shape, scale=1.0):
        return jax.random.normal(nk(), shape, F32) * scale

    L = DEPTH
    return {
        'x_prompt': nrm((BATCH, SEQ, D_MODEL)),
        'x_sample': nrm((DEC_BATCH, DEC_SEQ, D_MODEL)),
        'cache_k': nrm((DEC_BATCH, L, ATT_KV_HEADS, PAST_LEN, HEAD_DIM)),
        'cache_v': nrm((DEC_BATCH, L, ATT_KV_HEADS, PAST_LEN, HEAD_DIM)),
        'state_gla': nrm((DEC_BATCH, L, 2, GLA_HEADS, GLA_DK, GLA_DV), 0.1),
        'state_rwkv': nrm((DEC_BATCH, L, 2, RWKV_HEADS, RWKV_HD, RWKV_HD), 0.1),
        'c': nrm((DEC_BATCH, D_MODEL)),
        'c_ctx': nrm((D_MODEL,)),
        'w_mod': nrm((L, D_MODEL, 6 * D_MODEL), 0.5 * D_MODEL ** -0.5),
        'b_mod': nrm((L, 6 * D_MODEL), 0.01),
        'norm_mix': 1.0 + nrm((L, D_MODEL), 0.01),
        'w_in': nrm((L, D_MODEL, IN_COLS), D_MODEL ** -0.5),
        'q_norm': 1.0 + nrm((L, HEAD_DIM), 0.01),
        'k_norm': 1.0 + nrm((L, HEAD_DIM), 0.01),
        'gla_a_up': nrm((L, 2, GLA_RANK, GLA_W), GLA_RANK ** -0.5),
        'gla_a_bias': nrm((L, 2, GLA_W), 0.1),
        'gla_norm': 1.0 + nrm((L, GLA_DV), 0.01),
        'rwkv_mu': jax.random.uniform(nk(), (L, RWKV_COLS), F32),
        'rwkv_w0': nrm((L, 2, RWKV_W), 0.5),
        'rwkv_w_up': nrm((L, 2, RWKV_W_RANK, RWKV_W), 0.5 * RWKV_W_RANK ** -0.5),
        'rwkv_a0': nrm((L, RWKV_W), 0.1),
        'rwkv_a_up': nrm((L, RWKV_A_RANK, RWKV_W), RWKV_A_RANK ** -0.5),
        'rwkv_g_up': nrm((L, RWKV_G_RANK, RWKV_W), RWKV_G_RANK ** -0.5),
        'rwkv_k_xi': 0.85 + nrm((L, RWKV_W), 0.05),
        'rwkv_k_alpha': 1.0 + nrm((L, RWKV_W), 0.05),
        'rwkv_bonus': nrm((L, RWKV_W), 0.1),
        'rwkv_ln_w': 1.0 + nrm((L, RWKV_W), 0.01),
        'rwkv_ln_b': nrm((L, RWKV_W), 0.01),
        'w_br_att': nrm((L, ATT_Q_W, D_MODEL), ATT_Q_W ** -0.5),
        'w_br_gla': nrm((L, GLA_V_W, D_MODEL), GLA_V_W ** -0.5),
        'w_br_rwkv': nrm((L, RWKV_W, D_MODEL), RWKV_W ** -0.5),
        'w_out': nrm((L, D_MODEL, D_MODEL), D_MODEL ** -0.5),
        'norm_ffn': 1.0 + nrm((L, D_MODEL), 0.01),
        'ffn_up': nrm((L, D_MODEL, 2 * FFN_HIDDEN), D_MODEL ** -0.5),
        'ffn_conv_w': nrm((L, 3, 2 * FFN_HIDDEN), 3 ** -0.5),
        'ffn_conv_b': nrm((L, 2 * FFN_HIDDEN), 0.01),
        'ffn_down': nrm((L, FFN_HIDDEN, D_MODEL), FFN_HIDDEN ** -0.5),
    }


def reference(x_prompt, x_sample, cache_k, cache_v, state_gla, state_rwkv, c, c_ctx,
              w_mod, b_mod, norm_mix, w_in, q_norm, k_norm, gla_a_up, gla_a_bias, gla_norm,
              rwkv_mu, rwkv_w0, rwkv_w_up, rwkv_a0, rwkv_a_up, rwkv_g_up, rwkv_k_xi, rwkv_k_alpha,
              rwkv_bonus, rwkv_ln_w, rwkv_ln_b, w_br_att, w_br_gla, w_br_rwkv, w_out, norm_ffn,
              ffn_up, ffn_conv_w, ffn_conv_b, ffn_down):
    def layer_params(l):
        return {
            'norm_mix': norm_mix[l], 'w_in': w_in[l], 'q_norm': q_norm[l], 'k_norm': k_norm[l],
            'gla_a_up': gla_a_up[l], 'gla_a_bias': gla_a_bias[l], 'gla_norm': gla_norm[l],
            'rwkv_mu': rwkv_mu[l], 'rwkv_w0': rwkv_w0[l], 'rwkv_w_up': rwkv_w_up[l],
            'rwkv_a0': rwkv_a0[l], 'rwkv_a_up': rwkv_a_up[l], 'rwkv_g_up': rwkv_g_up[l],
            'rwkv_k_xi': rwkv_k_xi[l], 'rwkv_k_alpha': rwkv_k_alpha[l], 'rwkv_bonus': rwkv_bonus[l],
            'rwkv_ln_w': rwkv_ln_w[l], 'rwkv_ln_b': rwkv_ln_b[l], 'w_br_att': w_br_att[l],
            'w_br_gla': w_br_gla[l], 'w_br_rwkv': w_br_rwkv[l], 'w_out': w_out[l],
            'norm_ffn': norm_ffn[l], 'ffn_up': ffn_up[l], 'ffn_conv_w': ffn_conv_w[l],
            'ffn_conv_b': ffn_conv_b[l], 'ffn_down': ffn_down[l],
        }

    b_ctx = x_prompt.shape[0]
    gla_zero = jnp.zeros((b_ctx, 2, GLA_HEADS, GLA_DK, GLA_DV), F32)
    rwkv_zero = jnp.zeros((b_ctx, 2, RWKV_HEADS, RWKV_HD, RWKV_HD), F32)
    x = x_prompt
    ks, vs, gla_states, rwkv_states = [], [], [], []
    for l in range(DEPTH):
        p = layer_params(l)
        mod = (jax.nn.silu(c_ctx) @ w_mod[l] + b_mod[l])[None, None, :]
        x, k_l, v_l, gs, rs = _layer(x, mod, p, None, None, gla_zero, rwkv_zero)
        ks.append(k_l)
        vs.append(v_l)
        gla_states.append(gs)
        rwkv_states.append(rs)
    y_prompt = x
    new_cache_k = jnp.stack(ks, axis=1)
    new_cache_v = jnp.stack(vs, axis=1)
    new_state_gla = jnp.stack(gla_states, axis=1)
    new_state_rwkv = jnp.stack(rwkv_states, axis=1)

    rope = _rope_tables(x_sample.shape[1])
    x = x_sample
    for l in range(DEPTH):
        p = layer_params(l)
        mod = (jax.nn.silu(c) @ w_mod[l] + b_mod[l])[:, None, :]
        x, _, _, _, _ = _layer(x, mod, p, rope, (cache_k[:, l], cache_v[:, l]), state_gla[:, l], state_rwkv[:, l])
    y_sample = x

    return (y_prompt, y_sample, new_cache_k, new_cache_v, new_state_gla, new_state_rwkv)
```

```python
import numpy as np
from contextlib import ExitStack
import concourse.bass as bass
import concourse.mybir as mybir
from concourse.bass_utils import run_bass_kernel_spmd

F32 = mybir.dt.float32
BF16 = mybir.dt.bfloat16
ALU = mybir.AluOpType
AF = mybir.ActivationFunctionType
AX = mybir.AxisListType

D = 2048
KC = 16
T = 1024
NT = 8
L = 2
IN_COLS = 11616
FFN_H = 5504
C_AQ, C_AK, C_AV, C_GQ, C_GK, C_GV, C_GG, C_GAD = 0, 1024, 1280, 1536, 2048, 2560, 3072, 3584
C_RW = 3616
C_RR, C_RK, C_RV, C_RWD, C_RAD, C_RGD = 3616, 4128, 4640, 5152, 5280, 5344
C_GATE = 5472
EPS = 1e-6
SELF_SYNC = True
NS = True


class Buf:
    __slots__ = ("name", "last_w", "readers", "dcount")

    def __init__(self, name):
        self.name = name
        self.last_w = None
        self.readers = []
        self.dcount = 0


class Prog:
    ENGS = ("pe", "act", "dve", "pool", "sp")
    EPOCH = 8000

    def __init__(self, nc, self_sync=True):
        self.nc = nc
        self.ops = []
        self.self_sync = self_sync
        self.eng = {"pe": nc.tensor, "act": nc.scalar, "dve": nc.vector,
                    "pool": nc.gpsimd, "sp": nc.sync}

    wg = None
    hoist_depth = 2
    n_wg = 0

    def op(self, eng, emit, reads=(), writes=(), ns=False):
        self.ops.append(dict(eng=eng, emit=emit, reads=list(reads), writes=list(writes),
                             dma=None, bar=False, wg=self.wg, ns=ns))

    def dma(self, out, in_, reads=(), writes=(), sem=None, queue="sp", **kw):
        self.ops.append(dict(eng=queue, emit=lambda e: e.dma_start(out=out, in_=in_, **kw),
                             reads=list(reads), writes=list(writes), dma=sem, bar=False, wg=self.wg))

    def hoist_weight_groups(self):
        ops = self.ops
        groups = {}
        for i, o in enumerate(ops):
            if o.get("wg") is not None:
                groups.setdefault(o["wg"], []).append(i)
        order = sorted(groups)
        moved = set()
        before = {}

        def try_move(op_idx_list, t, first):
            if t >= first or ops[t].get("wg") is not None:
                return False
            if any(ops[j]["bar"] for j in range(t, first)):
                return False
            before.setdefault(t, []).extend(op_idx_list)
            moved.update(op_idx_list)
            return True

        for n_, g in enumerate(order):
            if n_ == 0:
                continue
            idx = groups[g]
            dmas = [i for i in idx if ops[i]["dma"] is not None]
            casts = [i for i in idx if ops[i]["dma"] is None]
            first = idx[0]
            a1 = order[n_ - 1]
            t1 = groups[a1][-1] + 1
            done = False
            if self.hoist_depth >= 2 and n_ >= 2:
                a2 = order[n_ - 2]
                t2 = groups[a2][-1] + 1
                if try_move(dmas, t2, first):
                    try_move(casts, t1, first)
                    done = True
            if not done:
                try_move(idx, t1, first)
        new_ops = []
        for i in range(len(ops)):
            if i in moved:
                continue
            for j in before.get(i, []):
                new_ops.append(ops[j])
            new_ops.append(ops[i])
        assert len(new_ops) == len(ops)
        self.ops = new_ops

    def barrier(self):
        for e in self.ENGS:
            self.ops.append(dict(eng=e, emit=None, reads=[], writes=[], dma=None, bar=True, wg=None))

    def finalize(self, stack, final_bufs=()):
        nc = self.nc
        self.hoist_weight_groups()
        ops = self.ops
        ops.append(dict(eng="sp", emit=None, reads=list(final_bufs), writes=[], dma=None, bar=False))
        n = len(ops)
        ev = [None] * n
        seq = {e: 0 for e in self.ENGS}
        deps = [None] * n
        waited = {e: {} for e in self.ENGS}
        needed = set()
        dma_bufs = {}
        for i, o in enumerate(ops):
            e = o["eng"]
            d = {}
            if o["bar"]:
                for x in self.ENGS:
                    if seq[x] > 0 and (x != e or (self.self_sync and e not in ("pe", "sp"))):
                        d[x] = seq[x]
                for name, sb in dma_bufs.items():
                    d["D:" + name] = sb.dcount
            else:
                cand = []
                for b in o["reads"]:
                    if b.last_w is not None:
                        cand.append(b.last_w)
                for b in o["writes"]:
                    if b.last_w is not None:
                        cand.append(b.last_w)
                    cand.extend(b.readers)
                for j in cand:
                    k, v = ev[j]
                    oj = ops[j]
                    if oj["dma"] is not None:
                        v = oj["dma"].dcount
                    elif oj["eng"] == e:
                        if e == "pe" or e == "sp" or not self.self_sync or o.get("ns"):
                            continue
                    if v > d.get(k, -1):
                        d[k] = v
            dl = []
            for k, v in d.items():
                if waited[e].get(k, -1) >= v:
                    continue
                waited[e][k] = v
                dl.append((k, v))
                if k in self.ENGS:
                    needed.add((k, v))
            deps[i] = dl
            if o["dma"] is not None:
                sb = o["dma"]
                sb.dcount += 1
                dma_bufs[sb.name] = sb
                ev[i] = ("D:" + sb.name, sb.dcount)
            elif o["emit"] is not None:
                seq[e] += 1
                ev[i] = (e, seq[e])
            else:
                ev[i] = (e, seq[e])
            for b in o["reads"]:
                b.readers.append(i)
            for b in o["writes"]:
                b.last_w = i
                b.readers = []
        rank = {}
        nep = {}
        for e in self.ENGS:
            vs = sorted(v for (k, v) in needed if k == e)
            for r, v in enumerate(vs):
                rank[(e, v)] = (r // self.EPOCH, r % self.EPOCH + 1)
            nep[e] = (len(vs) - 1) // self.EPOCH + 1 if vs else 1
        sems = {}
        for e in self.ENGS:
            for ep in range(nep[e]):
                sems[(e, ep)] = stack.enter_context(nc.semaphore("s_%s%d" % (e, ep)))
        for name in dma_bufs:
            sems["D:" + name] = stack.enter_context(nc.semaphore("d_" + name))
        self.n_waits = 0
        self.n_sems = len(sems)
        for i, o in enumerate(ops):
            e = o["eng"]
            eng = self.eng[e]
            for (k, v) in deps[i]:
                if k in self.ENGS:
                    ep, val = rank[(k, v)]
                    eng.wait_ge(sems[(k, ep)], val)
                else:
                    eng.wait_ge(sems[k], 16 * v)
                self.n_waits += 1
            if o["emit"] is None:
                continue
            ins = o["emit"](eng)
            k, v = ev[i]
            if o["dma"] is not None:
                ins.then_inc(sems[k], 16)
            elif (k, v) in needed:
                ep, val = rank[(k, v)]
                ins.then_inc(sems[(k, ep)], 1)
        return len(ops)


class Tl:
    def __init__(self, t, b):
        self.t = t
        self.b = b

    def __getitem__(self, k):
        return self.t[k]


class Kern:
    def __init__(self, stage=99):
        self.stage = stage
        self.nc = bass.Bass("TRN2", target_bir_lowering=False)
        self.st = ExitStack()
        self.p = Prog(self.nc, self_sync=SELF_SYNC)
        self.din = {}
        self.dout = {}
        self.outbufs = []
        self._n = 0

    def inp(self, name, shape):
        t = self.nc.dram_tensor(name, list(shape), F32, kind="ExternalInput")
        self.din[name] = t
        return t.ap()

    def outp(self, name, shape):
        t = self.nc.dram_tensor(name, list(shape), F32, kind="ExternalOutput")
        self.dout[name] = t
        b = Buf("o_" + name)
        self.outbufs.append(b)
        return t.ap(), b

    ARENA_F32 = (24576, 6144, 9216)

    def sb(self, name, shape, dt=F32):
        if getattr(self, "in_arena", None) is not None:
            w = self.in_arena
            n = int(np.prod(shape[1:]))
            nf = n if dt == F32 else (n + 1) // 2
            nf = (nf + 7) // 8 * 8
            assert self.aoff[w] + nf <= self.ARENA_F32[w], (name, w, self.aoff[w], nf)
            v = self.arenas[w][0:shape[0], self.aoff[w]:self.aoff[w] + nf]
            self.aoff[w] += nf
            if dt != F32:
                v = v.bitcast(dt)
            v = v[:, 0:n]
            if len(shape) == 3:
                v = v.rearrange("p (a b) -> p a b", b=shape[2])
            return Tl(v, Buf(name))
        t = self.st.enter_context(self.nc.sbuf_tensor("sb_" + name, list(shape), dt))
        return Tl(t, Buf(name))

    def arena_begin(self, w=0, keep=False):
        if not hasattr(self, "arenas"):
            self.arenas = [self.st.enter_context(self.nc.sbuf_tensor("sb_arena%d" % i, [128, self.ARENA_F32[i]], F32))
                           for i in range(3)]
            self.aoff = [0, 0, 0]
        self.in_arena = w
        if not keep:
            self.aoff[w] = 0

    def arena_end(self, limit=None):
        if limit is not None:
            assert self.aoff[self.in_arena] <= limit, (self.in_arena, self.aoff[self.in_arena])
        self.in_arena = None

    def ring(self, name, n, shape, dt=F32):
        tl = [self.sb("%s%d" % (name, i), shape, dt) for i in range(n)]
        return Ring(tl)

    def setup_psum(self):
        tl = [Tl(self.st.enter_context(self.nc.psum_tensor("ps%d" % i, [128, 512], F32)),
                 Buf("ps%d" % i)) for i in range(8)]
        self.ps = Ring(tl[0:3])
        self.pmod = tl[3]
        self.psa = Ring(tl[4:8])
        self.ps8 = Ring(tl[0:8])


class Ring:
    def __init__(self, tl):
        self.tl = tl
        self.i = 0

    def get(self):
        t = self.tl[self.i % len(self.tl)]
        self.i += 1
        return t


def build(stage=99):
    K = Kern(stage)
    nc, p = K.nc, K.p
    x_in = K.inp("x", [T, D])
    cvec = K.inp("cvec", [KC, 128])
    w_mod = K.inp("w_mod", [L, D, 6 * D])
    b_mod = K.inp("b_mod", [L, 96, 128])
    norm_mix = K.inp("norm_mix", [L, KC, 128])
    norm_ffn = K.inp("norm_ffn", [L, KC, 128])
    w_in = K.inp("w_in", [L, D, IN_COLS])
    qk_norm = K.inp("qk_norm", [L, 2, 128])
    ck_in = K.inp("ck", [L, 2, 256, 128])
    cv_in = K.inp("cv", [L, 2, 256, 128])
    ident_in = K.inp("ident", [128, 128])
    rot_in = K.inp("rotT", [128, 128])
    cos_in = K.inp("cosT", [128, T])
    sin_in = K.inp("sinT", [128, T])
    amask_in = K.inp("amask", [128, 40])
    segf_in = K.inp("segf", [128, 4])
    tri_in = K.inp("tri", [128, 4, 128])
    sg0_in = K.inp("sg0", [L, 2, 4, 4, 128, 128])
    gla_aup = K.inp("gla_aup", [L, 2, 17, 512])
    gla_norm = K.inp("gla_norm", [L, 1, 128])
    ng_out, ng_ob = K.outp("ng", [L, 2, 4, 4, 128, 128])
    rw_mu = K.inp("rw_mu", [L, 16, 128])
    sel2_in = K.inp("sel2", [128, 2, 128])
    rw_wup = K.inp("rw_wup", [L, 2, 65, 512])
    rw_aup = K.inp("rw_aup", [L, 65, 512])
    rw_gup = K.inp("rw_gup", [L, 128, 512])
    rw_rows = K.inp("rw_rows", [L, 5, 512])
    sr0_in = K.inp("sr0", [L, 4, 128, 512])
    nr_out, nr_ob = K.outp("nr", [L, 2, 4, 64, 512])
    w_br = [K.inp("w_br_att", [L, 1024, D]), K.inp("w_br_gla", [L, 512, D]), K.inp("w_br_rwkv", [L, 512, D])]
    w_out = K.inp("w_out", [L, D, D])
    ffn_up = K.inp("ffn_up", [L, D, 2 * FFN_H])
    ffn_cw = K.inp("ffn_cw", [L, 3, 86, 128])
    ffn_cb = K.inp("ffn_cb", [L, 86, 128])
    ffn_down = K.inp("ffn_down", [L, FFN_H, D])
    y_out, y_ob = K.outp("y", [T, D])
    nk_out, nk_ob = K.outp("nk", [L, 2, T, 128])
    nv_out, nv_ob = K.outp("nv", [L, 2, T, 128])
    xs = nc.dram_tensor("xs", [KC, 128, T], F32).ap()
    xs_b = [Buf("xs%d" % c) for c in range(KC)]

    K.setup_psum()
    ps = K.ps
    psa = K.psa
    ident = K.sb("ident", [128, 128])
    ones_f = K.sb("ones_f", [128, 128])
    ones_b = K.sb("ones_b", [128, 128], BF16)
    rotT = K.sb("rotT", [128, 128])
    amask = K.sb("amask", [128, 40])
    p.dma(ident[:], ident_in[:, :], writes=[ident.b], sem=ident.b)
    p.dma(rotT[:], rot_in[:, :], writes=[rotT.b], sem=rotT.b)
    p.dma(amask[:], amask_in[:, :], writes=[amask.b], sem=amask.b)
    tri = K.sb("tri", [128, 4, 128])
    p.dma(tri[:], tri_in[:, :, :], writes=[tri.b], sem=tri.b)
    segf = K.sb("segf", [128, 4])
    p.dma(segf[:], segf_in[:, :], writes=[segf.b], sem=segf.b)
    p.op("pool", lambda e: e.memset(ones_f[:], 1.0), writes=[ones_f.b])
    p.op("pool", lambda e: e.memset(ones_b[:], 1.0), writes=[ones_b.b])
    sel2 = K.sb("sel2", [128, 2, 128])
    p.dma(sel2[:], sel2_in[:, :, :], writes=[sel2.b], sem=sel2.b)
    ident_b = K.sb("ident_b", [128, 128], BF16)
    p.op("act", lambda e: e.activation(ident_b[:], ident[:], AF.Copy), reads=[ident.b], writes=[ident_b.b])

    NW = 3
    WQ2 = False
    K.arena_begin(2)
    wst = K.ring("wst", NW, [128, 2048], F32)
    wbf = K.ring("wbf", NW, [128, 2048], BF16)
    K.arena_end()

    def wload(src, kc, ncols, cast=True, rows=128):
        s = wst.get()
        n = kc * ncols
        p.n_wg += 1
        p.wg = p.n_wg
        sv = s[0:rows, 0:n].rearrange("p (k n) -> p k n", n=ncols)
        srcv = src.rearrange("(k p) n -> p k n", p=rows)
        nsplit = 1
        kq = kc // nsplit
        for q_ in range(nsplit):
            p.dma(sv[:, q_ * kq:(q_ + 1) * kq, :], srcv[:, q_ * kq:(q_ + 1) * kq, :], writes=[s.b], sem=s.b,
                  queue=("sp", "act")[q_ % 2] if WQ2 else "sp")
        if not cast:
            p.wg = None
            return sv, s.b
        w = wbf.get()
        wv = w[0:rows, 0:n].rearrange("p (k n) -> p k n", n=ncols)
        p.op("act", lambda e: e.activation(w[0:rows, 0:n], s[0:rows, 0:n], AF.Copy), reads=[s.b], writes=[w.b])
        p.wg = None
        return wv, w.b

    clr = K.ring("clr", 2, [128, 128])
    def col_load(dst, dstb, src_rows, nrows):
        tmp = clr.get()
        p.dma(tmp[0:nrows, :], src_rows, writes=[tmp.b], sem=tmp.b)
        pt = ps.get()
        p.op("pe", lambda e: e.transpose(pt[:, 0:nrows], tmp[0:nrows, :], ident[0:nrows, 0:nrows]),
             reads=[tmp.b, ident.b], writes=[pt.b])
        p.op("dve", lambda e: e.tensor_copy(dst, pt[:, 0:nrows]), reads=[pt.b], writes=[dstb])

    K.arena_begin()
    xtm = K.ring("xtm", 2, [128, D])
    xev = K.ring("xev", 2, [128, 4, 128])
    yo = K.ring("yo", 2, [128, D])
    K.arena_end(limit=18432)
    for tt in range(NT):
        xt_ = xtm.get()
        p.dma(xt_[:], x_in[tt * 128:(tt + 1) * 128, :], writes=[xt_.b], sem=xt_.b)
        for g in range(4):
            pt = ps.get()
            for j in range(4):
                c = g * 4 + j
                p.op("pe", lambda e, pt=pt, j=j, c=c, xt_=xt_: e.transpose(
                    pt[:, j * 128:(j + 1) * 128], xt_[:, c * 128:(c + 1) * 128], ident[:]),
                    reads=[xt_.b, ident.b], writes=[pt.b])
            ev = xev.get()
            evv = ev[:, :, 0:128]
            p.op("act", lambda e, pt=pt, evv=evv: e.activation(
                evv, pt[:].rearrange("p (j t) -> p j t", t=128), AF.Copy), reads=[pt.b], writes=[ev.b])
            p.dma(xs[g * 4:(g + 1) * 4, :, tt * 128:(tt + 1) * 128].rearrange("c p t -> p c t"), evv,
                  reads=[ev.b], writes=xs_b[g * 4:(g + 1) * 4], sem=ev.b)

    p.barrier()
    modc = K.sb("modc", [128, 96])
    bmodc = K.sb("bmodc", [128, 96])
    silc = K.sb("silc", [128, KC])
    gcol = K.sb("gcol", [128, 2, KC])
    Gc = K.sb("Gc", [128, 2, KC])
    qkn = K.sb("qkn", [128, 2])
    col_load(silc[:], silc.b, cvec[:, :], KC)
    p.op("act", lambda e: e.activation(silc[:], silc[:], AF.Silu), reads=[], writes=[silc.b])

    K.arena_begin(0)
    K.aoff[0] = 18432
    xch = K.ring("xch", 2, [128, T])
    sqr = K.ring("sqr", 1, [128, T])
    K.arena_end()
    rstd = K.sb("rstd", [128, T])
    hT = K.sb("hT", [128, KC, T], BF16)
    nrm_tmp = Ring([Tl(sqr.tl[0].t, Buf("nrmt0"))])

    mred = K.sb("mred", [128, 4])

    def mod_superblock(l, G):
        pm = K.pmod
        for q_ in range(4):
            wv, wb = wload(w_mod[l, q_ * 512:(q_ + 1) * 512, G * 512:(G + 1) * 512], 4, 512, cast=False)
            for g in range(4):
                for kc in range(4):
                    col = g * 4 + q_
                    p.op("pe", lambda e, wv=wv, g=g, kc=kc, q_=q_, col=col: e.matmul(
                        pm[:, col:col + 1], wv[:, kc, g * 128:(g + 1) * 128], silc[:, 4 * q_ + kc:4 * q_ + kc + 1],
                        start=(kc == 0), stop=(kc == 3)), reads=[wb, silc.b], writes=[pm.b])
            yield
        p.op("dve", lambda e: e.tensor_reduce(mred[:], pm[:, 0:16].rearrange("p (g q) -> p g q", q=4), AX.X, ALU.add),
             reads=[pm.b], writes=[mred.b])
        p.op("dve", lambda e, G=G: e.tensor_tensor(modc[:, 4 * G:4 * G + 4], mred[:], bmodc[:, 4 * G:4 * G + 4], ALU.add),
             reads=[mred.b, bmodc.b], writes=[modc.b])

    def mod_gc(i, off):
        p.op("dve", lambda e, i=i, off=off: e.scalar_tensor_tensor(
            Gc[:, i, :], modc[:, off:off + 16], 1.0, gcol[:, i, :], ALU.add, ALU.mult),
            reads=[modc.b, gcol.b], writes=[Gc.b])

    def compute_mod(l):
        col_load(bmodc[:], bmodc.b, b_mod[l, :, :], 96)
        col_load(gcol[:, 0, :], gcol.b, norm_mix[l, :, :], KC)
        col_load(gcol[:, 1, :], gcol.b, norm_ffn[l, :, :], KC)
        col_load(qkn[:], qkn.b, qk_norm[l, :, :], 2)
        for G in range(8):
            for _ in mod_superblock(l, G):
                pass
        mod_gc(0, 16)

    def mod_deferred(l):
        for G in range(8, 24):
            for _ in mod_superblock(l, G):
                yield
        mod_gc(1, 64)

    def norm(l, which):
        sh_off = 0 if which == 0 else 48
        pa, pb = psa.get(), psa.get()
        for c in range(KC):
            xc = xch.get()
            p.dma(xc[:], xs[c, :, :], reads=[xs_b[c]], writes=[xc.b], sem=xc.b)
            sq = sqr.get()
            p.op("act", lambda e, sq=sq, xc=xc: e.activation(sq[:], xc[:], AF.Square),
                 reads=[xc.b], writes=[sq.b])
            for hf, pp in ((0, pa), (1, pb)):
                p.op("pe", lambda e, sq=sq, hf=hf, pp=pp, c=c: e.matmul(
                    pp[:], ones_f[:], sq[:, hf * 512:(hf + 1) * 512], start=(c == 0), stop=(c == KC - 1)),
                    reads=[sq.b, ones_f.b], writes=[pp.b])
        for hf, pp in ((0, pa), (1, pb)):
            sl = slice(hf * 512, (hf + 1) * 512)
            p.op("dve", lambda e, pp=pp, sl=sl: e.tensor_scalar(
                rstd[:, sl], pp[:], 1.0 / D, EPS, ALU.mult, ALU.add), reads=[pp.b], writes=[rstd.b])
        p.op("act", lambda e: e.activation(rstd[:], rstd[:], AF.Sqrt), reads=[], writes=[rstd.b])
        p.op("dve", lambda e: e.reciprocal(rstd[:], rstd[:]), reads=[], writes=[rstd.b])
        for c in range(KC):
            xc = xch.get()
            p.dma(xc[:], xs[c, :, :], reads=[xs_b[c]], writes=[xc.b], sem=xc.b)
            tm = nrm_tmp.get()
            p.op("dve", lambda e, tm=tm, xc=xc: e.tensor_tensor(tm[:], xc[:], rstd[:], ALU.mult),
                 reads=[xc.b, rstd.b], writes=[tm.b])
            p.op("dve", lambda e, tm=tm, c=c: e.tensor_scalar(
                hT[:, c, :], tm[:], Gc[:, which, c:c + 1], modc[:, sh_off + c:sh_off + c + 1],
                ALU.mult, ALU.add), reads=[tm.b, Gc.b, modc.b], writes=[hT.b])

    def proj_fm(l, col0, ncols, wsrc=None):
        src = (w_in[l, :, col0:col0 + ncols] if wsrc is None else wsrc)
        wv, wb = wload(src, KC, ncols)
        pa, pb = psa.get(), psa.get()
        for k in range(KC):
            for hf, pp in ((0, pa), (1, pb)):
                p.op("pe", lambda e, wv=wv, k=k, hf=hf, pp=pp: e.matmul(
                    pp[0:ncols, :], wv[:, k, :], hT[:, k, hf * 512:(hf + 1) * 512],
                    start=(k == 0), stop=(k == KC - 1)), reads=[wb, hT.b], writes=[pp.b])
        return pa, pb

    def proj_tm(l, col0, ncols, consume):
        wv, wb = wload(w_in[l, :, col0:col0 + ncols], KC, ncols)
        for tt in range(NT):
            pt = psa.get()
            for k in range(KC):
                p.op("pe", lambda e, wv=wv, k=k, tt=tt, pt=pt: e.matmul(
                    pt[:, 0:ncols], hT[:, k, tt * 128:(tt + 1) * 128], wv[:, k, :],
                    start=(k == 0), stop=(k == KC - 1)), reads=[wb, hT.b], writes=[pt.b])
            consume(tt, pt)

    K.arena_begin(1)
    oattT = K.sb("oattT", [128, 8, T], BF16)
    oglaT = K.sb("oglaT", [128, 4, T], BF16)
    K.arena_end()
    orwT = K.sb("orwT", [128, 4, T], BF16)

    RW_PIECES = [(C_RR + 128 * i, 128) for i in range(4)] + [(C_RK + 128 * i, 128) for i in range(4)] + \
                [(C_RV + 128 * i, 128) for i in range(4)] + [(C_RWD, 64), (C_RWD + 64, 64), (C_RAD, 64), (C_RGD, 128)]
    K.arena_begin(0)
    kap_tm = K.sb("kap_tm", [128, NT, 512], BF16)
    b_tm = K.sb("b_tm", [128, NT, 512], BF16)
    r_tm = K.sb("r_tm", [128, NT, 512], BF16)
    wk_mark = K.aoff[0]
    k_tm = K.sb("k_tm", [128, NT, 512], BF16)
    w_tm = [K.sb("w_tm%d" % d_, [128, NT, 512], BF16) for d_ in range(2)]
    rw_mark = K.aoff[0]
    K.aoff[0] = wk_mark
    wkr = [Tl(K.sb("wkr%d" % i, [128, 4 * 2 * 512], BF16).t, Buf("wkr%d" % i)) for i in range(3)]
    assert K.aoff[0] <= rw_mark
    K.aoff[0] = rw_mark
    prow = K.sb("prow", [128, 3, 512])
    muc = K.sb("muc", [128, 16])
    omc = K.sb("omc", [128, 16])
    hmc = K.sb("hmc", [128, 16])
    aupa = K.sb("aupa", [65, 512])
    TFB = [K.sb("TFB%d" % d_, [65, T]) for d_ in range(2)]
    RA = K.sb("RA", [65, T])
    mpad = K.ring("mpad", 1, [128, T + 2])
    mt_ = K.ring("mt_", 1, [128, T])
    RRc = K.sb("RRc", [128, T])
    RKc = K.sb("RKc", [128, T])
    RVc = K.sb("RVc", [128, T])
    rkv32 = K.ring("rkv32", 2, [128, 384])
    a32 = K.ring("a32", 1, [128, 128])
    wsg = K.ring("wsg", 1, [128, 128])
    kx32 = K.ring("kx32", 1, [128, 128])
    ksq = K.ring("ksq", 1, [128, 128])
    ss2 = K.ring("ss2", 2, [128, 2])
    kap32 = K.ring("kap32", 1, [128, 128])
    kr32 = K.ring("kr32", 1, [128, 128])
    bu32 = K.ring("bu32", 1, [128, 128])
    bs2 = K.ring("bs2", 2, [128, 2])
    ybon = K.ring("ybon", 2, [128, 128])
    K.arena_end()
    K.arena_begin(1)
    VV = K.sb("VV", [128, 8, T], BF16)
    SG = K.sb("SG", [128, T])
    wupa = K.sb("wupa", [65, 2, 512])
    K.arena_end()
    K.arena_begin(0, keep=True)
    K.aoff[0] = rw_mark
    Ysum = K.sb("Ysum", [128, NT, 512])
    gup = K.sb("gup", [128, 512])
    prow2 = K.sb("prow2", [128, 2, 512])
    ln_a = K.ring("ln_a", 2, [128, 512])
    ln_b_ = K.ring("ln_b", 2, [128, 512])
    ln_s = K.ring("ln_s", 2, [128, 8])
    ln_r = K.ring("ln_r", 2, [128, 8])
    ybuf = K.sb("ybuf", [128, 128, 8])
    yrev = K.sb("yrev", [128, 128, 8])
    ybufD = K.sb("ybufDt", [128, 128, 4])
    K.arena_end()
    K.arena_begin(2)
    Sst = K.sb("Sst", [128, 512])
    S0t = K.ring("S0t", 1, [128, 512])
    stmp = K.ring("stmp", 2, [128, 512])
    t4r = K.ring("t4r", 2, [128, 512])
    yjunk = K.sb("yjunk", [128, 64])
    Swr = K.ring("Swr", 2, [128, 512])
    t3r = K.ring("t3r", 2, [128, 512])
    Spr = K.ring("Spr", 2, [128, 512])
    skr = K.ring("skr", 2, [128, 8])
    K.arena_end()
    ysd = nc.dram_tensor("ysd", [NT, 128, 512], F32).ap()
    ysd_b = Buf("ysd")
    wkd = nc.dram_tensor("wkd", [3, T, 512], BF16).ap()
    wkd_b = Buf("wkd")

    def rw_mix(pa, pb, piece, ncols, out_ap, out_b, func=None):
        u_ = mpad.get()
        for hf, pp in ((0, pa), (1, pb)):
            p.op("act", lambda e, pp=pp, hf=hf: e.activation(
                u_[0:ncols, 1 + hf * 512:1 + (hf + 1) * 512], pp[0:ncols, :], AF.Copy), reads=[pp.b], writes=[u_.b])
        t = mt_.get()
        p.op("dve", lambda e: e.tensor_tensor(t[0:ncols, :], u_[0:ncols, 0:T], u_[0:ncols, 2:T + 2], ALU.add),
             reads=[u_.b], writes=[t.b])
        p.op("dve", lambda e: e.scalar_tensor_tensor(
            t[0:ncols, 256:1024:256], u_[0:ncols, 256:1024:256], segf[0:ncols, 2:3], t[0:ncols, 256:1024:256],
            ALU.mult, ALU.add), reads=[u_.b, segf.b], writes=[t.b])
        p.op("dve", lambda e: e.scalar_tensor_tensor(
            t[0:ncols, 255:1023:256], u_[0:ncols, 257:1025:256], segf[0:ncols, 2:3], t[0:ncols, 255:1023:256],
            ALU.mult, ALU.add), reads=[u_.b, segf.b], writes=[t.b])
        p.op("dve", lambda e: e.tensor_scalar(t[0:ncols, :], t[0:ncols, :], hmc[0:ncols, piece:piece + 1], None, ALU.mult),
             reads=[hmc.b], writes=[t.b])
        if func is None:
            p.op("dve", lambda e: e.scalar_tensor_tensor(
                out_ap, u_[0:ncols, 1:T + 1], omc[0:ncols, piece:piece + 1], t[0:ncols, :], ALU.mult, ALU.add),
                reads=[u_.b, omc.b, t.b], writes=[out_b])
        else:
            p.op("dve", lambda e: e.scalar_tensor_tensor(
                t[0:ncols, :], u_[0:ncols, 1:T + 1], omc[0:ncols, piece:piece + 1], t[0:ncols, :], ALU.mult, ALU.add),
                reads=[u_.b, omc.b], writes=[t.b])
            p.op("act", lambda e: e.activation(out_ap, t[0:ncols, :], func), reads=[t.b], writes=[out_b])

    def rwkv_prep(l):
        for u_ in mpad.tl:
            p.op("pool", lambda e, u_=u_: e.memset(u_[:], 0.0), writes=[u_.b])
        col_load(muc[:], muc.b, rw_mu[l, :, :], 16)
        p.op("dve", lambda e: e.tensor_scalar(omc[:], muc[:], -1.0, 1.0, ALU.mult, ALU.add), reads=[muc.b], writes=[omc.b])
        p.op("dve", lambda e: e.tensor_scalar(hmc[:], muc[:], 0.5, None, ALU.mult), reads=[muc.b], writes=[hmc.b])
        p.dma(wupa[:], rw_wup[l].rearrange("d r c -> r d c"), writes=[wupa.b], sem=wupa.b)
        p.dma(aupa[:], rw_aup[l, :, :], writes=[aupa.b], sem=aupa.b)
        p.dma(prow[:], rw_rows[l:l + 1, 0:3, :].to_broadcast([128, 3, 512]), writes=[prow.b], sem=prow.b)
        for d_ in range(2):
            p.op("pool", lambda e, d_=d_: e.memset(TFB[d_][:], 1.0), writes=[TFB[d_].b])
            pa, pb = proj_fm(l, RW_PIECES[12 + d_][0], 64)
            rw_mix(pa, pb, 12 + d_, 64, TFB[d_][0:64, :], TFB[d_].b, func=AF.Tanh)
        p.op("pool", lambda e: e.memset(RA[:], 1.0), writes=[RA.b])
        pa, pb = proj_fm(l, C_RAD, 64)
        rw_mix(pa, pb, 14, 64, RA[0:64, :], RA.b)
        pa, pb = proj_fm(l, C_RGD, 128)
        rw_mix(pa, pb, 15, 128, SG[:], SG.b, func=AF.Sigmoid)
        for c in range(4):
            cs = slice(c * 128, (c + 1) * 128)
            for (col0, dst, piece) in ((C_RR + 128 * c, RRc, c), (C_RK + 128 * c, RKc, 4 + c), (C_RV + 128 * c, RVc, 8 + c)):
                pa, pb = proj_fm(l, col0, 128)
                rw_mix(pa, pb, piece, 128, dst[:], dst.b)
            for h2 in range(2):
                sel = sel2[:, h2, :]
                for hf in range(2):
                    pv = ps.get()
                    p.op("pe", lambda e, pv=pv, sel=sel, hf=hf: e.matmul(
                        pv[:], sel, RVc[:, hf * 512:(hf + 1) * 512], start=True, stop=True),
                        reads=[RVc.b, sel2.b], writes=[pv.b])
                    p.op("act", lambda e, pv=pv, hf=hf, h=2 * c + h2: e.activation(
                        VV[0:64, h, hf * 512:(hf + 1) * 512], pv[0:64, :], AF.Copy), reads=[pv.b], writes=[VV.b])
                    p.op("act", lambda e, pv=pv, hf=hf, h=2 * c + h2: e.activation(
                        VV[64:128, h, T - (hf + 1) * 512:T - hf * 512], pv[64:128, ::-1], AF.Copy),
                        reads=[pv.b], writes=[VV.b])
            for tt in range(NT):
                ts_ = slice(tt * 128, (tt + 1) * 128)
                pt = ps.get()
                for i, src in enumerate((RRc, RKc, RVc)):
                    p.op("pe", lambda e, pt=pt, i=i, src=src, ts_=ts_: e.transpose(
                        pt[:, i * 128:(i + 1) * 128], src[:, ts_], ident[:]), reads=[src.b, ident.b], writes=[pt.b])
                x3 = rkv32.get()
                p.op("act", lambda e, pt=pt, x3=x3: e.activation(x3[:], pt[:, 0:384], AF.Copy), reads=[pt.b], writes=[x3.b])
                r32, rk32, v32 = x3[:, 0:128], x3[:, 128:256], x3[:, 256:384]
                pz = ps.get()
                p.op("pe", lambda e, pz=pz, ts_=ts_, cs=cs: e.matmul(pz[:, 0:128], RA[:, ts_], aupa[:, cs], start=True, stop=True),
                     reads=[RA.b, aupa.b], writes=[pz.b])
                a_ = a32.get()
                p.op("act", lambda e, pz=pz, a_=a_: e.activation(a_[:], pz[:, 0:128], AF.Sigmoid), reads=[pz.b], writes=[a_.b])
                for d_ in range(2):
                    pw = ps.get()
                    p.op("pe", lambda e, pw=pw, ts_=ts_, cs=cs, d_=d_: e.matmul(
                        pw[:, 0:128], TFB[d_][:, ts_], wupa[:, d_, cs], start=True, stop=True),
                        reads=[TFB[d_].b, wupa.b], writes=[pw.b])
                    ws_ = wsg.get()
                    p.op("act", lambda e, pw=pw, ws_=ws_: e.activation(ws_[:], pw[:, 0:128], AF.Sigmoid), reads=[pw.b], writes=[ws_.b])
                    p.op("act", lambda e, ws_=ws_, d_=d_, tt=tt, cs=cs: e.activation(
                        w_tm[d_][:, tt, cs], ws_[:], AF.Exp, scale=-0.606531), reads=[ws_.b], writes=[w_tm[d_].b])
                kx = kx32.get()
                p.op("dve", lambda e, kx=kx, rk32=rk32, cs=cs: e.tensor_tensor(kx[:], rk32, prow[:, 0, cs], ALU.mult),
                     reads=[x3.b, prow.b], writes=[kx.b])
                kq = ksq.get()
                p.op("dve", lambda e, kx=kx, kq=kq: e.tensor_tensor(kq[:], kx[:], kx[:], ALU.mult), reads=[kx.b], writes=[kq.b])
                s2 = ss2.get()
                p.op("dve", lambda e, kq=kq, s2=s2: e.tensor_reduce(
                    s2[:], kq[:].rearrange("p (h k) -> p h k", k=64), AX.X, ALU.add), reads=[kq.b], writes=[s2.b])
                p.op("dve", lambda e, s2=s2: e.tensor_scalar(s2[:], s2[:], EPS, None, ALU.add), reads=[], writes=[s2.b])
                p.op("act", lambda e, s2=s2: e.activation(s2[:], s2[:], AF.Sqrt), reads=[], writes=[s2.b])
                p.op("dve", lambda e, s2=s2: e.reciprocal(s2[:], s2[:]), reads=[], writes=[s2.b])
                kp = kap32.get()
                p.op("dve", lambda e, kp=kp, kx=kx, s2=s2: e.tensor_tensor(
                    kp[:].rearrange("p (h k) -> p h k", k=64), kx[:].rearrange("p (h k) -> p h k", k=64),
                    s2[:].unsqueeze(2).to_broadcast([128, 2, 64]), ALU.mult), reads=[kx.b, s2.b], writes=[kp.b])
                p.op("act", lambda e, kp=kp, tt=tt, cs=cs: e.activation(kap_tm[:, tt, cs], kp[:], AF.Copy),
                     reads=[kp.b], writes=[kap_tm.b])
                p.op("dve", lambda e, kp=kp, a_=a_, tt=tt, cs=cs: e.tensor_tensor(b_tm[:, tt, cs], kp[:], a_[:], ALU.mult),
                     reads=[kp.b, a_.b], writes=[b_tm.b])
                kr = kr32.get()
                p.op("dve", lambda e, kr=kr, a_=a_, cs=cs: e.scalar_tensor_tensor(
                    kr[:], a_[:], -1.0, prow[:, 1, cs], ALU.add, ALU.mult), reads=[a_.b, prow.b], writes=[kr.b])
                p.op("dve", lambda e, kr=kr, rk32=rk32: e.scalar_tensor_tensor(
                    kr[:], kr[:], 1.0, rk32, ALU.add, ALU.mult), reads=[x3.b], writes=[kr.b])
                p.op("act", lambda e, kr=kr, tt=tt, cs=cs: e.activation(k_tm[:, tt, cs], kr[:], AF.Copy),
                     reads=[kr.b], writes=[k_tm.b])
                p.op("act", lambda e, r32=r32, tt=tt, cs=cs: e.activation(r_tm[:, tt, cs], r32, AF.Copy),
                     reads=[x3.b], writes=[r_tm.b])
                bu = bu32.get()
                p.op("dve", lambda e, bu=bu, kr=kr, r32=r32: e.tensor_tensor(bu[:], kr[:], r32, ALU.mult),
                     reads=[kr.b, x3.b], writes=[bu.b])
                p.op("dve", lambda e, bu=bu, cs=cs: e.tensor_tensor(bu[:], bu[:], prow[:, 2, cs], ALU.mult),
                     reads=[prow.b], writes=[bu.b])
                b2 = bs2.get()
                p.op("dve", lambda e, bu=bu, b2=b2: e.tensor_reduce(
                    b2[:], bu[:].rearrange("p (h k) -> p h k", k=64), AX.X, ALU.add), reads=[bu.b], writes=[b2.b])
                yb_ = ybon.get()
                p.op("dve", lambda e, b2=b2, v32=v32, yb_=yb_: e.tensor_tensor(
                    yb_[:].rearrange("p (h k) -> p h k", k=64), v32.rearrange("p (h k) -> p h k", k=64),
                    b2[:].unsqueeze(2).to_broadcast([128, 2, 64]), ALU.mult), reads=[b2.b, x3.b], writes=[yb_.b])
                p.dma(ysd[tt, :, cs], yb_[:], reads=[yb_.b], writes=[ysd_b], sem=yb_.b)

    ybD = Buf("ybufD")

    def rwkv_scan(l):
        ps8 = K.ps8
        p.dma(Ysum[:], ysd.rearrange("t p c -> p t c"), reads=[ysd_b], writes=[Ysum.b], sem=Ysum.b)
        p.dma(gup[:], rw_gup[l, :, :], writes=[gup.b], sem=gup.b)
        p.dma(prow2[:], rw_rows[l:l + 1, 3:5, :].to_broadcast([128, 2, 512]), writes=[prow2.b], sem=prow2.b)
        p.op("pool", lambda e: e.memset(Sst[:], 0.0), writes=[Sst.b])
        p.op("pool", lambda e: e.memset(ybuf[:], 0.0), writes=[ybuf.b])
        p.op("pool", lambda e: e.memset(ybufD[:], 0.0), writes=[ybufD.b])
        for j_, src_ in enumerate((w_tm[0], w_tm[1], k_tm)):
            p.dma(wkd[j_].rearrange("(t p) c -> p t c", p=128), src_[:], reads=[src_.b], writes=[wkd_b], sem=wkd_b)
        srcs = [kap_tm, b_tm, r_tm]
        WB = 4
        for i in range(T):
            tf, tb = i, T - 1 - i
            q = i // 256
            if i % 256 == 0:
                s0 = S0t.get()
                p.dma(s0[:], sr0_in[l, q, :, :], writes=[s0.b], sem=s0.b)
                p.op("dve", lambda e, s0=s0: e.scalar_tensor_tensor(
                    Sst[:], Sst[:], segf[:, 1:2], s0[:], ALU.mult, ALU.add), reads=[s0.b, segf.b], writes=[Sst.b])
            if i % WB == 0:
                wk = wkr[(i // WB) % 3]
                wk4 = wk[:].rearrange("p (s a c) -> p s a c", a=2, c=512)
                for a_, (jf, jb) in enumerate(((0, 1), (2, 2))):
                    p.dma(wk4[0:64, :, a_, :], wkd[jf:jf + 1, i:i + WB, :].to_broadcast([64, WB, 512]),
                          reads=[wkd_b], writes=[wk.b], sem=wk.b)
                    p.dma(wk4[64:128, :, a_, :], wkd[jb:jb + 1, T - WB - i:T - i, :][:, ::-1, :].to_broadcast([64, WB, 512]),
                          reads=[wkd_b], writes=[wk.b], sem=wk.b)
            w_bc = wk4[:, i % WB, 0, :]
            k_bc = wk4[:, i % WB, 1, :]
            banks = []
            for sf in srcs:
                bk = ps8.get()
                p.op("pe", lambda e, bk=bk, sf=sf, tf=tf: e.matmul(
                    bk[0:64, :], ident_b[:, tf % 128:tf % 128 + 1].to_broadcast([128, 64]), sf[:, tf // 128, :],
                    start=True, stop=True), reads=[sf.b, ident_b.b], writes=[bk.b])
                p.op("pe", lambda e, bk=bk, sf=sf, tb=tb: e.matmul(
                    bk[64:128, :], ident_b[:, tb % 128:tb % 128 + 1].to_broadcast([128, 64]), sf[:, tb // 128, :],
                    start=True, stop=True), reads=[sf.b, ident_b.b], writes=[bk.b])
                banks.append(bk)
            kb_, bb_, rb_ = banks
            t3 = t3r.get()
            p.op("pool", lambda e, t3=t3, k_bc=k_bc, i=i: e.tensor_tensor(
                t3[:].rearrange("p (h k) -> p h k", k=64), k_bc.rearrange("p (h k) -> p h k", k=64),
                VV[:, :, i:i + 1].to_broadcast([128, 8, 64]), ALU.mult), reads=[wk.b, VV.b], writes=[t3.b])
            Sw = Swr.get()
            p.op("pool", lambda e, Sw=Sw, w_bc=w_bc: e.tensor_tensor(Sw[:], Sst[:], w_bc, ALU.mult),
                 reads=[Sst.b, wk.b], writes=[Sw.b])
            Sp = Spr.get()
            p.op("pool", lambda e, Sp=Sp, Sw=Sw, t3=t3: e.tensor_tensor(Sp[:], Sw[:], t3[:], ALU.add),
                 reads=[Sw.b, t3.b], writes=[Sp.b])
            t1 = stmp.get()
            p.op("dve", lambda e, t1=t1, kb_=kb_: e.tensor_tensor(t1[:], Sst[:], kb_[:], ALU.mult),
                 reads=[Sst.b, kb_.b], writes=[t1.b], ns=NS)
            sk = skr.get()
            p.op("dve", lambda e, t1=t1, sk=sk: e.tensor_reduce(
                sk[:], t1[:].rearrange("p (h k) -> p h k", k=64), AX.X, ALU.add), reads=[t1.b], writes=[sk.b], ns=NS)
            t2 = stmp.get()
            p.op("dve", lambda e, t2=t2, bb_=bb_, sk=sk: e.tensor_tensor(
                t2[:].rearrange("p (h k) -> p h k", k=64), bb_[:].rearrange("p (h k) -> p h k", k=64),
                sk[:].unsqueeze(2).to_broadcast([128, 8, 64]), ALU.mult), reads=[bb_.b, sk.b], writes=[t2.b], ns=NS)
            p.op("dve", lambda e, t2=t2, Sp=Sp: e.tensor_tensor(Sst[:], Sp[:], t2[:], ALU.subtract),
                 reads=[t2.b, Sp.b], writes=[Sst.b], ns=NS)
            t4 = t4r.get()
            p.op("dve", lambda e, t4=t4, rb_=rb_: e.tensor_tensor(t4[:], Sst[:], rb_[:], ALU.mult),
                 reads=[Sst.b, rb_.b], writes=[t4.b], ns=NS)
            NA = 4
            for h_ in range(NA):
                p.op("act", lambda e, t4=t4, i=i, h_=h_: e.activation(
                    yjunk[:], t4[:, h_ * 64:(h_ + 1) * 64], AF.Copy, accum_out=ybuf[:, i % 128, h_:h_ + 1]),
                    reads=[t4.b], writes=[ybuf.b], ns=True)
            p.op("dve", lambda e, t4=t4, i=i: e.tensor_reduce(
                ybufD[:, i % 128, :], t4[:, NA * 64:512].rearrange("p (h k) -> p h k", k=64), AX.X, ALU.add),
                reads=[t4.b], writes=[ybufD.b], ns=NS)
            if i % 128 == 127:
                p.op("act", lambda e: e.activation(yrev[0:64, :, 0:4], ybuf[0:64, :, 0:4], AF.Copy), reads=[ybuf.b], writes=[yrev.b])
                p.op("act", lambda e: e.activation(yrev[64:128, :, 0:4], ybuf[64:128, ::-1, 0:4], AF.Copy), reads=[ybuf.b], writes=[yrev.b])
                p.op("act", lambda e: e.activation(yrev[0:64, :, 4:8], ybufD[0:64, :, :], AF.Copy), reads=[ybufD.b], writes=[yrev.b])
                p.op("act", lambda e: e.activation(yrev[64:128, :, 4:8], ybufD[64:128, ::-1, :], AF.Copy), reads=[ybufD.b], writes=[yrev.b])
                for hg in range(2):
                    pt = ps8.get()
                    for h4 in range(4):
                        h = hg * 4 + h4
                        p.op("pe", lambda e, pt=pt, h=h, h4=h4: e.transpose(
                            pt[:, h4 * 128:(h4 + 1) * 128], yrev[:, :, h], ident[:]), reads=[yrev.b, ident.b], writes=[pt.b])
                    for (lo, tile_) in ((0, i // 128), (64, 7 - i // 128)):
                        p.op("dve", lambda e, pt=pt, lo=lo, tile_=tile_, hg=hg: e.tensor_tensor(
                            Ysum[:, tile_, hg * 256:(hg + 1) * 256].rearrange("p (h k) -> p h k", k=64),
                            Ysum[:, tile_, hg * 256:(hg + 1) * 256].rearrange("p (h k) -> p h k", k=64),
                            pt[:].rearrange("p (h k) -> p h k", k=128)[:, :, lo:lo + 64], ALU.add),
                            reads=[pt.b], writes=[Ysum.b])
            if i % 256 == 255:
                p.dma(nr_out[l, 0, q, :, :], Sst[0:64, :], reads=[Sst.b], writes=[nr_ob], sem=Sst.b)
                p.dma(nr_out[l, 1, 3 - q, :, :], Sst[64:128, :], reads=[Sst.b], writes=[nr_ob], sem=Sst.b)

    def rwkv_post(l):
        for tt in range(NT):
            ts_ = slice(tt * 128, (tt + 1) * 128)
            y3 = Ysum[:, tt, :].rearrange("p (h k) -> p h k", k=64)
            sm = ln_s.get()
            p.op("dve", lambda e, sm=sm, y3=y3: e.tensor_reduce(sm[:], y3, AX.X, ALU.add), reads=[Ysum.b], writes=[sm.b])
            p.op("dve", lambda e, sm=sm: e.tensor_scalar(sm[:], sm[:], 1.0 / 64, None, ALU.mult), reads=[], writes=[sm.b])
            xc = ln_a.get()
            xc3 = xc[:].rearrange("p (h k) -> p h k", k=64)
            p.op("dve", lambda e, xc3=xc3, y3=y3, sm=sm: e.tensor_tensor(
                xc3, y3, sm[:].unsqueeze(2).to_broadcast([128, 8, 64]), ALU.subtract), reads=[Ysum.b, sm.b], writes=[xc.b])
            sq = ln_b_.get()
            p.op("dve", lambda e, sq=sq, xc=xc: e.tensor_tensor(sq[:], xc[:], xc[:], ALU.mult), reads=[xc.b], writes=[sq.b])
            vr = ln_r.get()
            p.op("dve", lambda e, vr=vr, sq=sq: e.tensor_reduce(
                vr[:], sq[:].rearrange("p (h k) -> p h k", k=64), AX.X, ALU.add), reads=[sq.b], writes=[vr.b])
            p.op("dve", lambda e, vr=vr: e.tensor_scalar(vr[:], vr[:], 1.0 / 64, 64e-5, ALU.mult, ALU.add), reads=[], writes=[vr.b])
            p.op("act", lambda e, vr=vr: e.activation(vr[:], vr[:], AF.Sqrt), reads=[], writes=[vr.b])
            p.op("dve", lambda e, vr=vr: e.reciprocal(vr[:], vr[:]), reads=[], writes=[vr.b])
            p.op("dve", lambda e, xc3=xc3, vr=vr: e.tensor_tensor(
                xc3, xc3, vr[:].unsqueeze(2).to_broadcast([128, 8, 64]), ALU.mult), reads=[vr.b], writes=[xc.b])
            p.op("dve", lambda e, xc=xc: e.tensor_tensor(xc[:], xc[:], prow2[:, 0, :], ALU.mult), reads=[prow2.b], writes=[xc.b])
            p.op("dve", lambda e, xc=xc: e.tensor_tensor(xc[:], xc[:], prow2[:, 1, :], ALU.add), reads=[prow2.b], writes=[xc.b])
            pg = ps.get()
            p.op("pe", lambda e, pg=pg, ts_=ts_: e.matmul(pg[:], SG[:, ts_], gup[:], start=True, stop=True),
                 reads=[SG.b, gup.b], writes=[pg.b])
            p.op("dve", lambda e, xc=xc, pg=pg: e.tensor_tensor(xc[:], xc[:], pg[:], ALU.mult), reads=[pg.b], writes=[xc.b])
            pt = ps.get()
            for c in range(4):
                p.op("pe", lambda e, pt=pt, xc=xc, c=c: e.transpose(
                    pt[:, c * 128:(c + 1) * 128], xc[:, c * 128:(c + 1) * 128], ident[:]), reads=[xc.b, ident.b], writes=[pt.b])
            p.op("act", lambda e, pt=pt, ts_=ts_: e.activation(
                orwT[:, :, ts_], pt[:].rearrange("p (c t) -> p c t", t=128), AF.Copy), reads=[pt.b], writes=[orwT.b])

    K.arena_begin()
    cosT = K.sb("cosT", [128, T])
    sinT = K.sb("sinT", [128, T])
    Kall = K.sb("Kall", [128, 2, 1280], BF16)
    Vall = K.sb("Vall", [128, 10, 256], BF16)
    hraw = K.ring("hraw", 2, [128, T])
    hsq = K.ring("hsq", 1, [128, T])
    hrs = K.ring("hrs", 1, [128, T])
    hrot = K.ring("hrot", 2, [128, T])
    ktm = K.ring("ktm", 2, [128, 128])
    vtm = K.ring("vtm", 2, [128, 256])
    cst = K.ring("cst", 2, [128, 128])

    def head_norm_rope(pa, pb, gidx, out_ap, out_b):
        raw = hraw.get()
        sq = hsq.get()
        for hf, pp in ((0, pa), (1, pb)):
            sl = slice(hf * 512, (hf + 1) * 512)
            p.op("act", lambda e, pp=pp, sl=sl: e.activation(raw[:, sl], pp[:], AF.Copy),
                 reads=[pp.b], writes=[raw.b])
            p.op("act", lambda e, pp=pp, sl=sl: e.activation(sq[:, sl], pp[:], AF.Square),
                 reads=[pp.b], writes=[sq.b])
        rs = hrs.get()
        for hf in range(2):
            sl = slice(hf * 512, (hf + 1) * 512)
            pq = ps.get()
            p.op("pe", lambda e, pq=pq, sl=sl: e.matmul(pq[:], ones_f[:], sq[:, sl], start=True, stop=True),
                 reads=[sq.b, ones_f.b], writes=[pq.b])
            p.op("dve", lambda e, pq=pq, sl=sl: e.tensor_scalar(
                rs[:, sl], pq[:], 1.0 / 128, EPS, ALU.mult, ALU.add), reads=[pq.b], writes=[rs.b])
        p.op("act", lambda e: e.activation(rs[:], rs[:], AF.Sqrt), reads=[], writes=[rs.b])
        p.op("dve", lambda e: e.reciprocal(rs[:], rs[:]), reads=[], writes=[rs.b])
        p.op("dve", lambda e: e.scalar_tensor_tensor(raw[:], raw[:], qkn[:, gidx:gidx + 1], rs[:], ALU.mult, ALU.mult),
             reads=[rs.b, qkn.b], writes=[raw.b])
        rt = hrot.get()
        for hf in range(2):
            sl = slice(hf * 512, (hf + 1) * 512)
            pq = ps.get()
            p.op("pe", lambda e, pq=pq, sl=sl: e.matmul(pq[:], rotT[:], raw[:, sl], start=True, stop=True),
                 reads=[raw.b, rotT.b], writes=[pq.b])
            p.op("dve", lambda e, pq=pq, sl=sl: e.tensor_tensor(rt[:, sl], pq[:], sinT[:, sl], ALU.mult),
                 reads=[pq.b, sinT.b], writes=[rt.b])
        p.op("dve", lambda e: e.tensor_tensor(raw[:], raw[:], cosT[:], ALU.mult), reads=[cosT.b], writes=[raw.b])
        p.op("dve", lambda e: e.tensor_tensor(out_ap, raw[:], rt[:], ALU.add), reads=[raw.b, rt.b], writes=[out_b])

    kf32 = K.ring("kf32", 2, [128, T])

    def attention_kv(l):
        p.dma(cosT[:], cos_in[:, :], writes=[cosT.b], sem=cosT.b)
        p.dma(sinT[:], sin_in[:, :], writes=[sinT.b], sem=sinT.b)
        for j in range(2):
            for tt in range(2):
                c_ = cst.get()
                p.dma(c_[:], ck_in[l, j, tt * 128:(tt + 1) * 128, :], writes=[c_.b], sem=c_.b)
                pt = ps.get()
                p.op("pe", lambda e, pt=pt, c_=c_: e.transpose(pt[:, 0:128], c_[:], ident[:]),
                     reads=[c_.b, ident.b], writes=[pt.b])
                p.op("act", lambda e, pt=pt, j=j, tt=tt: e.activation(
                    Kall[:, j, tt * 128:(tt + 1) * 128], pt[:, 0:128], AF.Copy), reads=[pt.b], writes=[Kall.b])
                c2 = cst.get()
                p.dma(c2[:], cv_in[l, j, tt * 128:(tt + 1) * 128, :], writes=[c2.b], sem=c2.b)
                p.op("act", lambda e, c2=c2, j=j, tt=tt: e.activation(
                    Vall[:, tt, j * 128:(j + 1) * 128], c2[:], AF.Copy), reads=[c2.b], writes=[Vall.b])
        if stage == 3.1:
            return
        for j in range(2):
            pa, pb = proj_fm(l, C_AK + 128 * j, 128)
            if stage == 3.2:
                kf = kf32.get()
                for hf, pp in ((0, pa), (1, pb)):
                    p.op("act", lambda e, pp=pp, hf=hf, kf=kf: e.activation(kf[:, hf * 512:(hf + 1) * 512], pp[:], AF.Copy),
                         reads=[pp.b], writes=[kf.b])
                continue
            kf = kf32.get()
            head_norm_rope(pa, pb, 1, kf[:], kf.b)
            p.op("act", lambda e, kf=kf, j=j: e.activation(Kall[:, j, 256:1280], kf[:], AF.Copy),
                 reads=[kf.b], writes=[Kall.b])
            for tt in range(NT):
                pt = ps.get()
                p.op("pe", lambda e, pt=pt, kf=kf, tt=tt: e.transpose(
                    pt[:, 0:128], kf[:, tt * 128:(tt + 1) * 128], ident[:]),
                    reads=[kf.b, ident.b], writes=[pt.b])
                kt = ktm.get()
                p.op("dve", lambda e, pt=pt, kt=kt: e.tensor_copy(kt[:], pt[:, 0:128]), reads=[pt.b], writes=[kt.b])
                p.dma(nk_out[l, j, tt * 128:(tt + 1) * 128, :], kt[:], reads=[kt.b], writes=[nk_ob], sem=kt.b)
        if stage <= 3.3:
            return
        for half in range(2):
            def cons_half(tt, pt, half=half):
                vt = vtm.get()
                p.op("act", lambda e: e.activation(vt[:, 0:128], pt[:, 0:128], AF.Copy), reads=[pt.b], writes=[vt.b])
                p.op("dve", lambda e: e.tensor_copy(Vall[:, 2 + tt, half * 128:(half + 1) * 128], vt[:, 0:128]),
                     reads=[vt.b], writes=[Vall.b])
                p.dma(nv_out[l, half, tt * 128:(tt + 1) * 128, :], vt[:, 0:128],
                      reads=[vt.b], writes=[nv_ob], sem=vt.b)
            proj_tm(l, C_AV + 128 * half, 128, cons_half)


    qf32 = K.ring("qf32", 2, [128, T])
    qbf = K.ring("qbf", 2, [128, T], BF16)
    pexp = K.ring("pexp", 3, [128, 256], BF16)
    rden = K.ring("rden", 2, [128, 256])
    K.arena_end(limit=18432)

    modgen = {}

    def attention(l):
        gen = modgen[l] = mod_deferred(l)
        for h in range(8):
            pa, pb = proj_fm(l, C_AQ + 128 * h, 128)
            qf = qf32.get()
            head_norm_rope(pa, pb, 0, qf[:], qf.b)
            qb = qbf.get()
            p.op("act", lambda e, qb=qb, qf=qf: e.activation(qb[:], qf[:], AF.Copy), reads=[qf.b], writes=[qb.b])
            j = h // 4
            for qt in range(4):
                po, pd = psa.get(), psa.get()
                qs = slice(qt * 256, (qt + 1) * 256)
                for kt in range(10):
                    pS = ps.get()
                    p.op("pe", lambda e, pS=pS, kt=kt, qb=qb, qs=qs, j=j: e.matmul(
                        pS[:, 0:256], Kall[:, j, kt * 128:(kt + 1) * 128], qb[:, qs], start=True, stop=True),
                        reads=[Kall.b, qb.b], writes=[pS.b])
                    pe_ = pexp.get()
                    p.op("act", lambda e, pS=pS, pe_=pe_, qt=qt, kt=kt: e.activation(
                        pe_[:], pS[:, 0:256], AF.Exp, bias=amask[:, qt * 10 + kt:qt * 10 + kt + 1],
                        scale=float(128 ** -0.5)), reads=[pS.b, amask.b], writes=[pe_.b])
                    p.op("pe", lambda e, po=po, pe_=pe_, kt=kt, j=j: e.matmul(
                        po[:, 0:256], Vall[:, kt, j * 128:(j + 1) * 128], pe_[:], start=(kt == 0), stop=(kt == 9)),
                        reads=[Vall.b, pe_.b], writes=[po.b])
                    p.op("pe", lambda e, pd=pd, pe_=pe_, kt=kt: e.matmul(
                        pd[:, 0:256], ones_b[:], pe_[:], start=(kt == 0), stop=(kt == 9)),
                        reads=[ones_b.b, pe_.b], writes=[pd.b])
                next(gen, None)
                rd = rden.get()
                p.op("dve", lambda e, rd=rd, pd=pd: e.reciprocal(rd[:], pd[:, 0:256]), reads=[pd.b], writes=[rd.b])
                p.op("dve", lambda e, rd=rd, po=po, h=h, qs=qs: e.tensor_tensor(
                    oattT[:, h, qs], po[:, 0:256], rd[:], ALU.mult), reads=[po.b, rd.b], writes=[oattT.b])

    K.arena_begin()
    gfa = [K.sb("gfa%d" % d_, [17, T]) for d_ in range(2)]
    aup = K.sb("aup", [17, 2, 512])
    glan = K.sb("glan", [128, 1])
    gq = K.sb("gq", [128, T])
    gk = K.sb("gk", [128, T])
    gktm = K.sb("gktm", [128, NT, 128])
    gvtm = K.sb("gvtm", [128, NT, 128], BF16)
    lp = [K.sb("lp%d" % d_, [128, NT, 128]) for d_ in range(2)]
    og = K.sb("og", [128, T])
    Sf2 = [K.sb("Sf%d" % d_, [128, 128]) for d_ in range(2)]
    Sb2 = [K.sb("Sb%d" % d_, [128, 128], BF16) for d_ in range(2)]
    s0t = K.ring("s0t", 2, [128, 128])
    gtmp = K.ring("gtmp", 6, [128, 128])
    gez = K.ring("gez", 2, [128, 128])
    qeb = K.ring("qeb", 2, [128, 128], BF16)
    keb = K.ring("keb", 2, [128, 128], BF16)
    klb = K.ring("klb", 2, [128, 128], BF16)
    attb = K.ring("attb", 2, [128, 128], BF16)
    dcol = K.ring("dcol", 2, [128, 1])
    gsq = K.sb("gsq", [128, T])
    grs = K.sb("grs", [128, T])
    gsg = K.sb("gsg", [128, T])
    K.arena_end(limit=18432)

    def gla(l):
        col_load(glan[:], glan.b, gla_norm[l, :, :], 1)
        p.dma(aup[:], gla_aup[l].rearrange("d r c -> r d c"), writes=[aup.b], sem=aup.b)
        for d_ in range(2):
            p.op("pool", lambda e, d_=d_: e.memset(gfa[d_][:], 1.0), writes=[gfa[d_].b])
            pa, pb = proj_fm(l, C_GAD + 16 * d_, 16)
            for hf, pp in ((0, pa), (1, pb)):
                p.op("act", lambda e, pp=pp, hf=hf, d_=d_: e.activation(
                    gfa[d_][0:16, hf * 512:(hf + 1) * 512], pp[0:16, :], AF.Copy), reads=[pp.b], writes=[gfa[d_].b])
        for h in range(4):
            pa, pb = proj_fm(l, C_GQ + 128 * h, 128)
            for hf, pp in ((0, pa), (1, pb)):
                p.op("act", lambda e, pp=pp, hf=hf: e.activation(
                    gq[:, hf * 512:(hf + 1) * 512], pp[:], AF.Copy, scale=float(128 ** -0.5)), reads=[pp.b], writes=[gq.b])
            pa, pb = proj_fm(l, C_GK + 128 * h, 128)
            for hf, pp in ((0, pa), (1, pb)):
                p.op("act", lambda e, pp=pp, hf=hf: e.activation(
                    gk[:, hf * 512:(hf + 1) * 512], pp[:], AF.Copy), reads=[pp.b], writes=[gk.b])
            proj_tm(l, C_GK + 128 * h, 128, lambda tt, pt: p.op(
                "act", lambda e: e.activation(gktm[:, tt, :], pt[:, 0:128], AF.Copy), reads=[pt.b], writes=[gktm.b]))
            proj_tm(l, C_GV + 128 * h, 128, lambda tt, pt: p.op(
                "act", lambda e: e.activation(gvtm[:, tt, :], pt[:, 0:128], AF.Copy), reads=[pt.b], writes=[gvtm.b]))
            for d_ in range(2):
                for tt in range(NT):
                    pz = ps.get()
                    p.op("pe", lambda e, pz=pz, d_=d_, tt=tt, h=h: e.matmul(
                        pz[:, 0:128], gfa[d_][:, tt * 128:(tt + 1) * 128], aup[:, d_, h * 128:(h + 1) * 128],
                        start=True, stop=True), reads=[gfa[d_].b, aup.b], writes=[pz.b])
                    ez = gez.get()
                    p.op("act", lambda e, pz=pz, ez=ez: e.activation(ez[:], pz[:, 0:128], AF.Exp, scale=-1.0),
                         reads=[pz.b], writes=[ez.b])
                    p.op("act", lambda e, ez=ez, d_=d_, tt=tt: e.activation(lp[d_][:, tt, :], ez[:], AF.Ln, bias=1.0),
                         reads=[ez.b], writes=[lp[d_].b])
            p.op("pool", lambda e: e.memset(og[:], 0.0), writes=[og.b])
            for d_ in range(2):
                p.op("pool", lambda e, d_=d_: e.memset(Sf2[d_][:], 0.0), writes=[Sf2[d_].b])
            for idx_ in range(2 * NT):
                if idx_ % 2 == 1:
                    next(modgen[l], None)
                d_ = idx_ % 2
                n = (idx_ // 2) if d_ == 0 else (NT - 1 - idx_ // 2)
                Sf, Sb_ = Sf2[d_], Sb2[d_]
                mU, mLs, dc = (0, 3, 127) if d_ == 0 else (1, 2, 0)
                for n in (n,):
                    seg = n // 2
                    ts_ = slice(n * 128, (n + 1) * 128)
                    is_start = (n % 2 == 0) if d_ == 0 else (n % 2 == 1)
                    if is_start:
                        s0 = s0t.get()
                        p.dma(s0[:], sg0_in[l, d_, seg, h, :, :], writes=[s0.b], sem=s0.b)
                        p.op("dve", lambda e, s0=s0, Sf=Sf: e.scalar_tensor_tensor(
                            Sf[:], Sf[:], segf[:, 1:2], s0[:], ALU.mult, ALU.add), reads=[s0.b, segf.b], writes=[Sf.b])
                        p.op("act", lambda e, Sf=Sf, Sb_=Sb_: e.activation(Sb_[:], Sf[:], AF.Copy), reads=[Sf.b], writes=[Sb_.b])
                    pc = ps.get()
                    p.op("pe", lambda e, pc=pc, n=n, d_=d_, mU=mU: e.matmul(
                        pc[:, 0:128], lp[d_][:, n, :], tri[:, mU, :], start=True, stop=True),
                        reads=[lp[d_].b, tri.b], writes=[pc.b])
                    t1, t2, t3 = gtmp.get(), gtmp.get(), gtmp.get()
                    p.op("act", lambda e, pc=pc, t1=t1: e.activation(t1[:], pc[:, 0:128], AF.Exp, scale=-1.0 / 16),
                         reads=[pc.b], writes=[t1.b])
                    p.op("act", lambda e, pc=pc, t2=t2: e.activation(t2[:], pc[:, 0:128], AF.Exp, scale=1.0 / 16),
                         reads=[pc.b], writes=[t2.b])
                    dcl = dcol.get()
                    p.op("act", lambda e, pc=pc, dcl=dcl, dc=dc: e.activation(
                        dcl[:], pc[:, dc:dc + 1], AF.Exp, scale=-1.0 / 16), reads=[pc.b], writes=[dcl.b])
                    qe, ke = qeb.get(), keb.get()
                    p.op("dve", lambda e, qe=qe, t1=t1, ts_=ts_: e.tensor_tensor(qe[:], gq[:, ts_], t1[:], ALU.mult),
                         reads=[gq.b, t1.b], writes=[qe.b])
                    p.op("dve", lambda e, ke=ke, t2=t2, ts_=ts_: e.tensor_tensor(ke[:], gk[:, ts_], t2[:], ALU.mult),
                         reads=[gk.b, t2.b], writes=[ke.b])
                    pg = ps.get()
                    p.op("pe", lambda e, pg=pg, n=n, d_=d_, mLs=mLs: e.matmul(
                        pg[:, 0:128], tri[:, mLs, :], lp[d_][:, n, :], start=True, stop=True),
                        reads=[lp[d_].b, tri.b], writes=[pg.b])
                    p.op("act", lambda e, pg=pg, t3=t3: e.activation(t3[:], pg[:, 0:128], AF.Exp, scale=-1.0 / 16),
                         reads=[pg.b], writes=[t3.b])
                    kl = klb.get()
                    p.op("dve", lambda e, kl=kl, t3=t3, n=n: e.tensor_tensor(kl[:], gktm[:, n, :], t3[:], ALU.mult),
                         reads=[gktm.b, t3.b], writes=[kl.b])
                    pat = ps.get()
                    p.op("pe", lambda e, pat=pat, ke=ke, qe=qe: e.matmul(pat[:, 0:128], ke[:], qe[:], start=True, stop=True),
                         reads=[ke.b, qe.b], writes=[pat.b])
                    ab = attb.get()
                    p.op("dve", lambda e, ab=ab, pat=pat, mU=mU: e.tensor_tensor(ab[:], pat[:, 0:128], tri[:, mU, :], ALU.mult),
                         reads=[pat.b, tri.b], writes=[ab.b])
                    po = psa.get()
                    p.op("pe", lambda e, po=po, ab=ab, n=n: e.matmul(po[:, 0:128], gvtm[:, n, :], ab[:], start=True, stop=False),
                         reads=[gvtm.b, ab.b], writes=[po.b])
                    p.op("pe", lambda e, po=po, qe=qe, Sb_=Sb_: e.matmul(po[:, 0:128], Sb_[:], qe[:], start=False, stop=True),
                         reads=[Sb_.b, qe.b], writes=[po.b])
                    p.op("dve", lambda e, po=po, ts_=ts_: e.tensor_tensor(og[:, ts_], og[:, ts_], po[:, 0:128], ALU.add),
                         reads=[po.b], writes=[og.b])
                    pst = psa.get()
                    p.op("pe", lambda e, pst=pst, kl=kl, n=n: e.matmul(pst[:, 0:128], kl[:], gvtm[:, n, :], start=True, stop=True),
                         reads=[kl.b, gvtm.b], writes=[pst.b])
                    p.op("dve", lambda e, pst=pst, dcl=dcl, Sf=Sf: e.scalar_tensor_tensor(
                        Sf[:], Sf[:], dcl[:, 0:1], pst[:, 0:128], ALU.mult, ALU.add), reads=[pst.b, dcl.b], writes=[Sf.b])
                    p.op("act", lambda e, Sf=Sf, Sb_=Sb_: e.activation(Sb_[:], Sf[:], AF.Copy), reads=[Sf.b], writes=[Sb_.b])
                    if not is_start:
                        p.dma(ng_out[l, d_, seg, h, :, :], Sf[:], reads=[Sf.b], writes=[ng_ob], sem=Sf.b)
            p.op("act", lambda e: e.activation(gsq[:], og[:], AF.Square), reads=[og.b], writes=[gsq.b])
            for hf in range(2):
                sl = slice(hf * 512, (hf + 1) * 512)
                pq = ps.get()
                p.op("pe", lambda e, pq=pq, sl=sl: e.matmul(pq[:], ones_f[:], gsq[:, sl], start=True, stop=True),
                     reads=[gsq.b, ones_f.b], writes=[pq.b])
                p.op("dve", lambda e, pq=pq, sl=sl: e.tensor_scalar(grs[:, sl], pq[:], 1.0 / 128, EPS, ALU.mult, ALU.add),
                     reads=[pq.b], writes=[grs.b])
            p.op("act", lambda e: e.activation(grs[:], grs[:], AF.Sqrt), reads=[], writes=[grs.b])
            p.op("dve", lambda e: e.reciprocal(grs[:], grs[:]), reads=[], writes=[grs.b])
            p.op("dve", lambda e: e.scalar_tensor_tensor(og[:], og[:], glan[:, 0:1], grs[:], ALU.mult, ALU.mult),
                 reads=[grs.b, glan.b], writes=[og.b])
            pa, pb = proj_fm(l, C_GG + 128 * h, 128)
            for hf, pp in ((0, pa), (1, pb)):
                sl = slice(hf * 512, (hf + 1) * 512)
                p.op("act", lambda e, pp=pp, sl=sl: e.activation(gsg[:, sl], pp[:], AF.Silu), reads=[pp.b], writes=[gsg.b])
            p.op("dve", lambda e, h=h: e.tensor_tensor(oglaT[:, h, :], og[:], gsg[:], ALU.mult),
                 reads=[og.b, gsg.b], writes=[oglaT.b])
        for _ in modgen[l]:
            pass

    K.arena_begin()
    mergedT = K.sb("mergedT", [128, KC, T], BF16)
    sig = K.ring("sig", 2, [128, T])
    macc = K.ring("macc", 2, [128, T])
    mtmp = K.ring("mtmp", 2, [128, T])
    K.arena_end(limit=18432)

    def residual_update(l, c, pa, pb, gt_off):
        xc = xch.get()
        p.dma(xc[:], xs[c, :, :], reads=[xs_b[c]], writes=[xc.b], sem=xc.b)
        for hf, pp in ((0, pa), (1, pb)):
            sl = slice(hf * 512, (hf + 1) * 512)
            p.op("dve", lambda e, pp=pp, sl=sl, xc=xc: e.scalar_tensor_tensor(
                xc[:, sl], pp[:], modc[:, gt_off + c:gt_off + c + 1], xc[:, sl], ALU.mult, ALU.add),
                reads=[pp.b, modc.b], writes=[xc.b])
        p.dma(xs[c, :, :], xc[:], reads=[xc.b], writes=[xs_b[c]], sem=xc.b)

    def merge_out(l, branches):
        srcs = [(oattT, 8), (oglaT, 4), (orwT, 4)]
        for c in range(KC):
            acc = macc.get()
            first = True
            for br in branches:
                src, kc = srcs[br]
                wv, wb = wload(w_br[br][l, :, c * 128:(c + 1) * 128], kc, 128)
                pa, pb = psa.get(), psa.get()
                for k in range(kc):
                    for hf, pp in ((0, pa), (1, pb)):
                        p.op("pe", lambda e, wv=wv, k=k, hf=hf, pp=pp, src=src, kc=kc: e.matmul(
                            pp[:], wv[:, k, :], src[:, k, hf * 512:(hf + 1) * 512],
                            start=(k == 0), stop=(k == kc - 1)), reads=[wb, src.b], writes=[pp.b])
                ga, gb = proj_fm(l, C_GATE + br * D + c * 128, 128)
                sg_ = sig.get()
                for hf, gg in ((0, ga), (1, gb)):
                    sl = slice(hf * 512, (hf + 1) * 512)
                    p.op("act", lambda e, gg=gg, sl=sl, sg_=sg_: e.activation(sg_[:, sl], gg[:], AF.Sigmoid),
                         reads=[gg.b], writes=[sg_.b])
                for hf, pp in ((0, pa), (1, pb)):
                    sl = slice(hf * 512, (hf + 1) * 512)
                    if first:
                        p.op("dve", lambda e, pp=pp, sl=sl, sg_=sg_, acc=acc: e.tensor_tensor(
                            acc[:, sl], pp[:], sg_[:, sl], ALU.mult), reads=[pp.b, sg_.b], writes=[acc.b])
                    else:
                        mt = mtmp.get()
                        p.op("dve", lambda e, pp=pp, sl=sl, sg_=sg_, mt=mt: e.tensor_tensor(
                            mt[:, sl], pp[:], sg_[:, sl], ALU.mult), reads=[pp.b, sg_.b], writes=[mt.b])
                        p.op("dve", lambda e, sl=sl, mt=mt, acc=acc: e.tensor_tensor(
                            acc[:, sl], acc[:, sl], mt[:, sl], ALU.add), reads=[mt.b], writes=[acc.b])
                first = False
            p.op("act", lambda e, acc=acc, c=c: e.activation(mergedT[:, c, :], acc[:], AF.Copy),
                 reads=[acc.b], writes=[mergedT.b])
        for c in range(KC):
            wv, wb = wload(w_out[l, :, c * 128:(c + 1) * 128], KC, 128)
            pa, pb = psa.get(), psa.get()
            for k in range(KC):
                for hf, pp in ((0, pa), (1, pb)):
                    p.op("pe", lambda e, wv=wv, k=k, hf=hf, pp=pp: e.matmul(
                        pp[:], wv[:, k, :], mergedT[:, k, hf * 512:(hf + 1) * 512],
                        start=(k == 0), stop=(k == KC - 1)), reads=[wb, mergedT.b], writes=[pp.b])
            residual_update(l, c, pa, pb, 32)

    K.arena_begin()
    cwc = K.sb("cwc", [128, 3, 86])
    cbc = K.sb("cbc", [128, 86])
    cwn = K.sb("cwn", [128, 2, 86])
    upad = K.ring("upad", 2, [128, T + 2])
    ct1 = K.ring("ct1", 2, [128, T])
    csl = K.ring("csl", 2, [128, T])
    actT = K.sb("actT", [128, 22, T], BF16)
    K.arena_end(limit=18432)

    def conv_chunk(pa, pb, j):
        u_ = upad.get()
        for hf, pp in ((0, pa), (1, pb)):
            p.op("act", lambda e, pp=pp, hf=hf, u_=u_: e.activation(
                u_[:, 1 + hf * 512:1 + (hf + 1) * 512], pp[:], AF.Copy), reads=[pp.b], writes=[u_.b])
        t1 = ct1.get()
        p.op("dve", lambda e: e.tensor_scalar(t1[:], u_[:, 1:T + 1], cwc[:, 1, j:j + 1], cbc[:, j:j + 1],
                                              ALU.mult, ALU.add), reads=[u_.b, cwc.b, cbc.b], writes=[t1.b])
        p.op("dve", lambda e: e.scalar_tensor_tensor(t1[:], u_[:, 0:T], cwc[:, 0, j:j + 1], t1[:], ALU.mult, ALU.add),
             reads=[u_.b, cwc.b], writes=[t1.b])
        p.op("dve", lambda e: e.scalar_tensor_tensor(t1[:], u_[:, 2:T + 2], cwc[:, 2, j:j + 1], t1[:], ALU.mult, ALU.add),
             reads=[u_.b, cwc.b], writes=[t1.b])
        p.op("dve", lambda e: e.scalar_tensor_tensor(
            t1[:, 256:1024:256], u_[:, 256:1024:256], cwn[:, 0, j:j + 1], t1[:, 256:1024:256], ALU.mult, ALU.add),
            reads=[u_.b, cwn.b], writes=[t1.b])
        p.op("dve", lambda e: e.scalar_tensor_tensor(
            t1[:, 255:1023:256], u_[:, 257:1025:256], cwn[:, 1, j:j + 1], t1[:, 255:1023:256], ALU.mult, ALU.add),
            reads=[u_.b, cwn.b], writes=[t1.b])
        return t1

    def ffn(l):
        for u_ in upad.tl:
            p.op("pool", lambda e, u_=u_: e.memset(u_[:], 0.0), writes=[u_.b])
        for i in range(3):
            col_load(cwc[:, i, :], cwc.b, ffn_cw[l, i, :, :], 86)
        col_load(cbc[:], cbc.b, ffn_cb[l, :, :], 86)
        for i, wi in ((0, 0), (1, 2)):
            p.op("dve", lambda e, i=i, wi=wi: e.tensor_scalar(
                cwn[:, i, :], cwc[:, wi, :], segf[:, 2:3], None, ALU.mult), reads=[cwc.b, segf.b], writes=[cwn.b])
        norm(l, 1)
        for (j0, j1) in ((0, 22), (22, 43)):
            for j in range(j0, j1):
                pa, pb = proj_fm(l, 0, 128, wsrc=ffn_up[l, :, j * 128:(j + 1) * 128])
                tv = conv_chunk(pa, pb, j)
                pa, pb = proj_fm(l, 0, 128, wsrc=ffn_up[l, :, FFN_H + j * 128:FFN_H + (j + 1) * 128])
                tg = conv_chunk(pa, pb, 43 + j)
                sl_ = csl.get()
                p.op("act", lambda e, tg=tg, sl_=sl_: e.activation(sl_[:], tg[:], AF.Silu), reads=[tg.b], writes=[sl_.b])
                p.op("dve", lambda e, tv=tv, sl_=sl_, j=j, j0=j0: e.tensor_tensor(
                    actT[:, j - j0, :], tv[:], sl_[:], ALU.mult), reads=[tv.b, sl_.b], writes=[actT.b])
            nk = j1 - j0
            for c in range(KC):
                pa, pb = psa.get(), psa.get()
                k = 0
                while k < nk:
                    kk = min(16, nk - k)
                    wv, wb = wload(ffn_down[l, (j0 + k) * 128:(j0 + k + kk) * 128, c * 128:(c + 1) * 128], kk, 128)
                    for q in range(kk):
                        for hf, pp in ((0, pa), (1, pb)):
                            p.op("pe", lambda e, wv=wv, q=q, k=k, hf=hf, pp=pp, nk=nk: e.matmul(
                                pp[:], wv[:, q, :], actT[:, k + q, hf * 512:(hf + 1) * 512],
                                start=(k + q == 0), stop=(k + q == nk - 1)), reads=[wb, actT.b], writes=[pp.b])
                    k += kk
                residual_update(l, c, pa, pb, 80)

    for l in range(L):
        if stage <= 0:
            break
        compute_mod(l)
        if stage == 1:
            break
        norm(l, 0)
        if stage == 2:
            break
        p.barrier()
        rwkv_prep(l)
        p.barrier()
        if stage == 8:
            break
        rwkv_scan(l)
        p.barrier()
        rwkv_post(l)
        p.barrier()
        if stage == 9:
            break
        attention_kv(l)
        attention(l)
        p.barrier()
        gla(l)
        p.barrier()
        merge_out(l, [0, 1, 2])
        p.barrier()
        ffn(l)
        p.barrier()

    p.barrier()
    for tt in range(NT):
        yt = yo.get()
        for g in range(4):
            pt = ps.get()
            xg = xev.get()
            p.dma(xg[:, :, 0:128], xs[g * 4:(g + 1) * 4, :, tt * 128:(tt + 1) * 128].rearrange("c p t -> p c t"),
                  reads=xs_b[g * 4:(g + 1) * 4], writes=[xg.b], sem=xg.b)
            for j in range(4):
                p.op("pe", lambda e, pt=pt, j=j, xg=xg: e.transpose(
                    pt[:, j * 128:(j + 1) * 128], xg[:, j, 0:128], ident[:]),
                    reads=[xg.b, ident.b], writes=[pt.b])
            p.op("act", lambda e, pt=pt, g=g, yt=yt: e.activation(yt[:, g * 512:(g + 1) * 512], pt[:], AF.Copy),
                 reads=[pt.b], writes=[yt.b])
        p.dma(y_out[tt * 128:(tt + 1) * 128, :], yt[:], reads=[yt.b], writes=[y_ob], sem=yt.b)

    with K.st:
        n = p.finalize(K.st, final_bufs=K.outbufs)
        K.info = dict(ops=n, waits=p.n_waits, sems=p.n_sems)
    return K


def _rope_tables():
    rows = T // 64
    row = np.repeat(np.arange(rows, dtype=np.float32), 64)
    col = np.tile(np.arange(64, dtype=np.float32), rows)
    half = 64
    inv = (10000.0 ** (-np.arange(0, half, 2, dtype=np.float32) / half)).astype(np.float32)

    def cs(pos):
        ang = pos[:, None] * inv[None, :]
        ang = np.concatenate([ang, ang], axis=-1)
        return np.cos(ang), np.sin(ang)
    cr, sr = cs(row)
    cc, sc = cs(col)
    cosT = np.concatenate([cr, cc], axis=1).T.astype(np.float32)
    sinT = np.concatenate([sr, sc], axis=1).T.astype(np.float32)
    return np.ascontiguousarray(cosT), np.ascontiguousarray(sinT)


def _consts():
    ident = np.eye(128, dtype=np.float32)
    R = np.zeros((128, 128), np.float32)
    for base in (0, 64):
        for d in range(32):
            R[base + d, base + d + 32] = -1.0
            R[base + d + 32, base + d] = 1.0
    return ident, np.ascontiguousarray(R.T)


def make_in_maps(inputs):
    f = lambda a: np.ascontiguousarray(np.asarray(a, dtype=np.float32))
    ident, rotT = _consts()
    cosT, sinT = _rope_tables()
    ii = np.arange(128)
    tri = np.stack([(ii[:, None] <= ii[None, :]), (ii[:, None] >= ii[None, :]),
                    (ii[:, None] < ii[None, :]), (ii[:, None] > ii[None, :])], axis=1).astype(np.float32)
    aup = np.concatenate([f(inputs["gla_a_up"]), f(inputs["gla_a_bias"])[:, :, None, :]], axis=2)
    shared = {
        "w_mod": f(inputs["w_mod"]),
        "b_mod": f(inputs["b_mod"]).reshape(L, 96, 128),
        "norm_mix": f(inputs["norm_mix"]).reshape(L, KC, 128),
        "norm_ffn": f(inputs["norm_ffn"]).reshape(L, KC, 128),
        "w_in": f(inputs["w_in"]),
        "qk_norm": np.ascontiguousarray(np.stack([f(inputs["q_norm"]), f(inputs["k_norm"])], axis=1)),
        "ident": ident, "rotT": rotT, "tri": np.ascontiguousarray(tri), "gla_aup": np.ascontiguousarray(aup),
        "gla_norm": f(inputs["gla_norm"]).reshape(L, 1, 128),
        "w_br_att": f(inputs["w_br_att"]), "w_br_gla": f(inputs["w_br_gla"]), "w_br_rwkv": f(inputs["w_br_rwkv"]),
        "w_out": f(inputs["w_out"]), "ffn_up": f(inputs["ffn_up"]), "ffn_down": f(inputs["ffn_down"]),
        "ffn_cw": f(inputs["ffn_conv_w"]).reshape(L, 3, 86, 128), "ffn_cb": f(inputs["ffn_conv_b"]).reshape(L, 86, 128),
    }
    mu = f(inputs["rwkv_mu"])
    rw_mu = np.zeros((L, 16, 128), np.float32)
    offs = [(128 * i, 128) for i in range(12)] + [(1536, 64), (1600, 64), (1664, 64), (1728, 128)]
    for i, (o, n) in enumerate(offs):
        rw_mu[:, i, :n] = mu[:, o:o + n]
    shared["rw_mu"] = rw_mu
    sel2 = np.zeros((128, 2, 128), np.float32)
    for h2 in range(2):
        for v in range(64):
            sel2[h2 * 64 + v, h2, v] = 1.0
            sel2[h2 * 64 + v, h2, 64 + v] = 1.0
    shared["sel2"] = sel2
    shared["rw_wup"] = np.ascontiguousarray(np.concatenate([f(inputs["rwkv_w_up"]), f(inputs["rwkv_w0"])[:, :, None, :]], axis=2))
    shared["rw_aup"] = np.ascontiguousarray(np.concatenate([f(inputs["rwkv_a_up"]), f(inputs["rwkv_a0"])[:, None, :]], axis=1))
    shared["rw_gup"] = f(inputs["rwkv_g_up"])
    shared["rw_rows"] = np.ascontiguousarray(np.stack([f(inputs["rwkv_k_xi"]), f(inputs["rwkv_k_alpha"]), f(inputs["rwkv_bonus"]),
                                                       f(inputs["rwkv_ln_w"]), f(inputs["rwkv_ln_b"])], axis=1))
    maps = []
    for r in range(8):
        m = dict(shared)
        if r < 4:
            m["x"] = f(inputs["x_sample"][r])
            m["cvec"] = f(inputs["c"][r]).reshape(KC, 128)
            m["ck"] = f(inputs["cache_k"][r])
            m["cv"] = f(inputs["cache_v"][r])
            m["cosT"], m["sinT"] = cosT, sinT
            sg0 = np.zeros((L, 2, 4, 4, 128, 128), np.float32)
            sg0[:, 0, 0] = f(inputs["state_gla"][r])[:, 0]
            sg0[:, 1, 3] = f(inputs["state_gla"][r])[:, 1]
            m["sg0"] = sg0
            sr = f(inputs["state_rwkv"][r])
            sr0 = np.zeros((L, 4, 128, 512), np.float32)
            for d_ in range(2):
                sr0[:, 0, 64 * d_:64 * d_ + 64, :] = sr[:, d_].transpose(0, 2, 1, 3).reshape(L, 64, 512)
            m["sr0"] = sr0
            m["amask"] = np.zeros((128, 40), np.float32)
            m["segf"] = np.ascontiguousarray(np.broadcast_to(np.array([0, 1, 0, 0], np.float32), (128, 4)))
        else:
            q = r - 4
            m["x"] = f(inputs["x_prompt"][4 * q:4 * q + 4]).reshape(T, D)
            m["cvec"] = f(inputs["c_ctx"]).reshape(KC, 128)
            m["ck"] = np.zeros((L, 2, 256, 128), np.float32)
            m["cv"] = np.zeros((L, 2, 256, 128), np.float32)
            m["cosT"] = np.ones((128, T), np.float32)
            m["sg0"] = np.zeros((L, 2, 4, 4, 128, 128), np.float32)
            m["sr0"] = np.zeros((L, 4, 128, 512), np.float32)
            m["sinT"] = np.zeros((128, T), np.float32)
            m["segf"] = np.ascontiguousarray(np.broadcast_to(np.array([1, 0, -1, 0], np.float32), (128, 4)))
            am = np.full((4, 10), -30000.0, np.float32)
            for qt in range(4):
                am[qt, 2 + 2 * qt] = 0.0
                am[qt, 3 + 2 * qt] = 0.0
            m["amask"] = np.ascontiguousarray(np.broadcast_to(am.reshape(1, 40), (128, 40)))
        maps.append(m)
    return maps


def assemble(res):
    y_prompt = np.zeros((16, 256, D), np.float32)
    y_sample = np.zeros((4, T, D), np.float32)
    nck = np.zeros((16, L, 2, 256, 128), np.float32)
    ncv = np.zeros((16, L, 2, 256, 128), np.float32)
    nsg = np.zeros((16, L, 2, 4, 128, 128), np.float32)
    nsr = np.zeros((16, L, 2, 8, 64, 64), np.float32)
    for r in range(8):
        o = res[r]
        if r < 4:
            y_sample[r] = o["y"]
        else:
            q = r - 4
            y_prompt[4 * q:4 * q + 4] = o["y"].reshape(4, 256, D)
            nk = o["nk"].reshape(L, 2, 4, 256, 128)
            nv = o["nv"].reshape(L, 2, 4, 256, 128)
            nck[4 * q:4 * q + 4] = nk.transpose(2, 0, 1, 3, 4)
            ncv[4 * q:4 * q + 4] = nv.transpose(2, 0, 1, 3, 4)
            if "ng" in o:
                nsg[4 * q:4 * q + 4] = o["ng"].transpose(2, 0, 1, 3, 4, 5)
            if "nr" in o:
                nr = o["nr"].reshape(L, 2, 4, 64, 8, 64)
                nsr[4 * q:4 * q + 4] = nr.transpose(2, 0, 1, 4, 3, 5)
    return (y_prompt, y_sample, nck, ncv, nsg, nsr)


_CACHE = {}


def kernel(**inputs):
    if "K" not in _CACHE:
        _CACHE["K"] = build()
    K = _CACHE["K"]
    maps = make_in_maps(inputs)
    maps = [{k: v for k, v in m.items() if k in K.din} for m in maps]
    sub = _CACHE.get("sub")
    if sub is not None:
        res = run_bass_kernel_spmd(K.nc, [maps[r] for r in sub], core_ids=list(range(len(sub))))
        full = [res.results[sub.index(r)] if r in sub else {k: np.zeros(t.shape, np.float32) for k, t in K.dout.items()} for r in range(8)]
        return assemble(full)
    res = run_bass_kernel_spmd(K.nc, maps, core_ids=list(range(8)))
    return assemble(res.results)
```

```python
import numpy as np
from contextlib import ExitStack
import concourse.bass as bass
import concourse.mybir as mybir
from concourse.bass_utils import run_bass_kernel_spmd

F32 = mybir.dt.float32
BF16 = mybir.dt.bfloat16
ALU = mybir.AluOpType
AF = mybir.ActivationFunctionType
AX = mybir.AxisListType

D = 2048
KC = 16
T = 1024
NT = 8
L = 2
IN_COLS = 11616
FFN_H = 5504
C_AQ, C_AK, C_AV, C_GQ, C_GK, C_GV, C_GG, C_GAD = 0, 1024, 1280, 1536, 2048, 2560, 3072, 3584
C_RW = 3616
C_RR, C_RK, C_RV, C_RWD, C_RAD, C_RGD = 3616, 4128, 4640, 5152, 5280, 5344
C_GATE = 5472
EPS = 1e-6
SELF_SYNC = True
NS = True


class Buf:
    __slots__ = ("name", "last_w", "readers", "dcount")

    def __init__(self, name):
        self.name = name
        self.last_w = None
        self.readers = []
        self.dcount = 0


class Prog:
    ENGS = ("pe", "act", "dve", "pool", "sp")
    EPOCH = 8000

    def __init__(self, nc, self_sync=True):
        self.nc = nc
        self.ops = []
        self.self_sync = self_sync
        self.eng = {"pe": nc.tensor, "act": nc.scalar, "dve": nc.vector,
                    "pool": nc.gpsimd, "sp": nc.sync}

    wg = None
    hoist_depth = 2
    n_wg = 0

    def op(self, eng, emit, reads=(), writes=(), ns=False):
        self.ops.append(dict(eng=eng, emit=emit, reads=list(reads), writes=list(writes),
                             dma=None, bar=False, wg=self.wg, ns=ns))

    def dma(self, out, in_, reads=(), writes=(), sem=None, queue="sp", **kw):
        self.ops.append(dict(eng=queue, emit=lambda e: e.dma_start(out=out, in_=in_, **kw),
                             reads=list(reads), writes=list(writes), dma=sem, bar=False, wg=self.wg))

    def hoist_weight_groups(self):
        ops = self.ops
        groups = {}
        for i, o in enumerate(ops):
            if o.get("wg") is not None:
                groups.setdefault(o["wg"], []).append(i)
        order = sorted(groups)
        moved = set()
        before = {}

        def try_move(op_idx_list, t, first):
            if t >= first or ops[t].get("wg") is not None:
                return False
            if any(ops[j]["bar"] for j in range(t, first)):
                return False
            before.setdefault(t, []).extend(op_idx_list)
            moved.update(op_idx_list)
            return True

        for n_, g in enumerate(order):
            if n_ == 0:
                continue
            idx = groups[g]
            dmas = [i for i in idx if ops[i]["dma"] is not None]
            casts = [i for i in idx if ops[i]["dma"] is None]
            first = idx[0]
            a1 = order[n_ - 1]
            t1 = groups[a1][-1] + 1
            done = False
            if self.hoist_depth >= 2 and n_ >= 2:
                a2 = order[n_ - 2]
                t2 = groups[a2][-1] + 1
                if try_move(dmas, t2, first):
                    try_move(casts, t1, first)
                    done = True
            if not done:
                try_move(idx, t1, first)
        new_ops = []
        for i in range(len(ops)):
            if i in moved:
                continue
            for j in before.get(i, []):
                new_ops.append(ops[j])
            new_ops.append(ops[i])
        assert len(new_ops) == len(ops)
        self.ops = new_ops

    def barrier(self):
        for e in self.ENGS:
            self.ops.append(dict(eng=e, emit=None, reads=[], writes=[], dma=None, bar=True, wg=None))

    def finalize(self, stack, final_bufs=()):
        nc = self.nc
        self.hoist_weight_groups()
        ops = self.ops
        ops.append(dict(eng="sp", emit=None, reads=list(final_bufs), writes=[], dma=None, bar=False))
        n = len(ops)
        ev = [None] * n
        seq = {e: 0 for e in self.ENGS}
        deps = [None] * n
        waited = {e: {} for e in self.ENGS}
        needed = set()
        dma_bufs = {}
        for i, o in enumerate(ops):
            e = o["eng"]
            d = {}
            if o["bar"]:
                for x in self.ENGS:
                    if seq[x] > 0 and (x != e or (self.self_sync and e not in ("pe", "sp"))):
                        d[x] = seq[x]
                for name, sb in dma_bufs.items():
                    d["D:" + name] = sb.dcount
            else:
                cand = []
                for b in o["reads"]:
                    if b.last_w is not None:
                        cand.append(b.last_w)
                for b in o["writes"]:
                    if b.last_w is not None:
                        cand.append(b.last_w)
                    cand.extend(b.readers)
                for j in cand:
                    k, v = ev[j]
                    oj = ops[j]
                    if oj["dma"] is not None:
                        v = oj["dma"].dcount
                    elif oj["eng"] == e:
                        if e == "pe" or e == "sp" or not self.self_sync or o.get("ns"):
                            continue
                    if v > d.get(k, -1):
                        d[k] = v
            dl = []
            for k, v in d.items():
                if waited[e].get(k, -1) >= v:
                    continue
                waited[e][k] = v
                dl.append((k, v))
                if k in self.ENGS:
                    needed.add((k, v))
            deps[i] = dl
            if o["dma"] is not None:
                sb = o["dma"]
                sb.dcount += 1
                dma_bufs[sb.name] = sb
                ev[i] = ("D:" + sb.name, sb.dcount)
            elif o["emit"] is not None:
                seq[e] += 1
                ev[i] = (e, seq[e])
            else:
                ev[i] = (e, seq[e])
            for b in o["reads"]:
                b.readers.append(i)
            for b in o["writes"]:
                b.last_w = i
                b.readers = []
        rank = {}
        nep = {}
        for e in self.ENGS:
            vs = sorted(v for (k, v) in needed if k == e)
            for r, v in enumerate(vs):
                rank[(e, v)] = (r // self.EPOCH, r % self.EPOCH + 1)
            nep[e] = (len(vs) - 1) // self.EPOCH + 1 if vs else 1
        sems = {}
        for e in self.ENGS:
            for ep in range(nep[e]):
                sems[(e, ep)] = stack.enter_context(nc.semaphore("s_%s%d" % (e, ep)))
        for name in dma_bufs:
            sems["D:" + name] = stack.enter_context(nc.semaphore("d_" + name))
        self.n_waits = 0
        self.n_sems = len(sems)
        for i, o in enumerate(ops):
            e = o["eng"]
            eng = self.eng[e]
            for (k, v) in deps[i]:
                if k in self.ENGS:
                    ep, val = rank[(k, v)]
                    eng.wait_ge(sems[(k, ep)], val)
                else:
                    eng.wait_ge(sems[k], 16 * v)
                self.n_waits += 1
            if o["emit"] is None:
                continue
            ins = o["emit"](eng)
            k, v = ev[i]
            if o["dma"] is not None:
                ins.then_inc(sems[k], 16)
            elif (k, v) in needed:
                ep, val = rank[(k, v)]
                ins.then_inc(sems[(k, ep)], 1)
        return len(ops)


class Tl:
    def __init__(self, t, b):
        self.t = t
        self.b = b

    def __getitem__(self, k):
        return self.t[k]


class Kern:
    def __init__(self, stage=99):
        self.stage = stage
        self.nc = bass.Bass("TRN2", target_bir_lowering=False)
        self.st = ExitStack()
        self.p = Prog(self.nc, self_sync=SELF_SYNC)
        self.din = {}
        self.dout = {}
        self.outbufs = []
        self._n = 0

    def inp(self, name, shape):
        t = self.nc.dram_tensor(name, list(shape), F32, kind="ExternalInput")
        self.din[name] = t
        return t.ap()

    def outp(self, name, shape):
        t = self.nc.dram_tensor(name, list(shape), F32, kind="ExternalOutput")
        self.dout[name] = t
        b = Buf("o_" + name)
        self.outbufs.append(b)
        return t.ap(), b

    ARENA_F32 = (24576, 6144, 9216)

    def sb(self, name, shape, dt=F32):
        if getattr(self, "in_arena", None) is not None:
            w = self.in_arena
            n = int(np.prod(shape[1:]))
            nf = n if dt == F32 else (n + 1) // 2
            nf = (nf + 7) // 8 * 8
            assert self.aoff[w] + nf <= self.ARENA_F32[w], (name, w, self.aoff[w], nf)
            v = self.arenas[w][0:shape[0], self.aoff[w]:self.aoff[w] + nf]
            self.aoff[w] += nf
            if dt != F32:
                v = v.bitcast(dt)
            v = v[:, 0:n]
            if len(shape) == 3:
                v = v.rearrange("p (a b) -> p a b", b=shape[2])
            return Tl(v, Buf(name))
        t = self.st.enter_context(self.nc.sbuf_tensor("sb_" + name, list(shape), dt))
        return Tl(t, Buf(name))

    def arena_begin(self, w=0, keep=False):
        if not hasattr(self, "arenas"):
            self.arenas = [self.st.enter_context(self.nc.sbuf_tensor("sb_arena%d" % i, [128, self.ARENA_F32[i]], F32))
                           for i in range(3)]
            self.aoff = [0, 0, 0]
        self.in_arena = w
        if not keep:
            self.aoff[w] = 0

    def arena_end(self, limit=None):
        if limit is not None:
            assert self.aoff[self.in_arena] <= limit, (self.in_arena, self.aoff[self.in_arena])
        self.in_arena = None

    def ring(self, name, n, shape, dt=F32):
        tl = [self.sb("%s%d" % (name, i), shape, dt) for i in range(n)]
        return Ring(tl)

    def setup_psum(self):
        tl = [Tl(self.st.enter_context(self.nc.psum_tensor("ps%d" % i, [128, 512], F32)),
                 Buf("ps%d" % i)) for i in range(8)]
        self.ps = Ring(tl[0:3])
        self.pmod = tl[3]
        self.psa = Ring(tl[4:8])
        self.ps8 = Ring(tl[0:8])


class Ring:
    def __init__(self, tl):
        self.tl = tl
        self.i = 0

    def get(self):
        t = self.tl[self.i % len(self.tl)]
        self.i += 1
        return t


def build(stage=99):
    K = Kern(stage)
    nc, p = K.nc, K.p
    x_in = K.inp("x", [T, D])
    cvec = K.inp("cvec", [KC, 128])
    w_mod = K.inp("w_mod", [L, D, 6 * D])
    b_mod = K.inp("b_mod", [L, 96, 128])
    norm_mix = K.inp("norm_mix", [L, KC, 128])
    norm_ffn = K.inp("norm_ffn", [L, KC, 128])
    w_in = K.inp("w_in", [L, D, IN_COLS])
    qk_norm = K.inp("qk_norm", [L, 2, 128])
    ck_in = K.inp("ck", [L, 2, 256, 128])
    cv_in = K.inp("cv", [L, 2, 256, 128])
    ident_in = K.inp("ident", [128, 128])
    rot_in = K.inp("rotT", [128, 128])
    cos_in = K.inp("cosT", [128, T])
    sin_in = K.inp("sinT", [128, T])
    amask_in = K.inp("amask", [128, 40])
    segf_in = K.inp("segf", [128, 4])
    tri_in = K.inp("tri", [128, 4, 128])
    sg0_in = K.inp("sg0", [L, 2, 4, 4, 128, 128])
    gla_aup = K.inp("gla_aup", [L, 2, 17, 512])
    gla_norm = K.inp("gla_norm", [L, 1, 128])
    ng_out, ng_ob = K.outp("ng", [L, 2, 4, 4, 128, 128])
    rw_mu = K.inp("rw_mu", [L, 16, 128])
    sel2_in = K.inp("sel2", [128, 2, 128])
    rw_wup = K.inp("rw_wup", [L, 2, 65, 512])
    rw_aup = K.inp("rw_aup", [L, 65, 512])
    rw_gup = K.inp("rw_gup", [L, 128, 512])
    rw_rows = K.inp("rw_rows", [L, 5, 512])
    sr0_in = K.inp("sr0", [L, 4, 128, 512])
    nr_out, nr_ob = K.outp("nr", [L, 2, 4, 64, 512])
    w_br = [K.inp("w_br_att", [L, 1024, D]), K.inp("w_br_gla", [L, 512, D]), K.inp("w_br_rwkv", [L, 512, D])]
    w_out = K.inp("w_out", [L, D, D])
    ffn_up = K.inp("ffn_up", [L, D, 2 * FFN_H])
    ffn_cw = K.inp("ffn_cw", [L, 3, 86, 128])
    ffn_cb = K.inp("ffn_cb", [L, 86, 128])
    ffn_down = K.inp("ffn_down", [L, FFN_H, D])
    y_out, y_ob = K.outp("y", [T, D])
    nk_out, nk_ob = K.outp("nk", [L, 2, T, 128])
    nv_out, nv_ob = K.outp("nv", [L, 2, T, 128])
    xs = nc.dram_tensor("xs", [KC, 128, T], F32).ap()
    xs_b = [Buf("xs%d" % c) for c in range(KC)]

    K.setup_psum()
    ps = K.ps
    psa = K.psa
    ident = K.sb("ident", [128, 128])
    ones_f = K.sb("ones_f", [128, 128])
    ones_b = K.sb("ones_b", [128, 128], BF16)
    rotT = K.sb("rotT", [128, 128])
    amask = K.sb("amask", [128, 40])
    p.dma(ident[:], ident_in[:, :], writes=[ident.b], sem=ident.b)
    p.dma(rotT[:], rot_in[:, :], writes=[rotT.b], sem=rotT.b)
    p.dma(amask[:], amask_in[:, :], writes=[amask.b], sem=amask.b)
    tri = K.sb("tri", [128, 4, 128])
    p.dma(tri[:], tri_in[:, :, :], writes=[tri.b], sem=tri.b)
    segf = K.sb("segf", [128, 4])
    p.dma(segf[:], segf_in[:, :], writes=[segf.b], sem=segf.b)
    p.op("pool", lambda e: e.memset(ones_f[:], 1.0), writes=[ones_f.b])
    p.op("pool", lambda e: e.memset(ones_b[:], 1.0), writes=[ones_b.b])
    sel2 = K.sb("sel2", [128, 2, 128])
    p.dma(sel2[:], sel2_in[:, :, :], writes=[sel2.b], sem=sel2.b)
    ident_b = K.sb("ident_b", [128, 128], BF16)
    p.op("act", lambda e: e.activation(ident_b[:], ident[:], AF.Copy), reads=[ident.b], writes=[ident_b.b])

    NW = 3
    WQ2 = False
    K.arena_begin(2)
    wst = K.ring("wst", NW, [128, 2048], F32)
    wbf = K.ring("wbf", NW, [128, 2048], BF16)
    K.arena_end()

    def wload(src, kc, ncols, cast=True, rows=128):
        s = wst.get()
        n = kc * ncols
        p.n_wg += 1
        p.wg = p.n_wg
        sv = s[0:rows, 0:n].rearrange("p (k n) -> p k n", n=ncols)
        srcv = src.rearrange("(k p) n -> p k n", p=rows)
        nsplit = 1
        kq = kc // nsplit
        for q_ in range(nsplit):
            p.dma(sv[:, q_ * kq:(q_ + 1) * kq, :], srcv[:, q_ * kq:(q_ + 1) * kq, :], writes=[s.b], sem=s.b,
                  queue=("sp", "act")[q_ % 2] if WQ2 else "sp")
        if not cast:
            p.wg = None
            return sv, s.b
        w = wbf.get()
        wv = w[0:rows, 0:n].rearrange("p (k n) -> p k n", n=ncols)
        p.op("act", lambda e: e.activation(w[0:rows, 0:n], s[0:rows, 0:n], AF.Copy), reads=[s.b], writes=[w.b])
        p.wg = None
        return wv, w.b

    clr = K.ring("clr", 2, [128, 128])
    def col_load(dst, dstb, src_rows, nrows):
        tmp = clr.get()
        p.dma(tmp[0:nrows, :], src_rows, writes=[tmp.b], sem=tmp.b)
        pt = ps.get()
        p.op("pe", lambda e: e.transpose(pt[:, 0:nrows], tmp[0:nrows, :], ident[0:nrows, 0:nrows]),
             reads=[tmp.b, ident.b], writes=[pt.b])
        p.op("dve", lambda e: e.tensor_copy(dst, pt[:, 0:nrows]), reads=[pt.b], writes=[dstb])

    K.arena_begin()
    xtm = K.ring("xtm", 2, [128, D])
    xev = K.ring("xev", 2, [128, 4, 128])
    yo = K.ring("yo", 2, [128, D])
    K.arena_end(limit=18432)
    for tt in range(NT):
        xt_ = xtm.get()
        p.dma(xt_[:], x_in[tt * 128:(tt + 1) * 128, :], writes=[xt_.b], sem=xt_.b)
        for g in range(4):
            pt = ps.get()
            for j in range(4):
                c = g * 4 + j
                p.op("pe", lambda e, pt=pt, j=j, c=c, xt_=xt_: e.transpose(
                    pt[:, j * 128:(j + 1) * 128], xt_[:, c * 128:(c + 1) * 128], ident[:]),
                    reads=[xt_.b, ident.b], writes=[pt.b])
            ev = xev.get()
            evv = ev[:, :, 0:128]
            p.op("act", lambda e, pt=pt, evv=evv: e.activation(
                evv, pt[:].rearrange("p (j t) -> p j t", t=128), AF.Copy), reads=[pt.b], writes=[ev.b])
            p.dma(xs[g * 4:(g + 1) * 4, :, tt * 128:(tt + 1) * 128].rearrange("c p t -> p c t"), evv,
                  reads=[ev.b], writes=xs_b[g * 4:(g + 1) * 4], sem=ev.b)

    p.barrier()
    modc = K.sb("modc", [128, 96])
    bmodc = K.sb("bmodc", [128, 96])
    silc = K.sb("silc", [128, KC])
    gcol = K.sb("gcol", [128, 2, KC])
    Gc = K.sb("Gc", [128, 2, KC])
    qkn = K.sb("qkn", [128, 2])
    col_load(silc[:], silc.b, cvec[:, :], KC)
    p.op("act", lambda e: e.activation(silc[:], silc[:], AF.Silu), reads=[], writes=[silc.b])

    K.arena_begin(0)
    K.aoff[0] = 18432
    xch = K.ring("xch", 2, [128, T])
    sqr = K.ring("sqr", 1, [128, T])
    K.arena_end()
    rstd = K.sb("rstd", [128, T])
    hT = K.sb("hT", [128, KC, T], BF16)
    nrm_tmp = Ring([Tl(sqr.tl[0].t, Buf("nrmt0"))])

    mred = K.sb("mred", [128, 4])

    def mod_superblock(l, G):
        pm = K.pmod
        for q_ in range(4):
            wv, wb = wload(w_mod[l, q_ * 512:(q_ + 1) * 512, G * 512:(G + 1) * 512], 4, 512, cast=False)
            for g in range(4):
                for kc in range(4):
                    col = g * 4 + q_
                    p.op("pe", lambda e, wv=wv, g=g, kc=kc, q_=q_, col=col: e.matmul(
                        pm[:, col:col + 1], wv[:, kc, g * 128:(g + 1) * 128], silc[:, 4 * q_ + kc:4 * q_ + kc + 1],
                        start=(kc == 0), stop=(kc == 3)), reads=[wb, silc.b], writes=[pm.b])
            yield
        p.op("dve", lambda e: e.tensor_reduce(mred[:], pm[:, 0:16].rearrange("p (g q) -> p g q", q=4), AX.X, ALU.add),
             reads=[pm.b], writes=[mred.b])
        p.op("dve", lambda e, G=G: e.tensor_tensor(modc[:, 4 * G:4 * G + 4], mred[:], bmodc[:, 4 * G:4 * G + 4], ALU.add),
             reads=[mred.b, bmodc.b], writes=[modc.b])

    def mod_gc(i, off):
        p.op("dve", lambda e, i=i, off=off: e.scalar_tensor_tensor(
            Gc[:, i, :], modc[:, off:off + 16], 1.0, gcol[:, i, :], ALU.add, ALU.mult),
            reads=[modc.b, gcol.b], writes=[Gc.b])

    def compute_mod(l):
        col_load(bmodc[:], bmodc.b, b_mod[l, :, :], 96)
        col_load(gcol[:, 0, :], gcol.b, norm_mix[l, :, :], KC)
        col_load(gcol[:, 1, :], gcol.b, norm_ffn[l, :, :], KC)
        col_load(qkn[:], qkn.b, qk_norm[l, :, :], 2)
        for G in range(8):
            for _ in mod_superblock(l, G):
                pass
        mod_gc(0, 16)

    def mod_deferred(l):
        for G in range(8, 24):
            for _ in mod_superblock(l, G):
                yield
        mod_gc(1, 64)

    def norm(l, which):
        sh_off = 0 if which == 0 else 48
        pa, pb = psa.get(), psa.get()
        for c in range(KC):
            xc = xch.get()
            p.dma(xc[:], xs[c, :, :], reads=[xs_b[c]], writes=[xc.b], sem=xc.b)
            sq = sqr.get()
            p.op("act", lambda e, sq=sq, xc=xc: e.activation(sq[:], xc[:], AF.Square),
                 reads=[xc.b], writes=[sq.b])
            for hf, pp in ((0, pa), (1, pb)):
                p.op("pe", lambda e, sq=sq, hf=hf, pp=pp, c=c: e.matmul(
                    pp[:], ones_f[:], sq[:, hf * 512:(hf + 1) * 512], start=(c == 0), stop=(c == KC - 1)),
                    reads=[sq.b, ones_f.b], writes=[pp.b])
        for hf, pp in ((0, pa), (1, pb)):
            sl = slice(hf * 512, (hf + 1) * 512)
            p.op("dve", lambda e, pp=pp, sl=sl: e.tensor_scalar(
                rstd[:, sl], pp[:], 1.0 / D, EPS, ALU.mult, ALU.add), reads=[pp.b], writes=[rstd.b])
        p.op("act", lambda e: e.activation(rstd[:], rstd[:], AF.Sqrt), reads=[], writes=[rstd.b])
        p.op("dve", lambda e: e.reciprocal(rstd[:], rstd[:]), reads=[], writes=[rstd.b])
        for c in range(KC):
            xc = xch.get()
            p.dma(xc[:], xs[c, :, :], reads=[xs_b[c]], writes=[xc.b], sem=xc.b)
            tm = nrm_tmp.get()
            p.op("dve", lambda e, tm=tm, xc=xc: e.tensor_tensor(tm[:], xc[:], rstd[:], ALU.mult),
                 reads=[xc.b, rstd.b], writes=[tm.b])
            p.op("dve", lambda e, tm=tm, c=c: e.tensor_scalar(
                hT[:, c, :], tm[:], Gc[:, which, c:c + 1], modc[:, sh_off + c:sh_off + c + 1],
                ALU.mult, ALU.add), reads=[tm.b, Gc.b, modc.b], writes=[hT.b])

    def proj_fm(l, col0, ncols, wsrc=None):
        src = (w_in[l, :, col0:col0 + ncols] if wsrc is None else wsrc)
        wv, wb = wload(src, KC, ncols)
        pa, pb = psa.get(), psa.get()
        for k in range(KC):
            for hf, pp in ((0, pa), (1, pb)):
                p.op("pe", lambda e, wv=wv, k=k, hf=hf, pp=pp: e.matmul(
                    pp[0:ncols, :], wv[:, k, :], hT[:, k, hf * 512:(hf + 1) * 512],
                    start=(k == 0), stop=(k == KC - 1)), reads=[wb, hT.b], writes=[pp.b])
        return pa, pb

    def proj_tm(l, col0, ncols, consume):
        wv, wb = wload(w_in[l, :, col0:col0 + ncols], KC, ncols)
        for tt in range(NT):
            pt = psa.get()
            for k in range(KC):
                p.op("pe", lambda e, wv=wv, k=k, tt=tt, pt=pt: e.matmul(
                    pt[:, 0:ncols], hT[:, k, tt * 128:(tt + 1) * 128], wv[:, k, :],
                    start=(k == 0), stop=(k == KC - 1)), reads=[wb, hT.b], writes=[pt.b])
            consume(tt, pt)

    K.arena_begin(1)
    oattT = K.sb("oattT", [128, 8, T], BF16)
    oglaT = K.sb("oglaT", [128, 4, T], BF16)
    K.arena_end()
    orwT = K.sb("orwT", [128, 4, T], BF16)

    RW_PIECES = [(C_RR + 128 * i, 128) for i in range(4)] + [(C_RK + 128 * i, 128) for i in range(4)] + \
                [(C_RV + 128 * i, 128) for i in range(4)] + [(C_RWD, 64), (C_RWD + 64, 64), (C_RAD, 64), (C_RGD, 128)]
    K.arena_begin(0)
    kap_tm = K.sb("kap_tm", [128, NT, 512], BF16)
    b_tm = K.sb("b_tm", [128, NT, 512], BF16)
    r_tm = K.sb("r_tm", [128, NT, 512], BF16)
    wk_mark = K.aoff[0]
    k_tm = K.sb("k_tm", [128, NT, 512], BF16)
    w_tm = [K.sb("w_tm%d" % d_, [128, NT, 512], BF16) for d_ in range(2)]
    rw_mark = K.aoff[0]
    K.aoff[0] = wk_mark
    wkr = [Tl(K.sb("wkr%d" % i, [128, 4 * 2 * 512], BF16).t, Buf("wkr%d" % i)) for i in range(3)]
    assert K.aoff[0] <= rw_mark
    K.aoff[0] = rw_mark
    prow = K.sb("prow", [128, 3, 512])
    muc = K.sb("muc", [128, 16])
    omc = K.sb("omc", [128, 16])
    hmc = K.sb("hmc", [128, 16])
    aupa = K.sb("aupa", [65, 512])
    TFB = [K.sb("TFB%d" % d_, [65, T]) for d_ in range(2)]
    RA = K.sb("RA", [65, T])
    mpad = K.ring("mpad", 1, [128, T + 2])
    mt_ = K.ring("mt_", 1, [128, T])
    RRc = K.sb("RRc", [128, T])
    RKc = K.sb("RKc", [128, T])
    RVc = K.sb("RVc", [128, T])
    rkv32 = K.ring("rkv32", 2, [128, 384])
    a32 = K.ring("a32", 1, [128, 128])
    wsg = K.ring("wsg", 1, [128, 128])
    kx32 = K.ring("kx32", 1, [128, 128])
    ksq = K.ring("ksq", 1, [128, 128])
    ss2 = K.ring("ss2", 2, [128, 2])
    kap32 = K.ring("kap32", 1, [128, 128])
    kr32 = K.ring("kr32", 1, [128, 128])
    bu32 = K.ring("bu32", 1, [128, 128])
    bs2 = K.ring("bs2", 2, [128, 2])
    ybon = K.ring("ybon", 2, [128, 128])
    K.arena_end()
    K.arena_begin(1)
    VV = K.sb("VV", [128, 8, T], BF16)
    SG = K.sb("SG", [128, T])
    wupa = K.sb("wupa", [65, 2, 512])
    K.arena_end()
    K.arena_begin(0, keep=True)
    K.aoff[0] = rw_mark
    Ysum = K.sb("Ysum", [128, NT, 512])
    gup = K.sb("gup", [128, 512])
    prow2 = K.sb("prow2", [128, 2, 512])
    ln_a = K.ring("ln_a", 2, [128, 512])
    ln_b_ = K.ring("ln_b", 2, [128, 512])
    ln_s = K.ring("ln_s", 2, [128, 8])
    ln_r = K.ring("ln_r", 2, [128, 8])
    ybuf = K.sb("ybuf", [128, 128, 8])
    yrev = K.sb("yrev", [128, 128, 8])
    K.arena_end()
    K.arena_begin(2)
    Sst = K.sb("Sst", [128, 512])
    S0t = K.ring("S0t", 1, [128, 512])
    stmp = K.ring("stmp", 3, [128, 512])
    Swr = K.ring("Swr", 2, [128, 512])
    t3r = K.ring("t3r", 2, [128, 512])
    Spr = K.ring("Spr", 2, [128, 512])
    skr = K.ring("skr", 2, [128, 8])
    K.arena_end()
    ysd = nc.dram_tensor("ysd", [NT, 128, 512], F32).ap()
    ysd_b = Buf("ysd")
    wkd = nc.dram_tensor("wkd", [3, T, 512], BF16).ap()
    wkd_b = Buf("wkd")

    def rw_mix(pa, pb, piece, ncols, out_ap, out_b, func=None):
        u_ = mpad.get()
        for hf, pp in ((0, pa), (1, pb)):
            p.op("act", lambda e, pp=pp, hf=hf: e.activation(
                u_[0:ncols, 1 + hf * 512:1 + (hf + 1) * 512], pp[0:ncols, :], AF.Copy), reads=[pp.b], writes=[u_.b])
        t = mt_.get()
        p.op("dve", lambda e: e.tensor_tensor(t[0:ncols, :], u_[0:ncols, 0:T], u_[0:ncols, 2:T + 2], ALU.add),
             reads=[u_.b], writes=[t.b])
        p.op("dve", lambda e: e.scalar_tensor_tensor(
            t[0:ncols, 256:1024:256], u_[0:ncols, 256:1024:256], segf[0:ncols, 2:3], t[0:ncols, 256:1024:256],
            ALU.mult, ALU.add), reads=[u_.b, segf.b], writes=[t.b])
        p.op("dve", lambda e: e.scalar_tensor_tensor(
            t[0:ncols, 255:1023:256], u_[0:ncols, 257:1025:256], segf[0:ncols, 2:3], t[0:ncols, 255:1023:256],
            ALU.mult, ALU.add), reads=[u_.b, segf.b], writes=[t.b])
        p.op("dve", lambda e: e.tensor_scalar(t[0:ncols, :], t[0:ncols, :], hmc[0:ncols, piece:piece + 1], None, ALU.mult),
             reads=[hmc.b], writes=[t.b])
        if func is None:
            p.op("dve", lambda e: e.scalar_tensor_tensor(
                out_ap, u_[0:ncols, 1:T + 1], omc[0:ncols, piece:piece + 1], t[0:ncols, :], ALU.mult, ALU.add),
                reads=[u_.b, omc.b, t.b], writes=[out_b])
        else:
            p.op("dve", lambda e: e.scalar_tensor_tensor(
                t[0:ncols, :], u_[0:ncols, 1:T + 1], omc[0:ncols, piece:piece + 1], t[0:ncols, :], ALU.mult, ALU.add),
                reads=[u_.b, omc.b], writes=[t.b])
            p.op("act", lambda e: e.activation(out_ap, t[0:ncols, :], func), reads=[t.b], writes=[out_b])

    def rwkv_prep(l):
        for u_ in mpad.tl:
            p.op("pool", lambda e, u_=u_: e.memset(u_[:], 0.0), writes=[u_.b])
        col_load(muc[:], muc.b, rw_mu[l, :, :], 16)
        p.op("dve", lambda e: e.tensor_scalar(omc[:], muc[:], -1.0, 1.0, ALU.mult, ALU.add), reads=[muc.b], writes=[omc.b])
        p.op("dve", lambda e: e.tensor_scalar(hmc[:], muc[:], 0.5, None, ALU.mult), reads=[muc.b], writes=[hmc.b])
        p.dma(wupa[:], rw_wup[l].rearrange("d r c -> r d c"), writes=[wupa.b], sem=wupa.b)
        p.dma(aupa[:], rw_aup[l, :, :], writes=[aupa.b], sem=aupa.b)
        p.dma(prow[:], rw_rows[l:l + 1, 0:3, :].to_broadcast([128, 3, 512]), writes=[prow.b], sem=prow.b)
        for d_ in range(2):
            p.op("pool", lambda e, d_=d_: e.memset(TFB[d_][:], 1.0), writes=[TFB[d_].b])
            pa, pb = proj_fm(l, RW_PIECES[12 + d_][0], 64)
            rw_mix(pa, pb, 12 + d_, 64, TFB[d_][0:64, :], TFB[d_].b, func=AF.Tanh)
        p.op("pool", lambda e: e.memset(RA[:], 1.0), writes=[RA.b])
        pa, pb = proj_fm(l, C_RAD, 64)
        rw_mix(pa, pb, 14, 64, RA[0:64, :], RA.b)
        pa, pb = proj_fm(l, C_RGD, 128)
        rw_mix(pa, pb, 15, 128, SG[:], SG.b, func=AF.Sigmoid)
        for c in range(4):
            cs = slice(c * 128, (c + 1) * 128)
            for (col0, dst, piece) in ((C_RR + 128 * c, RRc, c), (C_RK + 128 * c, RKc, 4 + c), (C_RV + 128 * c, RVc, 8 + c)):
                pa, pb = proj_fm(l, col0, 128)
                rw_mix(pa, pb, piece, 128, dst[:], dst.b)
            for h2 in range(2):
                sel = sel2[:, h2, :]
                for hf in range(2):
                    pv = ps.get()
                    p.op("pe", lambda e, pv=pv, sel=sel, hf=hf: e.matmul(
                        pv[:], sel, RVc[:, hf * 512:(hf + 1) * 512], start=True, stop=True),
                        reads=[RVc.b, sel2.b], writes=[pv.b])
                    p.op("act", lambda e, pv=pv, hf=hf, h=2 * c + h2: e.activation(
                        VV[0:64, h, hf * 512:(hf + 1) * 512], pv[0:64, :], AF.Copy), reads=[pv.b], writes=[VV.b])
                    p.op("act", lambda e, pv=pv, hf=hf, h=2 * c + h2: e.activation(
                        VV[64:128, h, T - (hf + 1) * 512:T - hf * 512], pv[64:128, ::-1], AF.Copy),
                        reads=[pv.b], writes=[VV.b])
            for tt in range(NT):
                ts_ = slice(tt * 128, (tt + 1) * 128)
                pt = ps.get()
                for i, src in enumerate((RRc, RKc, RVc)):
                    p.op("pe", lambda e, pt=pt, i=i, src=src, ts_=ts_: e.transpose(
                        pt[:, i * 128:(i + 1) * 128], src[:, ts_], ident[:]), reads=[src.b, ident.b], writes=[pt.b])
                x3 = rkv32.get()
                p.op("act", lambda e, pt=pt, x3=x3: e.activation(x3[:], pt[:, 0:384], AF.Copy), reads=[pt.b], writes=[x3.b])
                r32, rk32, v32 = x3[:, 0:128], x3[:, 128:256], x3[:, 256:384]
                pz = ps.get()
                p.op("pe", lambda e, pz=pz, ts_=ts_, cs=cs: e.matmul(pz[:, 0:128], RA[:, ts_], aupa[:, cs], start=True, stop=True),
                     reads=[RA.b, aupa.b], writes=[pz.b])
                a_ = a32.get()
                p.op("act", lambda e, pz=pz, a_=a_: e.activation(a_[:], pz[:, 0:128], AF.Sigmoid), reads=[pz.b], writes=[a_.b])
                for d_ in range(2):
                    pw = ps.get()
                    p.op("pe", lambda e, pw=pw, ts_=ts_, cs=cs, d_=d_: e.matmul(
                        pw[:, 0:128], TFB[d_][:, ts_], wupa[:, d_, cs], start=True, stop=True),
                        reads=[TFB[d_].b, wupa.b], writes=[pw.b])
                    ws_ = wsg.get()
                    p.op("act", lambda e, pw=pw, ws_=ws_: e.activation(ws_[:], pw[:, 0:128], AF.Sigmoid), reads=[pw.b], writes=[ws_.b])
                    p.op("act", lambda e, ws_=ws_, d_=d_, tt=tt, cs=cs: e.activation(
                        w_tm[d_][:, tt, cs], ws_[:], AF.Exp, scale=-0.606531), reads=[ws_.b], writes=[w_tm[d_].b])
                kx = kx32.get()
                p.op("dve", lambda e, kx=kx, rk32=rk32, cs=cs: e.tensor_tensor(kx[:], rk32, prow[:, 0, cs], ALU.mult),
                     reads=[x3.b, prow.b], writes=[kx.b])
                kq = ksq.get()
                p.op("dve", lambda e, kx=kx, kq=kq: e.tensor_tensor(kq[:], kx[:], kx[:], ALU.mult), reads=[kx.b], writes=[kq.b])
                s2 = ss2.get()
                p.op("dve", lambda e, kq=kq, s2=s2: e.tensor_reduce(
                    s2[:], kq[:].rearrange("p (h k) -> p h k", k=64), AX.X, ALU.add), reads=[kq.b], writes=[s2.b])
                p.op("dve", lambda e, s2=s2: e.tensor_scalar(s2[:], s2[:], EPS, None, ALU.add), reads=[], writes=[s2.b])
                p.op("act", lambda e, s2=s2: e.activation(s2[:], s2[:], AF.Sqrt), reads=[], writes=[s2.b])
                p.op("dve", lambda e, s2=s2: e.reciprocal(s2[:], s2[:]), reads=[], writes=[s2.b])
                kp = kap32.get()
                p.op("dve", lambda e, kp=kp, kx=kx, s2=s2: e.tensor_tensor(
                    kp[:].rearrange("p (h k) -> p h k", k=64), kx[:].rearrange("p (h k) -> p h k", k=64),
                    s2[:].unsqueeze(2).to_broadcast([128, 2, 64]), ALU.mult), reads=[kx.b, s2.b], writes=[kp.b])
                p.op("act", lambda e, kp=kp, tt=tt, cs=cs: e.activation(kap_tm[:, tt, cs], kp[:], AF.Copy),
                     reads=[kp.b], writes=[kap_tm.b])
                p.op("dve", lambda e, kp=kp, a_=a_, tt=tt, cs=cs: e.tensor_tensor(b_tm[:, tt, cs], kp[:], a_[:], ALU.mult),
                     reads=[kp.b, a_.b], writes=[b_tm.b])
                kr = kr32.get()
                p.op("dve", lambda e, kr=kr, a_=a_, cs=cs: e.scalar_tensor_tensor(
                    kr[:], a_[:], -1.0, prow[:, 1, cs], ALU.add, ALU.mult), reads=[a_.b, prow.b], writes=[kr.b])
                p.op("dve", lambda e, kr=kr, rk32=rk32: e.scalar_tensor_tensor(
                    kr[:], kr[:], 1.0, rk32, ALU.add, ALU.mult), reads=[x3.b], writes=[kr.b])
                p.op("act", lambda e, kr=kr, tt=tt, cs=cs: e.activation(k_tm[:, tt, cs], kr[:], AF.Copy),
                     reads=[kr.b], writes=[k_tm.b])
                p.op("act", lambda e, r32=r32, tt=tt, cs=cs: e.activation(r_tm[:, tt, cs], r32, AF.Copy),
                     reads=[x3.b], writes=[r_tm.b])
                bu = bu32.get()
                p.op("dve", lambda e, bu=bu, kr=kr, r32=r32: e.tensor_tensor(bu[:], kr[:], r32, ALU.mult),
                     reads=[kr.b, x3.b], writes=[bu.b])
                p.op("dve", lambda e, bu=bu, cs=cs: e.tensor_tensor(bu[:], bu[:], prow[:, 2, cs], ALU.mult),
                     reads=[prow.b], writes=[bu.b])
                b2 = bs2.get()
                p.op("dve", lambda e, bu=bu, b2=b2: e.tensor_reduce(
                    b2[:], bu[:].rearrange("p (h k) -> p h k", k=64), AX.X, ALU.add), reads=[bu.b], writes=[b2.b])
                yb_ = ybon.get()
                p.op("dve", lambda e, b2=b2, v32=v32, yb_=yb_: e.tensor_tensor(
                    yb_[:].rearrange("p (h k) -> p h k", k=64), v32.rearrange("p (h k) -> p h k", k=64),
                    b2[:].unsqueeze(2).to_broadcast([128, 2, 64]), ALU.mult), reads=[b2.b, x3.b], writes=[yb_.b])
                p.dma(ysd[tt, :, cs], yb_[:], reads=[yb_.b], writes=[ysd_b], sem=yb_.b)

    def rwkv_scan(l):
        ps8 = K.ps8
        p.dma(Ysum[:], ysd.rearrange("t p c -> p t c"), reads=[ysd_b], writes=[Ysum.b], sem=Ysum.b)
        p.dma(gup[:], rw_gup[l, :, :], writes=[gup.b], sem=gup.b)
        p.dma(prow2[:], rw_rows[l:l + 1, 3:5, :].to_broadcast([128, 2, 512]), writes=[prow2.b], sem=prow2.b)
        p.op("pool", lambda e: e.memset(Sst[:], 0.0), writes=[Sst.b])
        p.op("pool", lambda e: e.memset(ybuf[:], 0.0), writes=[ybuf.b])
        for j_, src_ in enumerate((w_tm[0], w_tm[1], k_tm)):
            p.dma(wkd[j_].rearrange("(t p) c -> p t c", p=128), src_[:], reads=[src_.b], writes=[wkd_b], sem=wkd_b)
        srcs = [kap_tm, b_tm, r_tm]
        WB = 4
        for i in range(T):
            tf, tb = i, T - 1 - i
            q = i // 256
            if i % 256 == 0:
                s0 = S0t.get()
                p.dma(s0[:], sr0_in[l, q, :, :], writes=[s0.b], sem=s0.b)
                p.op("dve", lambda e, s0=s0: e.scalar_tensor_tensor(
                    Sst[:], Sst[:], segf[:, 1:2], s0[:], ALU.mult, ALU.add), reads=[s0.b, segf.b], writes=[Sst.b])
            if i % WB == 0:
                wk = wkr[(i // WB) % 3]
                wk4 = wk[:].rearrange("p (s a c) -> p s a c", a=2, c=512)
                for a_, (jf, jb) in enumerate(((0, 1), (2, 2))):
                    p.dma(wk4[0:64, :, a_, :], wkd[jf:jf + 1, i:i + WB, :].to_broadcast([64, WB, 512]),
                          reads=[wkd_b], writes=[wk.b], sem=wk.b)
                    p.dma(wk4[64:128, :, a_, :], wkd[jb:jb + 1, T - WB - i:T - i, :][:, ::-1, :].to_broadcast([64, WB, 512]),
                          reads=[wkd_b], writes=[wk.b], sem=wk.b)
            w_bc = wk4[:, i % WB, 0, :]
            k_bc = wk4[:, i % WB, 1, :]
            banks = []
            for sf in srcs:
                bk = ps8.get()
                p.op("pe", lambda e, bk=bk, sf=sf, tf=tf: e.matmul(
                    bk[0:64, :], ident_b[:, tf % 128:tf % 128 + 1].to_broadcast([128, 64]), sf[:, tf // 128, :],
                    start=True, stop=True), reads=[sf.b, ident_b.b], writes=[bk.b])
                p.op("pe", lambda e, bk=bk, sf=sf, tb=tb: e.matmul(
                    bk[64:128, :], ident_b[:, tb % 128:tb % 128 + 1].to_broadcast([128, 64]), sf[:, tb // 128, :],
                    start=True, stop=True), reads=[sf.b, ident_b.b], writes=[bk.b])
                banks.append(bk)
            kb_, bb_, rb_ = banks
            t3 = t3r.get()
            p.op("pool", lambda e, t3=t3, k_bc=k_bc, i=i: e.tensor_tensor(
                t3[:].rearrange("p (h k) -> p h k", k=64), k_bc.rearrange("p (h k) -> p h k", k=64),
                VV[:, :, i:i + 1].to_broadcast([128, 8, 64]), ALU.mult), reads=[wk.b, VV.b], writes=[t3.b])
            Sw = Swr.get()
            p.op("pool", lambda e, Sw=Sw, w_bc=w_bc: e.tensor_tensor(Sw[:], Sst[:], w_bc, ALU.mult),
                 reads=[Sst.b, wk.b], writes=[Sw.b])
            Sp = Spr.get()
            p.op("pool", lambda e, Sp=Sp, Sw=Sw, t3=t3: e.tensor_tensor(Sp[:], Sw[:], t3[:], ALU.add),
                 reads=[Sw.b, t3.b], writes=[Sp.b], ns=NS)
            t1 = stmp.get()
            p.op("dve", lambda e, t1=t1, kb_=kb_: e.tensor_tensor(t1[:], Sst[:], kb_[:], ALU.mult),
                 reads=[Sst.b, kb_.b], writes=[t1.b], ns=NS)
            sk = skr.get()
            p.op("dve", lambda e, t1=t1, sk=sk: e.tensor_reduce(
                sk[:], t1[:].rearrange("p (h k) -> p h k", k=64), AX.X, ALU.add), reads=[t1.b], writes=[sk.b], ns=NS)
            t2 = stmp.get()
            p.op("dve", lambda e, t2=t2, bb_=bb_, sk=sk: e.tensor_tensor(
                t2[:].rearrange("p (h k) -> p h k", k=64), bb_[:].rearrange("p (h k) -> p h k", k=64),
                sk[:].unsqueeze(2).to_broadcast([128, 8, 64]), ALU.mult), reads=[bb_.b, sk.b], writes=[t2.b], ns=NS)
            p.op("dve", lambda e, t2=t2, Sp=Sp: e.tensor_tensor(Sst[:], Sp[:], t2[:], ALU.subtract),
                 reads=[t2.b, Sp.b], writes=[Sst.b], ns=NS)
            t4 = stmp.get()
            p.op("dve", lambda e, t4=t4, rb_=rb_: e.tensor_tensor(t4[:], Sst[:], rb_[:], ALU.mult),
                 reads=[Sst.b, rb_.b], writes=[t4.b], ns=NS)
            p.op("dve", lambda e, t4=t4, i=i: e.tensor_reduce(
                ybuf[:, i % 128, :], t4[:].rearrange("p (h k) -> p h k", k=64), AX.X, ALU.add),
                reads=[t4.b], writes=[ybuf.b], ns=NS)
            if i % 128 == 127:
                p.op("act", lambda e: e.activation(yrev[0:64, :, :], ybuf[0:64, :, :], AF.Copy), reads=[ybuf.b], writes=[yrev.b])
                p.op("act", lambda e: e.activation(yrev[64:128, :, :], ybuf[64:128, ::-1, :], AF.Copy), reads=[ybuf.b], writes=[yrev.b])
                for hg in range(2):
                    pt = ps8.get()
                    for h4 in range(4):
                        h = hg * 4 + h4
                        p.op("pe", lambda e, pt=pt, h=h, h4=h4: e.transpose(
                            pt[:, h4 * 128:(h4 + 1) * 128], yrev[:, :, h], ident[:]), reads=[yrev.b, ident.b], writes=[pt.b])
                    for (lo, tile_) in ((0, i // 128), (64, 7 - i // 128)):
                        p.op("dve", lambda e, pt=pt, lo=lo, tile_=tile_, hg=hg: e.tensor_tensor(
                            Ysum[:, tile_, hg * 256:(hg + 1) * 256].rearrange("p (h k) -> p h k", k=64),
                            Ysum[:, tile_, hg * 256:(hg + 1) * 256].rearrange("p (h k) -> p h k", k=64),
                            pt[:].rearrange("p (h k) -> p h k", k=128)[:, :, lo:lo + 64], ALU.add),
                            reads=[pt.b], writes=[Ysum.b])
            if i % 256 == 255:
                p.dma(nr_out[l, 0, q, :, :], Sst[0:64, :], reads=[Sst.b], writes=[nr_ob], sem=Sst.b)
                p.dma(nr_out[l, 1, 3 - q, :, :], Sst[64:128, :], reads=[Sst.b], writes=[nr_ob], sem=Sst.b)

    def rwkv_post(l):
        for tt in range(NT):
            ts_ = slice(tt * 128, (tt + 1) * 128)
            y3 = Ysum[:, tt, :].rearrange("p (h k) -> p h k", k=64)
            sm = ln_s.get()
            p.op("dve", lambda e, sm=sm, y3=y3: e.tensor_reduce(sm[:], y3, AX.X, ALU.add), reads=[Ysum.b], writes=[sm.b])
            p.op("dve", lambda e, sm=sm: e.tensor_scalar(sm[:], sm[:], 1.0 / 64, None, ALU.mult), reads=[], writes=[sm.b])
            xc = ln_a.get()
            xc3 = xc[:].rearrange("p (h k) -> p h k", k=64)
            p.op("dve", lambda e, xc3=xc3, y3=y3, sm=sm: e.tensor_tensor(
                xc3, y3, sm[:].unsqueeze(2).to_broadcast([128, 8, 64]), ALU.subtract), reads=[Ysum.b, sm.b], writes=[xc.b])
            sq = ln_b_.get()
            p.op("dve", lambda e, sq=sq, xc=xc: e.tensor_tensor(sq[:], xc[:], xc[:], ALU.mult), reads=[xc.b], writes=[sq.b])
            vr = ln_r.get()
            p.op("dve", lambda e, vr=vr, sq=sq: e.tensor_reduce(
                vr[:], sq[:].rearrange("p (h k) -> p h k", k=64), AX.X, ALU.add), reads=[sq.b], writes=[vr.b])
            p.op("dve", lambda e, vr=vr: e.tensor_scalar(vr[:], vr[:], 1.0 / 64, 64e-5, ALU.mult, ALU.add), reads=[], writes=[vr.b])
            p.op("act", lambda e, vr=vr: e.activation(vr[:], vr[:], AF.Sqrt), reads=[], writes=[vr.b])
            p.op("dve", lambda e, vr=vr: e.reciprocal(vr[:], vr[:]), reads=[], writes=[vr.b])
            p.op("dve", lambda e, xc3=xc3, vr=vr: e.tensor_tensor(
                xc3, xc3, vr[:].unsqueeze(2).to_broadcast([128, 8, 64]), ALU.mult), reads=[vr.b], writes=[xc.b])
            p.op("dve", lambda e, xc=xc: e.tensor_tensor(xc[:], xc[:], prow2[:, 0, :], ALU.mult), reads=[prow2.b], writes=[xc.b])
            p.op("dve", lambda e, xc=xc: e.tensor_tensor(xc[:], xc[:], prow2[:, 1, :], ALU.add), reads=[prow2.b], writes=[xc.b])
            pg = ps.get()
            p.op("pe", lambda e, pg=pg, ts_=ts_: e.matmul(pg[:], SG[:, ts_], gup[:], start=True, stop=True),
                 reads=[SG.b, gup.b], writes=[pg.b])
            p.op("dve", lambda e, xc=xc, pg=pg: e.tensor_tensor(xc[:], xc[:], pg[:], ALU.mult), reads=[pg.b], writes=[xc.b])
            pt = ps.get()
            for c in range(4):
                p.op("pe", lambda e, pt=pt, xc=xc, c=c: e.transpose(
                    pt[:, c * 128:(c + 1) * 128], xc[:, c * 128:(c + 1) * 128], ident[:]), reads=[xc.b, ident.b], writes=[pt.b])
            p.op("act", lambda e, pt=pt, ts_=ts_: e.activation(
                orwT[:, :, ts_], pt[:].rearrange("p (c t) -> p c t", t=128), AF.Copy), reads=[pt.b], writes=[orwT.b])

    K.arena_begin()
    cosT = K.sb("cosT", [128, T])
    sinT = K.sb("sinT", [128, T])
    Kall = K.sb("Kall", [128, 2, 1280], BF16)
    Vall = K.sb("Vall", [128, 10, 256], BF16)
    hraw = K.ring("hraw", 2, [128, T])
    hsq = K.ring("hsq", 1, [128, T])
    hrs = K.ring("hrs", 1, [128, T])
    hrot = K.ring("hrot", 2, [128, T])
    ktm = K.ring("ktm", 2, [128, 128])
    vtm = K.ring("vtm", 2, [128, 256])
    cst = K.ring("cst", 2, [128, 128])

    def head_norm_rope(pa, pb, gidx, out_ap, out_b):
        raw = hraw.get()
        sq = hsq.get()
        for hf, pp in ((0, pa), (1, pb)):
            sl = slice(hf * 512, (hf + 1) * 512)
            p.op("act", lambda e, pp=pp, sl=sl: e.activation(raw[:, sl], pp[:], AF.Copy),
                 reads=[pp.b], writes=[raw.b])
            p.op("act", lambda e, pp=pp, sl=sl: e.activation(sq[:, sl], pp[:], AF.Square),
                 reads=[pp.b], writes=[sq.b])
        rs = hrs.get()
        for hf in range(2):
            sl = slice(hf * 512, (hf + 1) * 512)
            pq = ps.get()
            p.op("pe", lambda e, pq=pq, sl=sl: e.matmul(pq[:], ones_f[:], sq[:, sl], start=True, stop=True),
                 reads=[sq.b, ones_f.b], writes=[pq.b])
            p.op("dve", lambda e, pq=pq, sl=sl: e.tensor_scalar(
                rs[:, sl], pq[:], 1.0 / 128, EPS, ALU.mult, ALU.add), reads=[pq.b], writes=[rs.b])
        p.op("act", lambda e: e.activation(rs[:], rs[:], AF.Sqrt), reads=[], writes=[rs.b])
        p.op("dve", lambda e: e.reciprocal(rs[:], rs[:]), reads=[], writes=[rs.b])
        p.op("dve", lambda e: e.scalar_tensor_tensor(raw[:], raw[:], qkn[:, gidx:gidx + 1], rs[:], ALU.mult, ALU.mult),
             reads=[rs.b, qkn.b], writes=[raw.b])
        rt = hrot.get()
        for hf in range(2):
            sl = slice(hf * 512, (hf + 1) * 512)
            pq = ps.get()
            p.op("pe", lambda e, pq=pq, sl=sl: e.matmul(pq[:], rotT[:], raw[:, sl], start=True, stop=True),
                 reads=[raw.b, rotT.b], writes=[pq.b])
            p.op("dve", lambda e, pq=pq, sl=sl: e.tensor_tensor(rt[:, sl], pq[:], sinT[:, sl], ALU.mult),
                 reads=[pq.b, sinT.b], writes=[rt.b])
        p.op("dve", lambda e: e.tensor_tensor(raw[:], raw[:], cosT[:], ALU.mult), reads=[cosT.b], writes=[raw.b])
        p.op("dve", lambda e: e.tensor_tensor(out_ap, raw[:], rt[:], ALU.add), reads=[raw.b, rt.b], writes=[out_b])

    kf32 = K.ring("kf32", 2, [128, T])

    def attention_kv(l):
        p.dma(cosT[:], cos_in[:, :], writes=[cosT.b], sem=cosT.b)
        p.dma(sinT[:], sin_in[:, :], writes=[sinT.b], sem=sinT.b)
        for j in range(2):
            for tt in range(2):
                c_ = cst.get()
                p.dma(c_[:], ck_in[l, j, tt * 128:(tt + 1) * 128, :], writes=[c_.b], sem=c_.b)
                pt = ps.get()
                p.op("pe", lambda e, pt=pt, c_=c_: e.transpose(pt[:, 0:128], c_[:], ident[:]),
                     reads=[c_.b, ident.b], writes=[pt.b])
                p.op("act", lambda e, pt=pt, j=j, tt=tt: e.activation(
                    Kall[:, j, tt * 128:(tt + 1) * 128], pt[:, 0:128], AF.Copy), reads=[pt.b], writes=[Kall.b])
                c2 = cst.get()
                p.dma(c2[:], cv_in[l, j, tt * 128:(tt + 1) * 128, :], writes=[c2.b], sem=c2.b)
                p.op("act", lambda e, c2=c2, j=j, tt=tt: e.activation(
                    Vall[:, tt, j * 128:(j + 1) * 128], c2[:], AF.Copy), reads=[c2.b], writes=[Vall.b])
        if stage == 3.1:
            return
        for j in range(2):
            pa, pb = proj_fm(l, C_AK + 128 * j, 128)
            if stage == 3.2:
                kf = kf32.get()
                for hf, pp in ((0, pa), (1, pb)):
                    p.op("act", lambda e, pp=pp, hf=hf, kf=kf: e.activation(kf[:, hf * 512:(hf + 1) * 512], pp[:], AF.Copy),
                         reads=[pp.b], writes=[kf.b])
                continue
            kf = kf32.get()
            head_norm_rope(pa, pb, 1, kf[:], kf.b)
            p.op("act", lambda e, kf=kf, j=j: e.activation(Kall[:, j, 256:1280], kf[:], AF.Copy),
                 reads=[kf.b], writes=[Kall.b])
            for tt in range(NT):
                pt = ps.get()
                p.op("pe", lambda e, pt=pt, kf=kf, tt=tt: e.transpose(
                    pt[:, 0:128], kf[:, tt * 128:(tt + 1) * 128], ident[:]),
                    reads=[kf.b, ident.b], writes=[pt.b])
                kt = ktm.get()
                p.op("dve", lambda e, pt=pt, kt=kt: e.tensor_copy(kt[:], pt[:, 0:128]), reads=[pt.b], writes=[kt.b])
                p.dma(nk_out[l, j, tt * 128:(tt + 1) * 128, :], kt[:], reads=[kt.b], writes=[nk_ob], sem=kt.b)
        if stage <= 3.3:
            return
        for half in range(2):
            def cons_half(tt, pt, half=half):
                vt = vtm.get()
                p.op("act", lambda e: e.activation(vt[:, 0:128], pt[:, 0:128], AF.Copy), reads=[pt.b], writes=[vt.b])
                p.op("dve", lambda e: e.tensor_copy(Vall[:, 2 + tt, half * 128:(half + 1) * 128], vt[:, 0:128]),
                     reads=[vt.b], writes=[Vall.b])
                p.dma(nv_out[l, half, tt * 128:(tt + 1) * 128, :], vt[:, 0:128],
                      reads=[vt.b], writes=[nv_ob], sem=vt.b)
            proj_tm(l, C_AV + 128 * half, 128, cons_half)


    qf32 = K.ring("qf32", 2, [128, T])
    qbf = K.ring("qbf", 2, [128, T], BF16)
    pexp = K.ring("pexp", 3, [128, 256], BF16)
    rden = K.ring("rden", 2, [128, 256])
    K.arena_end(limit=18432)

    modgen = {}

    def attention(l):
        gen = modgen[l] = mod_deferred(l)
        for h in range(8):
            pa, pb = proj_fm(l, C_AQ + 128 * h, 128)
            qf = qf32.get()
            head_norm_rope(pa, pb, 0, qf[:], qf.b)
            qb = qbf.get()
            p.op("act", lambda e, qb=qb, qf=qf: e.activation(qb[:], qf[:], AF.Copy), reads=[qf.b], writes=[qb.b])
            j = h // 4
            for qt in range(4):
                po, pd = psa.get(), psa.get()
                qs = slice(qt * 256, (qt + 1) * 256)
                for kt in range(10):
                    pS = ps.get()
                    p.op("pe", lambda e, pS=pS, kt=kt, qb=qb, qs=qs, j=j: e.matmul(
                        pS[:, 0:256], Kall[:, j, kt * 128:(kt + 1) * 128], qb[:, qs], start=True, stop=True),
                        reads=[Kall.b, qb.b], writes=[pS.b])
                    pe_ = pexp.get()
                    p.op("act", lambda e, pS=pS, pe_=pe_, qt=qt, kt=kt: e.activation(
                        pe_[:], pS[:, 0:256], AF.Exp, bias=amask[:, qt * 10 + kt:qt * 10 + kt + 1],
                        scale=float(128 ** -0.5)), reads=[pS.b, amask.b], writes=[pe_.b])
                    p.op("pe", lambda e, po=po, pe_=pe_, kt=kt, j=j: e.matmul(
                        po[:, 0:256], Vall[:, kt, j * 128:(j + 1) * 128], pe_[:], start=(kt == 0), stop=(kt == 9)),
                        reads=[Vall.b, pe_.b], writes=[po.b])
                    p.op("pe", lambda e, pd=pd, pe_=pe_, kt=kt: e.matmul(
                        pd[:, 0:256], ones_b[:], pe_[:], start=(kt == 0), stop=(kt == 9)),
                        reads=[ones_b.b, pe_.b], writes=[pd.b])
                next(gen, None)
                rd = rden.get()
                p.op("dve", lambda e, rd=rd, pd=pd: e.reciprocal(rd[:], pd[:, 0:256]), reads=[pd.b], writes=[rd.b])
                p.op("dve", lambda e, rd=rd, po=po, h=h, qs=qs: e.tensor_tensor(
                    oattT[:, h, qs], po[:, 0:256], rd[:], ALU.mult), reads=[po.b, rd.b], writes=[oattT.b])

    K.arena_begin()
    gfa = [K.sb("gfa%d" % d_, [17, T]) for d_ in range(2)]
    aup = K.sb("aup", [17, 2, 512])
    glan = K.sb("glan", [128, 1])
    gq = K.sb("gq", [128, T])
    gk = K.sb("gk", [128, T])
    gktm = K.sb("gktm", [128, NT, 128])
    gvtm = K.sb("gvtm", [128, NT, 128], BF16)
    lp = [K.sb("lp%d" % d_, [128, NT, 128]) for d_ in range(2)]
    og = K.sb("og", [128, T])
    Sf2 = [K.sb("Sf%d" % d_, [128, 128]) for d_ in range(2)]
    Sb2 = [K.sb("Sb%d" % d_, [128, 128], BF16) for d_ in range(2)]
    s0t = K.ring("s0t", 2, [128, 128])
    gtmp = K.ring("gtmp", 6, [128, 128])
    gez = K.ring("gez", 2, [128, 128])
    qeb = K.ring("qeb", 2, [128, 128], BF16)
    keb = K.ring("keb", 2, [128, 128], BF16)
    klb = K.ring("klb", 2, [128, 128], BF16)
    attb = K.ring("attb", 2, [128, 128], BF16)
    dcol = K.ring("dcol", 2, [128, 1])
    gsq = K.sb("gsq", [128, T])
    grs = K.sb("grs", [128, T])
    gsg = K.sb("gsg", [128, T])
    K.arena_end(limit=18432)

    def gla(l):
        col_load(glan[:], glan.b, gla_norm[l, :, :], 1)
        p.dma(aup[:], gla_aup[l].rearrange("d r c -> r d c"), writes=[aup.b], sem=aup.b)
        for d_ in range(2):
            p.op("pool", lambda e, d_=d_: e.memset(gfa[d_][:], 1.0), writes=[gfa[d_].b])
            pa, pb = proj_fm(l, C_GAD + 16 * d_, 16)
            for hf, pp in ((0, pa), (1, pb)):
                p.op("act", lambda e, pp=pp, hf=hf, d_=d_: e.activation(
                    gfa[d_][0:16, hf * 512:(hf + 1) * 512], pp[0:16, :], AF.Copy), reads=[pp.b], writes=[gfa[d_].b])
        for h in range(4):
            pa, pb = proj_fm(l, C_GQ + 128 * h, 128)
            for hf, pp in ((0, pa), (1, pb)):
                p.op("act", lambda e, pp=pp, hf=hf: e.activation(
                    gq[:, hf * 512:(hf + 1) * 512], pp[:], AF.Copy, scale=float(128 ** -0.5)), reads=[pp.b], writes=[gq.b])
            pa, pb = proj_fm(l, C_GK + 128 * h, 128)
            for hf, pp in ((0, pa), (1, pb)):
                p.op("act", lambda e, pp=pp, hf=hf: e.activation(
                    gk[:, hf * 512:(hf + 1) * 512], pp[:], AF.Copy), reads=[pp.b], writes=[gk.b])
            proj_tm(l, C_GK + 128 * h, 128, lambda tt, pt: p.op(
                "act", lambda e: e.activation(gktm[:, tt, :], pt[:, 0:128], AF.Copy), reads=[pt.b], writes=[gktm.b]))
            proj_tm(l, C_GV + 128 * h, 128, lambda tt, pt: p.op(
                "act", lambda e: e.activation(gvtm[:, tt, :], pt[:, 0:128], AF.Copy), reads=[pt.b], writes=[gvtm.b]))
            for d_ in range(2):
                for tt in range(NT):
                    pz = ps.get()
                    p.op("pe", lambda e, pz=pz, d_=d_, tt=tt, h=h: e.matmul(
                        pz[:, 0:128], gfa[d_][:, tt * 128:(tt + 1) * 128], aup[:, d_, h * 128:(h + 1) * 128],
                        start=True, stop=True), reads=[gfa[d_].b, aup.b], writes=[pz.b])
                    ez = gez.get()
                    p.op("act", lambda e, pz=pz, ez=ez: e.activation(ez[:], pz[:, 0:128], AF.Exp, scale=-1.0),
                         reads=[pz.b], writes=[ez.b])
                    p.op("act", lambda e, ez=ez, d_=d_, tt=tt: e.activation(lp[d_][:, tt, :], ez[:], AF.Ln, bias=1.0),
                         reads=[ez.b], writes=[lp[d_].b])
            p.op("pool", lambda e: e.memset(og[:], 0.0), writes=[og.b])
            for d_ in range(2):
                p.op("pool", lambda e, d_=d_: e.memset(Sf2[d_][:], 0.0), writes=[Sf2[d_].b])
            for idx_ in range(2 * NT):
                if idx_ % 2 == 1:
                    next(modgen[l], None)
                d_ = idx_ % 2
                n = (idx_ // 2) if d_ == 0 else (NT - 1 - idx_ // 2)
                Sf, Sb_ = Sf2[d_], Sb2[d_]
                mU, mLs, dc = (0, 3, 127) if d_ == 0 else (1, 2, 0)
                for n in (n,):
                    seg = n // 2
                    ts_ = slice(n * 128, (n + 1) * 128)
                    is_start = (n % 2 == 0) if d_ == 0 else (n % 2 == 1)
                    if is_start:
                        s0 = s0t.get()
                        p.dma(s0[:], sg0_in[l, d_, seg, h, :, :], writes=[s0.b], sem=s0.b)
                        p.op("dve", lambda e, s0=s0, Sf=Sf: e.scalar_tensor_tensor(
                            Sf[:], Sf[:], segf[:, 1:2], s0[:], ALU.mult, ALU.add), reads=[s0.b, segf.b], writes=[Sf.b])
                        p.op("act", lambda e, Sf=Sf, Sb_=Sb_: e.activation(Sb_[:], Sf[:], AF.Copy), reads=[Sf.b], writes=[Sb_.b])
                    pc = ps.get()
                    p.op("pe", lambda e, pc=pc, n=n, d_=d_, mU=mU: e.matmul(
                        pc[:, 0:128], lp[d_][:, n, :], tri[:, mU, :], start=True, stop=True),
                        reads=[lp[d_].b, tri.b], writes=[pc.b])
                    t1, t2, t3 = gtmp.get(), gtmp.get(), gtmp.get()
                    p.op("act", lambda e, pc=pc, t1=t1: e.activation(t1[:], pc[:, 0:128], AF.Exp, scale=-1.0 / 16),
                         reads=[pc.b], writes=[t1.b])
                    p.op("act", lambda e, pc=pc, t2=t2: e.activation(t2[:], pc[:, 0:128], AF.Exp, scale=1.0 / 16),
                         reads=[pc.b], writes=[t2.b])
                    dcl = dcol.get()
                    p.op("act", lambda e, pc=pc, dcl=dcl, dc=dc: e.activation(
                        dcl[:], pc[:, dc:dc + 1], AF.Exp, scale=-1.0 / 16), reads=[pc.b], writes=[dcl.b])
                    qe, ke = qeb.get(), keb.get()
                    p.op("dve", lambda e, qe=qe, t1=t1, ts_=ts_: e.tensor_tensor(qe[:], gq[:, ts_], t1[:], ALU.mult),
                         reads=[gq.b, t1.b], writes=[qe.b])
                    p.op("dve", lambda e, ke=ke, t2=t2, ts_=ts_: e.tensor_tensor(ke[:], gk[:, ts_], t2[:], ALU.mult),
                         reads=[gk.b, t2.b], writes=[ke.b])
                    pg = ps.get()
                    p.op("pe", lambda e, pg=pg, n=n, d_=d_, mLs=mLs: e.matmul(
                        pg[:, 0:128], tri[:, mLs, :], lp[d_][:, n, :], start=True, stop=True),
                        reads=[lp[d_].b, tri.b], writes=[pg.b])
                    p.op("act", lambda e, pg=pg, t3=t3: e.activation(t3[:], pg[:, 0:128], AF.Exp, scale=-1.0 / 16),
                         reads=[pg.b], writes=[t3.b])
                    kl = klb.get()
                    p.op("dve", lambda e, kl=kl, t3=t3, n=n: e.tensor_tensor(kl[:], gktm[:, n, :], t3[:], ALU.mult),
                         reads=[gktm.b, t3.b], writes=[kl.b])
                    pat = ps.get()
                    p.op("pe", lambda e, pat=pat, ke=ke, qe=qe: e.matmul(pat[:, 0:128], ke[:], qe[:], start=True, stop=True),
                         reads=[ke.b, qe.b], writes=[pat.b])
                    ab = attb.get()
                    p.op("dve", lambda e, ab=ab, pat=pat, mU=mU: e.tensor_tensor(ab[:], pat[:, 0:128], tri[:, mU, :], ALU.mult),
                         reads=[pat.b, tri.b], writes=[ab.b])
                    po = psa.get()
                    p.op("pe", lambda e, po=po, ab=ab, n=n: e.matmul(po[:, 0:128], gvtm[:, n, :], ab[:], start=True, stop=False),
                         reads=[gvtm.b, ab.b], writes=[po.b])
                    p.op("pe", lambda e, po=po, qe=qe, Sb_=Sb_: e.matmul(po[:, 0:128], Sb_[:], qe[:], start=False, stop=True),
                         reads=[Sb_.b, qe.b], writes=[po.b])
                    p.op("dve", lambda e, po=po, ts_=ts_: e.tensor_tensor(og[:, ts_], og[:, ts_], po[:, 0:128], ALU.add),
                         reads=[po.b], writes=[og.b])
                    pst = psa.get()
                    p.op("pe", lambda e, pst=pst, kl=kl, n=n: e.matmul(pst[:, 0:128], kl[:], gvtm[:, n, :], start=True, stop=True),
                         reads=[kl.b, gvtm.b], writes=[pst.b])
                    p.op("dve", lambda e, pst=pst, dcl=dcl, Sf=Sf: e.scalar_tensor_tensor(
                        Sf[:], Sf[:], dcl[:, 0:1], pst[:, 0:128], ALU.mult, ALU.add), reads=[pst.b, dcl.b], writes=[Sf.b])
                    p.op("act", lambda e, Sf=Sf, Sb_=Sb_: e.activation(Sb_[:], Sf[:], AF.Copy), reads=[Sf.b], writes=[Sb_.b])
                    if not is_start:
                        p.dma(ng_out[l, d_, seg, h, :, :], Sf[:], reads=[Sf.b], writes=[ng_ob], sem=Sf.b)
            p.op("act", lambda e: e.activation(gsq[:], og[:], AF.Square), reads=[og.b], writes=[gsq.b])
            for hf in range(2):
                sl = slice(hf * 512, (hf + 1) * 512)
                pq = ps.get()
                p.op("pe", lambda e, pq=pq, sl=sl: e.matmul(pq[:], ones_f[:], gsq[:, sl], start=True, stop=True),
                     reads=[gsq.b, ones_f.b], writes=[pq.b])
                p.op("dve", lambda e, pq=pq, sl=sl: e.tensor_scalar(grs[:, sl], pq[:], 1.0 / 128, EPS, ALU.mult, ALU.add),
                     reads=[pq.b], writes=[grs.b])
            p.op("act", lambda e: e.activation(grs[:], grs[:], AF.Sqrt), reads=[], writes=[grs.b])
            p.op("dve", lambda e: e.reciprocal(grs[:], grs[:]), reads=[], writes=[grs.b])
            p.op("dve", lambda e: e.scalar_tensor_tensor(og[:], og[:], glan[:, 0:1], grs[:], ALU.mult, ALU.mult),
                 reads=[grs.b, glan.b], writes=[og.b])
            pa, pb = proj_fm(l, C_GG + 128 * h, 128)
            for hf, pp in ((0, pa), (1, pb)):
                sl = slice(hf * 512, (hf + 1) * 512)
                p.op("act", lambda e, pp=pp, sl=sl: e.activation(gsg[:, sl], pp[:], AF.Silu), reads=[pp.b], writes=[gsg.b])
            p.op("dve", lambda e, h=h: e.tensor_tensor(oglaT[:, h, :], og[:], gsg[:], ALU.mult),
                 reads=[og.b, gsg.b], writes=[oglaT.b])
        for _ in modgen[l]:
            pass

    K.arena_begin()
    mergedT = K.sb("mergedT", [128, KC, T], BF16)
    sig = K.ring("sig", 2, [128, T])
    macc = K.ring("macc", 2, [128, T])
    mtmp = K.ring("mtmp", 2, [128, T])
    K.arena_end(limit=18432)

    def residual_update(l, c, pa, pb, gt_off):
        xc = xch.get()
        p.dma(xc[:], xs[c, :, :], reads=[xs_b[c]], writes=[xc.b], sem=xc.b)
        for hf, pp in ((0, pa), (1, pb)):
            sl = slice(hf * 512, (hf + 1) * 512)
            p.op("dve", lambda e, pp=pp, sl=sl, xc=xc: e.scalar_tensor_tensor(
                xc[:, sl], pp[:], modc[:, gt_off + c:gt_off + c + 1], xc[:, sl], ALU.mult, ALU.add),
                reads=[pp.b, modc.b], writes=[xc.b])
        p.dma(xs[c, :, :], xc[:], reads=[xc.b], writes=[xs_b[c]], sem=xc.b)

    def merge_out(l, branches):
        srcs = [(oattT, 8), (oglaT, 4), (orwT, 4)]
        for c in range(KC):
            acc = macc.get()
            first = True
            for br in branches:
                src, kc = srcs[br]
                wv, wb = wload(w_br[br][l, :, c * 128:(c + 1) * 128], kc, 128)
                pa, pb = psa.get(), psa.get()
                for k in range(kc):
                    for hf, pp in ((0, pa), (1, pb)):
                        p.op("pe", lambda e, wv=wv, k=k, hf=hf, pp=pp, src=src, kc=kc: e.matmul(
                            pp[:], wv[:, k, :], src[:, k, hf * 512:(hf + 1) * 512],
                            start=(k == 0), stop=(k == kc - 1)), reads=[wb, src.b], writes=[pp.b])
                ga, gb = proj_fm(l, C_GATE + br * D + c * 128, 128)
                sg_ = sig.get()
                for hf, gg in ((0, ga), (1, gb)):
                    sl = slice(hf * 512, (hf + 1) * 512)
                    p.op("act", lambda e, gg=gg, sl=sl, sg_=sg_: e.activation(sg_[:, sl], gg[:], AF.Sigmoid),
                         reads=[gg.b], writes=[sg_.b])
                for hf, pp in ((0, pa), (1, pb)):
                    sl = slice(hf * 512, (hf + 1) * 512)
                    if first:
                        p.op("dve", lambda e, pp=pp, sl=sl, sg_=sg_, acc=acc: e.tensor_tensor(
                            acc[:, sl], pp[:], sg_[:, sl], ALU.mult), reads=[pp.b, sg_.b], writes=[acc.b])
                    else:
                        mt = mtmp.get()
                        p.op("dve", lambda e, pp=pp, sl=sl, sg_=sg_, mt=mt: e.tensor_tensor(
                            mt[:, sl], pp[:], sg_[:, sl], ALU.mult), reads=[pp.b, sg_.b], writes=[mt.b])
                        p.op("dve", lambda e, sl=sl, mt=mt, acc=acc: e.tensor_tensor(
                            acc[:, sl], acc[:, sl], mt[:, sl], ALU.add), reads=[mt.b], writes=[acc.b])
                first = False
            p.op("act", lambda e, acc=acc, c=c: e.activation(mergedT[:, c, :], acc[:], AF.Copy),
                 reads=[acc.b], writes=[mergedT.b])
        for c in range(KC):
            wv, wb = wload(w_out[l, :, c * 128:(c + 1) * 128], KC, 128)
            pa, pb = psa.get(), psa.get()
            for k in range(KC):
                for hf, pp in ((0, pa), (1, pb)):
                    p.op("pe", lambda e, wv=wv, k=k, hf=hf, pp=pp: e.matmul(
                        pp[:], wv[:, k, :], mergedT[:, k, hf * 512:(hf + 1) * 512],
                        start=(k == 0), stop=(k == KC - 1)), reads=[wb, mergedT.b], writes=[pp.b])
            residual_update(l, c, pa, pb, 32)

    K.arena_begin()
    cwc = K.sb("cwc", [128, 3, 86])
    cbc = K.sb("cbc", [128, 86])
    cwn = K.sb("cwn", [128, 2, 86])
    upad = K.ring("upad", 2, [128, T + 2])
    ct1 = K.ring("ct1", 2, [128, T])
    csl = K.ring("csl", 2, [128, T])
    actT = K.sb("actT", [128, 22, T], BF16)
    K.arena_end(limit=18432)

    def conv_chunk(pa, pb, j):
        u_ = upad.get()
        for hf, pp in ((0, pa), (1, pb)):
            p.op("act", lambda e, pp=pp, hf=hf, u_=u_: e.activation(
                u_[:, 1 + hf * 512:1 + (hf + 1) * 512], pp[:], AF.Copy), reads=[pp.b], writes=[u_.b])
        t1 = ct1.get()
        p.op("dve", lambda e: e.tensor_scalar(t1[:], u_[:, 1:T + 1], cwc[:, 1, j:j + 1], cbc[:, j:j + 1],
                                              ALU.mult, ALU.add), reads=[u_.b, cwc.b, cbc.b], writes=[t1.b])
        p.op("dve", lambda e: e.scalar_tensor_tensor(t1[:], u_[:, 0:T], cwc[:, 0, j:j + 1], t1[:], ALU.mult, ALU.add),
             reads=[u_.b, cwc.b], writes=[t1.b])
        p.op("dve", lambda e: e.scalar_tensor_tensor(t1[:], u_[:, 2:T + 2], cwc[:, 2, j:j + 1], t1[:], ALU.mult, ALU.add),
             reads=[u_.b, cwc.b], writes=[t1.b])
        p.op("dve", lambda e: e.scalar_tensor_tensor(
            t1[:, 256:1024:256], u_[:, 256:1024:256], cwn[:, 0, j:j + 1], t1[:, 256:1024:256], ALU.mult, ALU.add),
            reads=[u_.b, cwn.b], writes=[t1.b])
        p.op("dve", lambda e: e.scalar_tensor_tensor(
            t1[:, 255:1023:256], u_[:, 257:1025:256], cwn[:, 1, j:j + 1], t1[:, 255:1023:256], ALU.mult, ALU.add),
            reads=[u_.b, cwn.b], writes=[t1.b])
        return t1

    def ffn(l):
        for u_ in upad.tl:
            p.op("pool", lambda e, u_=u_: e.memset(u_[:], 0.0), writes=[u_.b])
        for i in range(3):
            col_load(cwc[:, i, :], cwc.b, ffn_cw[l, i, :, :], 86)
        col_load(cbc[:], cbc.b, ffn_cb[l, :, :], 86)
        for i, wi in ((0, 0), (1, 2)):
            p.op("dve", lambda e, i=i, wi=wi: e.tensor_scalar(
                cwn[:, i, :], cwc[:, wi, :], segf[:, 2:3], None, ALU.mult), reads=[cwc.b, segf.b], writes=[cwn.b])
        norm(l, 1)
        for (j0, j1) in ((0, 22), (22, 43)):
            for j in range(j0, j1):
                pa, pb = proj_fm(l, 0, 128, wsrc=ffn_up[l, :, j * 128:(j + 1) * 128])
                tv = conv_chunk(pa, pb, j)
                pa, pb = proj_fm(l, 0, 128, wsrc=ffn_up[l, :, FFN_H + j * 128:FFN_H + (j + 1) * 128])
                tg = conv_chunk(pa, pb, 43 + j)
                sl_ = csl.get()
                p.op("act", lambda e, tg=tg, sl_=sl_: e.activation(sl_[:], tg[:], AF.Silu), reads=[tg.b], writes=[sl_.b])
                p.op("dve", lambda e, tv=tv, sl_=sl_, j=j, j0=j0: e.tensor_tensor(
                    actT[:, j - j0, :], tv[:], sl_[:], ALU.mult), reads=[tv.b, sl_.b], writes=[actT.b])
            nk = j1 - j0
            for c in range(KC):
                pa, pb = psa.get(), psa.get()
                k = 0
                while k < nk:
                    kk = min(16, nk - k)
                    wv, wb = wload(ffn_down[l, (j0 + k) * 128:(j0 + k + kk) * 128, c * 128:(c + 1) * 128], kk, 128)
                    for q in range(kk):
                        for hf, pp in ((0, pa), (1, pb)):
                            p.op("pe", lambda e, wv=wv, q=q, k=k, hf=hf, pp=pp, nk=nk: e.matmul(
                                pp[:], wv[:, q, :], actT[:, k + q, hf * 512:(hf + 1) * 512],
                                start=(k + q == 0), stop=(k + q == nk - 1)), reads=[wb, actT.b], writes=[pp.b])
                    k += kk
                residual_update(l, c, pa, pb, 80)

    for l in range(L):
        if stage <= 0:
            break
        compute_mod(l)
        if stage == 1:
            break
        norm(l, 0)
        if stage == 2:
            break
        p.barrier()
        rwkv_prep(l)
        p.barrier()
        if stage == 8:
            break
        rwkv_scan(l)
        p.barrier()
        rwkv_post(l)
        p.barrier()
        if stage == 9:
            break
        attention_kv(l)
        attention(l)
        p.barrier()
        gla(l)
        p.barrier()
        merge_out(l, [0, 1, 2])
        p.barrier()
        ffn(l)
        p.barrier()

    p.barrier()
    for tt in range(NT):
        yt = yo.get()
        for g in range(4):
            pt = ps.get()
            xg = xev.get()
            p.dma(xg[:, :, 0:128], xs[g * 4:(g + 1) * 4, :, tt * 128:(tt + 1) * 128].rearrange("c p t -> p c t"),
                  reads=xs_b[g * 4:(g + 1) * 4], writes=[xg.b], sem=xg.b)
            for j in range(4):
                p.op("pe", lambda e, pt=pt, j=j, xg=xg: e.transpose(
                    pt[:, j * 128:(j + 1) * 128], xg[:, j, 0:128], ident[:]),
                    reads=[xg.b, ident.b], writes=[pt.b])
            p.op("act", lambda e, pt=pt, g=g, yt=yt: e.activation(yt[:, g * 512:(g + 1) * 512], pt[:], AF.Copy),
                 reads=[pt.b], writes=[yt.b])
        p.dma(y_out[tt * 128:(tt + 1) * 128, :], yt[:], reads=[yt.b], writes=[y_ob], sem=yt.b)

    with K.st:
        n = p.finalize(K.st, final_bufs=K.outbufs)
        K.info = dict(ops=n, waits=p.n_waits, sems=p.n_sems)
    return K


def _rope_tables():
    rows = T // 64
    row = np.repeat(np.arange(rows, dtype=np.float32), 64)
    col = np.tile(np.arange(64, dtype=np.float32), rows)
    half = 64
    inv = (10000.0 ** (-np.arange(0, half, 2, dtype=np.float32) / half)).astype(np.float32)

    def cs(pos):
        ang = pos[:, None] * inv[None, :]
        ang = np.concatenate([ang, ang], axis=-1)
        return np.cos(ang), np.sin(ang)
    cr, sr = cs(row)
    cc, sc = cs(col)
    cosT = np.concatenate([cr, cc], axis=1).T.astype(np.float32)
    sinT = np.concatenate([sr, sc], axis=1).T.astype(np.float32)
    return np.ascontiguousarray(cosT), np.ascontiguousarray(sinT)


def _consts():
    ident = np.eye(128, dtype=np.float32)
    R = np.zeros((128, 128), np.float32)
    for base in (0, 64):
        for d in range(32):
            R[base + d, base + d + 32] = -1.0
            R[base + d + 32, base + d] = 1.0
    return ident, np.ascontiguousarray(R.T)


def make_in_maps(inputs):
    f = lambda a: np.ascontiguousarray(np.asarray(a, dtype=np.float32))
    ident, rotT = _consts()
    cosT, sinT = _rope_tables()
    ii = np.arange(128)
    tri = np.stack([(ii[:, None] <= ii[None, :]), (ii[:, None] >= ii[None, :]),
                    (ii[:, None] < ii[None, :]), (ii[:, None] > ii[None, :])], axis=1).astype(np.float32)
    aup = np.concatenate([f(inputs["gla_a_up"]), f(inputs["gla_a_bias"])[:, :, None, :]], axis=2)
    shared = {
        "w_mod": f(inputs["w_mod"]),
        "b_mod": f(inputs["b_mod"]).reshape(L, 96, 128),
        "norm_mix": f(inputs["norm_mix"]).reshape(L, KC, 128),
        "norm_ffn": f(inputs["norm_ffn"]).reshape(L, KC, 128),
        "w_in": f(inputs["w_in"]),
        "qk_norm": np.ascontiguousarray(np.stack([f(inputs["q_norm"]), f(inputs["k_norm"])], axis=1)),
        "ident": ident, "rotT": rotT, "tri": np.ascontiguousarray(tri), "gla_aup": np.ascontiguousarray(aup),
        "gla_norm": f(inputs["gla_norm"]).reshape(L, 1, 128),
        "w_br_att": f(inputs["w_br_att"]), "w_br_gla": f(inputs["w_br_gla"]), "w_br_rwkv": f(inputs["w_br_rwkv"]),
        "w_out": f(inputs["w_out"]), "ffn_up": f(inputs["ffn_up"]), "ffn_down": f(inputs["ffn_down"]),
        "ffn_cw": f(inputs["ffn_conv_w"]).reshape(L, 3, 86, 128), "ffn_cb": f(inputs["ffn_conv_b"]).reshape(L, 86, 128),
    }
    mu = f(inputs["rwkv_mu"])
    rw_mu = np.zeros((L, 16, 128), np.float32)
    offs = [(128 * i, 128) for i in range(12)] + [(1536, 64), (1600, 64), (1664, 64), (1728, 128)]
    for i, (o, n) in enumerate(offs):
        rw_mu[:, i, :n] = mu[:, o:o + n]
    shared["rw_mu"] = rw_mu
    sel2 = np.zeros((128, 2, 128), np.float32)
    for h2 in range(2):
        for v in range(64):
            sel2[h2 * 64 + v, h2, v] = 1.0
            sel2[h2 * 64 + v, h2, 64 + v] = 1.0
    shared["sel2"] = sel2
    shared["rw_wup"] = np.ascontiguousarray(np.concatenate([f(inputs["rwkv_w_up"]), f(inputs["rwkv_w0"])[:, :, None, :]], axis=2))
    shared["rw_aup"] = np.ascontiguousarray(np.concatenate([f(inputs["rwkv_a_up"]), f(inputs["rwkv_a0"])[:, None, :]], axis=1))
    shared["rw_gup"] = f(inputs["rwkv_g_up"])
    shared["rw_rows"] = np.ascontiguousarray(np.stack([f(inputs["rwkv_k_xi"]), f(inputs["rwkv_k_alpha"]), f(inputs["rwkv_bonus"]),
                                                       f(inputs["rwkv_ln_w"]), f(inputs["rwkv_ln_b"])], axis=1))
    maps = []
    for r in range(8):
        m = dict(shared)
        if r < 4:
            m["x"] = f(inputs["x_sample"][r])
            m["cvec"] = f(inputs["c"][r]).reshape(KC, 128)
            m["ck"] = f(inputs["cache_k"][r])
            m["cv"] = f(inputs["cache_v"][r])
            m["cosT"], m["sinT"] = cosT, sinT
            sg0 = np.zeros((L, 2, 4, 4, 128, 128), np.float32)
            sg0[:, 0, 0] = f(inputs["state_gla"][r])[:, 0]
            sg0[:, 1, 3] = f(inputs["state_gla"][r])[:, 1]
            m["sg0"] = sg0
            sr = f(inputs["state_rwkv"][r])
            sr0 = np.zeros((L, 4, 128, 512), np.float32)
            for d_ in range(2):
                sr0[:, 0, 64 * d_:64 * d_ + 64, :] = sr[:, d_].transpose(0, 2, 1, 3).reshape(L, 64, 512)
            m["sr0"] = sr0
            m["amask"] = np.zeros((128, 40), np.float32)
            m["segf"] = np.ascontiguousarray(np.broadcast_to(np.array([0, 1, 0, 0], np.float32), (128, 4)))
        else:
            q = r - 4
            m["x"] = f(inputs["x_prompt"][4 * q:4 * q + 4]).reshape(T, D)
            m["cvec"] = f(inputs["c_ctx"]).reshape(KC, 128)
            m["ck"] = np.zeros((L, 2, 256, 128), np.float32)
            m["cv"] = np.zeros((L, 2, 256, 128), np.float32)
            m["cosT"] = np.ones((128, T), np.float32)
            m["sg0"] = np.zeros((L, 2, 4, 4, 128, 128), np.float32)
            m["sr0"] = np.zeros((L, 4, 128, 512), np.float32)
            m["sinT"] = np.zeros((128, T), np.float32)
            m["segf"] = np.ascontiguousarray(np.broadcast_to(np.array([1, 0, -1, 0], np.float32), (128, 4)))
            am = np.full((4, 10), -30000.0, np.float32)
            for qt in range(4):
                am[qt, 2 + 2 * qt] = 0.0
                am[qt, 3 + 2 * qt] = 0.0
            m["amask"] = np.ascontiguousarray(np.broadcast_to(am.reshape(1, 40), (128, 40)))
        maps.append(m)
    return maps


def assemble(res):
    y_prompt = np.zeros((16, 256, D), np.float32)
    y_sample = np.zeros((4, T, D), np.float32)
    nck = np.zeros((16, L, 2, 256, 128), np.float32)
    ncv = np.zeros((16, L, 2, 256, 128), np.float32)
    nsg = np.zeros((16, L, 2, 4, 128, 128), np.float32)
    nsr = np.zeros((16, L, 2, 8, 64, 64), np.float32)
    for r in range(8):
        o = res[r]
        if r < 4:
            y_sample[r] = o["y"]
        else:
            q = r - 4
            y_prompt[4 * q:4 * q + 4] = o["y"].reshape(4, 256, D)
            nk = o["nk"].reshape(L, 2, 4, 256, 128)
            nv = o["nv"].reshape(L, 2, 4, 256, 128)
            nck[4 * q:4 * q + 4] = nk.transpose(2, 0, 1, 3, 4)
            ncv[4 * q:4 * q + 4] = nv.transpose(2, 0, 1, 3, 4)
            if "ng" in o:
                nsg[4 * q:4 * q + 4] = o["ng"].transpose(2, 0, 1, 3, 4, 5)
            if "nr" in o:
                nr = o["nr"].reshape(L, 2, 4, 64, 8, 64)
                nsr[4 * q:4 * q + 4] = nr.transpose(2, 0, 1, 4, 3, 5)
    return (y_prompt, y_sample, nck, ncv, nsg, nsr)


_CACHE = {}


def kernel(**inputs):
    if "K" not in _CACHE:
        _CACHE["K"] = build()
    K = _CACHE["K"]
    maps = make_in_maps(inputs)
    maps = [{k: v for k, v in m.items() if k in K.din} for m in maps]
    sub = _CACHE.get("sub")
    if sub is not None:
        res = run_bass_kernel_spmd(K.nc, [maps[r] for r in sub], core_ids=list(range(len(sub))))
        full = [res.results[sub.index(r)] if r in sub else {k: np.zeros(t.shape, np.float32) for k, t in K.dout.items()} for r in range(8)]
        return assemble(full)
    res = run_bass_kernel_spmd(K.nc, maps, core_ids=list(range(8)))
    return assemble(res.results)
```
